# Optimizing a Trainium2 kernel written in Bass

```python
import jax, jax.numpy as jnp
from jax import lax
import numpy as np

D_MODEL = 1024
BATCH = 16
SEQ = 4096
DEPTH = 2
DEC_BATCH = 8
DEC_SEQ = 8192
PAST_LEN = 128

GRID_W = 64
N_EVEN = (DEPTH + 1) // 2
N_ODD = DEPTH // 2
D_FF = 2816
EPS = 1e-6
FOURIER_W = D_MODEL // 4
FOURIER_CH = 64
FOURIER_GROUPS = FOURIER_W // FOURIER_CH
HEAD_DIM = 64
ATTN_W = 3 * D_MODEL // 4
N_Q_HEADS = ATTN_W // HEAD_DIM
N_KV_HEADS = 4
Q_PER_KV = N_Q_HEADS // N_KV_HEADS
KV_W = N_KV_HEADS * HEAD_DIM
AB_IN_W = FOURIER_W + ATTN_W + 2 * KV_W
AB_MIX_W = FOURIER_W + ATTN_W
Q_BLOCK = 128
ROPE_THETA = 10000.0
ROPE_HALF = HEAD_DIM // 2
ROPE_FREQS = ROPE_HALF // 2
M_HEADS = 4
M_W = D_MODEL
M_HEAD_DIM = M_W // M_HEADS
M_CONV = 3
M_CHUNK = 64
M_IN_W = 4 * M_W + 4 * M_HEADS

kernel_name = "hybrid_fnet_gqa_mlstm_macaron_encoder"


def rmsnorm(x, g):
    xf = x.astype(jnp.float32)
    y = xf * lax.rsqrt(jnp.mean(xf * xf, axis=-1, keepdims=True) + EPS)
    return (y * g.astype(jnp.float32)).astype(x.dtype)


def swiglu(x, w_in, w_out):
    gate, up = jnp.split(x @ w_in, 2, axis=-1)
    return (jax.nn.silu(gate) * up) @ w_out


def axial_rope_tables(n_tok):
    rows = n_tok // GRID_W
    row_id = jnp.repeat(jnp.arange(rows, dtype=jnp.float32), GRID_W)
    col_id = jnp.tile(jnp.arange(GRID_W, dtype=jnp.float32), rows)
    freqs = ROPE_THETA ** (-jnp.arange(ROPE_FREQS, dtype=jnp.float32) / ROPE_FREQS)
    ang_r = row_id[:, None] * freqs[None, :]
    ang_c = col_id[:, None] * freqs[None, :]
    return jnp.cos(ang_r), jnp.sin(ang_r), jnp.cos(ang_c), jnp.sin(ang_c)


def _rotate(xh, c, s):
    x1, x2 = jnp.split(xh, 2, axis=-1)
    c = c[None, :, None, :]
    s = s[None, :, None, :]
    return jnp.concatenate([x1 * c - x2 * s, x1 * s + x2 * c], axis=-1)


def apply_axial_rope(x, tables):
    cr, sr, cc, sc = tables
    xf = x.astype(jnp.float32)
    xr, xc = jnp.split(xf, 2, axis=-1)
    return jnp.concatenate([_rotate(xr, cr, sr), _rotate(xc, cc, sc)], axis=-1).astype(x.dtype)


def gqa_bidirectional(q, k, v):
    B, S = q.shape[0], q.shape[1]
    nblk = S // Q_BLOCK
    qg = q.reshape(B, nblk, Q_BLOCK, N_KV_HEADS, Q_PER_KV, HEAD_DIM).transpose(1, 0, 2, 3, 4, 5)
    scale = HEAD_DIM ** -0.5

    def block(qb):
        s = jnp.einsum('bqhgd,bkhd->bhgqk', qb, k, preferred_element_type=jnp.float32) * scale
        p = jax.nn.softmax(s, axis=-1)
        return jnp.einsum('bhgqk,bkhd->bqhgd', p.astype(v.dtype), v)

    o = lax.map(block, qg)
    return o.transpose(1, 0, 2, 3, 4, 5).reshape(B, S, ATTN_W)


def fourier_gqa_mixer(u, w_in, q_norm, k_norm, w_out):
    B, S, _ = u.shape
    z = u @ w_in
    f, q, k, v = jnp.split(z, [FOURIER_W, FOURIER_W + ATTN_W, FOURIER_W + ATTN_W + KV_W], axis=-1)
    fg = f.reshape(B, S, FOURIER_GROUPS, FOURIER_CH).astype(jnp.float32)
    f_mix = jnp.fft.fft2(fg, axes=(1, 3), norm='ortho').real.reshape(B, S, FOURIER_W).astype(u.dtype)
    q = rmsnorm(q.reshape(B, S, N_Q_HEADS, HEAD_DIM), q_norm)
    k = rmsnorm(k.reshape(B, S, N_KV_HEADS, HEAD_DIM), k_norm)
    v = v.reshape(B, S, N_KV_HEADS, HEAD_DIM)
    tables = axial_rope_tables(S)
    q = apply_axial_rope(q, tables)
    k = apply_axial_rope(k, tables)
    a = gqa_bidirectional(q, k, v)
    return jnp.concatenate([f_mix, a], axis=-1) @ w_out


def centred_conv(x, w):
    S = x.shape[1]
    pad = M_CONV // 2
    xp = jnp.pad(x, ((0, 0), (pad, pad), (0, 0)))
    return sum(xp[:, j:j + S] * w[j] for j in range(M_CONV))


def mlstm_chunkwise(q, k, v, ig, lf):
    B, H, S, d = q.shape
    nC = S // M_CHUNK

    def chunks(a):
        return jnp.moveaxis(a.reshape(a.shape[:2] + (nC, M_CHUNK) + a.shape[3:]), 2, 0)

    tril = jnp.tril(jnp.ones((M_CHUNK, M_CHUNK), dtype=bool))

    def step(carry, xs):
        C, n, m = carry
        qc, kc, vc, ic, fc = xs
        b = jnp.cumsum(fc, axis=-1)
        D = jnp.where(tril, b[..., :, None] - b[..., None, :] + ic[..., None, :], -jnp.inf)
        inter = b + m[..., None]
        mj = jnp.maximum(inter, jnp.max(D, axis=-1))
        w_inter = jnp.exp(inter - mj)
        P = jnp.exp(D - mj[..., None])
        sqk = jnp.einsum('bhjd,bhsd->bhjs', qc, kc) * P
        num = (w_inter[..., None] * jnp.einsum('bhed,bhjd->bhje', C, qc)
               + jnp.einsum('bhjs,bhse->bhje', sqk, vc))
        den = w_inter * jnp.einsum('bhd,bhjd->bhj', n, qc) + jnp.sum(sqk, axis=-1)
        h = num / jnp.maximum(jnp.abs(den), jnp.exp(-mj))[..., None]
        bL = b[..., -1]
        gs = bL[..., None] - b + ic
        m_new = jnp.maximum(bL + m, jnp.max(gs, axis=-1))
        decay = jnp.exp(bL + m - m_new)
        ws = jnp.exp(gs - m_new[..., None])
        C = decay[..., None, None] * C + jnp.einsum('bhse,bhsd->bhed', vc * ws[..., None], kc)
        n = decay[..., None] * n + jnp.einsum('bhs,bhsd->bhd', ws, kc)
        return (C, n, m_new), h

    init = (jnp.zeros((B, H, d, d), jnp.float32), jnp.zeros((B, H, d), jnp.float32),
            jnp.zeros((B, H), jnp.float32))
    _, hs = lax.scan(step, init, (chunks(q), chunks(k), chunks(v), chunks(ig), chunks(lf)))
    return jnp.moveaxis(hs, 0, 2).reshape(B, H, S, d)


def mlstm_mixer(u, w_in, gate_bias, conv_w, head_norm, w_out):
    B, S, _ = u.shape
    z = u @ w_in
    qk, v, o, g = jnp.split(z, [2 * M_W, 3 * M_W, 4 * M_W], axis=-1)
    qk = jax.nn.silu(centred_conv(qk, conv_w))
    q, k = jnp.split(qk, 2, axis=-1)
    g = g.astype(jnp.float32) + gate_bias.astype(jnp.float32)
    i_f, f_f, i_b, f_b = jnp.split(g, 4, axis=-1)

    def heads(a):
        return a.reshape(B, S, M_HEADS, M_HEAD_DIM).transpose(0, 2, 1, 3).astype(jnp.float32)

    qh = heads(q)
    kh = heads(k) * (M_HEAD_DIM ** -0.5)
    vh = heads(v)
    tg = lambda a: a.transpose(0, 2, 1)
    h_f = mlstm_chunkwise(qh, kh, vh, tg(i_f), jax.nn.log_sigmoid(tg(f_f)))
    flip = lambda a: jnp.flip(a, axis=2)
    h_b = flip(mlstm_chunkwise(flip(qh), flip(kh), flip(vh), flip(tg(i_b)),
                               flip(jax.nn.log_sigmoid(tg(f_b)))))
    h = (h_f + h_b).transpose(0, 2, 1, 3).astype(u.dtype)
    h = rmsnorm(h, head_norm.reshape(M_HEADS, M_HEAD_DIM)).reshape(B, S, M_W)
    return (jax.nn.sigmoid(o) * h) @ w_out


def trunk(x, ffn1_norm, ffn1_w_in, ffn1_w_out, mix_norm, ab_w_in, ab_q_norm, ab_k_norm, ab_w_out,
          c_w_in, c_gate_bias, c_conv, c_head_norm, c_w_out, ffn2_norm, ffn2_w_in, ffn2_w_out):
    for l in range(DEPTH):
        x = x + 0.5 * swiglu(rmsnorm(x, ffn1_norm[l]), ffn1_w_in[l], ffn1_w_out[l])
        u = rmsnorm(x, mix_norm[l])
        if l % 2 == 0:
            j = l // 2
            x = x + fourier_gqa_mixer(u, ab_w_in[j], ab_q_norm[j], ab_k_norm[j], ab_w_out[j])
        else:
            j = l // 2
            x = x + mlstm_mixer(u, c_w_in[j], c_gate_bias[j], c_conv[j], c_head_norm[j], c_w_out[j])
        x = x + 0.5 * swiglu(rmsnorm(x, ffn2_norm[l]), ffn2_w_in[l], ffn2_w_out[l])
    return x


def setup_inputs(seed: int = 0) -> dict:
    key = jax.random.key(seed)
    ks = jax.random.split(key, 24)
    f32 = jnp.float32
    nrm = lambda k, shape, scale: jax.random.normal(k, shape, f32) * scale
    gain = lambda k, shape: 1.0 + 0.02 * jax.random.normal(k, shape, f32)
    f_bias = jnp.linspace(3.0, 6.0, M_HEADS, dtype=f32)
    gb_noise = nrm(ks[22], (N_ODD, 4, M_HEADS), 0.1)
    gate_bias = (gb_noise + jnp.stack([jnp.zeros_like(f_bias), f_bias,
                                      jnp.zeros_like(f_bias), f_bias])[None]).reshape(N_ODD, 4 * M_HEADS)
    return {
        "x_prompt": jax.random.normal(ks[0], (BATCH, SEQ, D_MODEL), f32),
        "x_sample": jax.random.normal(ks[1], (DEC_BATCH, DEC_SEQ, D_MODEL), f32),
        "ffn1_norm": gain(ks[2], (DEPTH, D_MODEL)),
        "ffn1_w_in": nrm(ks[3], (DEPTH, D_MODEL, 2 * D_FF), D_MODEL ** -0.5),
        "ffn1_w_out": nrm(ks[4], (DEPTH, D_FF, D_MODEL), D_FF ** -0.5),
        "mix_norm": gain(ks[5], (DEPTH, D_MODEL)),
        "ab_w_in": nrm(ks[6], (N_EVEN, D_MODEL, AB_IN_W), D_MODEL ** -0.5),
        "ab_q_norm": gain(ks[7], (N_EVEN, HEAD_DIM)),
        "ab_k_norm": gain(ks[8], (N_EVEN, HEAD_DIM)),
        "ab_w_out": nrm(ks[9], (N_EVEN, AB_MIX_W, D_MODEL), AB_MIX_W ** -0.5),
        "c_w_in": nrm(ks[10], (N_ODD, D_MODEL, M_IN_W), D_MODEL ** -0.5),
        "c_gate_bias": gate_bias,
        "c_conv": nrm(ks[11], (N_ODD, M_CONV, 2 * M_W), M_CONV ** -0.5),
        "c_head_norm": gain(ks[12], (N_ODD, M_W)),
        "c_w_out": nrm(ks[13], (N_ODD, M_W, D_MODEL), M_W ** -0.5),
        "ffn2_norm": gain(ks[14], (DEPTH, D_MODEL)),
        "ffn2_w_in": nrm(ks[15], (DEPTH, D_MODEL, 2 * D_FF), D_MODEL ** -0.5),
        "ffn2_w_out": nrm(ks[16], (DEPTH, D_FF, D_MODEL), D_FF ** -0.5),
    }


def reference(x_prompt, x_sample, ffn1_norm, ffn1_w_in, ffn1_w_out, mix_norm, ab_w_in, ab_q_norm,
              ab_k_norm, ab_w_out, c_w_in, c_gate_bias, c_conv, c_head_norm, c_w_out,
              ffn2_norm, ffn2_w_in, ffn2_w_out):
    y_prompt = trunk(x_prompt, ffn1_norm, ffn1_w_in, ffn1_w_out, mix_norm, ab_w_in, ab_q_norm,
                     ab_k_norm, ab_w_out, c_w_in, c_gate_bias, c_conv, c_head_norm, c_w_out,
                     ffn2_norm, ffn2_w_in, ffn2_w_out)
    y_sample = trunk(x_sample, ffn1_norm, ffn1_w_in, ffn1_w_out, mix_norm, ab_w_in, ab_q_norm,
                     ab_k_norm, ab_w_out, c_w_in, c_gate_bias, c_conv, c_head_norm, c_w_out,
                     ffn2_norm, ffn2_w_in, ffn2_w_out)
    return (y_prompt, y_sample)
```

```python
from contextlib import ExitStack
import math
import numpy as np
import concourse.bass as bass
import concourse.mybir as mybir
from concourse.bass_utils import run_bass_kernel_spmd

F32 = mybir.dt.float32
BF16 = mybir.dt.bfloat16
AF = mybir.ActivationFunctionType
ALU = mybir.AluOpType
AX = mybir.AxisListType

ENGS = ("tensor", "vector", "scalar", "gpsimd", "sync")


class Phase:
    def __init__(self, nc, name):
        self.nc = nc
        self.name = name
        self.stack = ExitStack()
        self.q = {e: [] for e in ENGS}
        self.sem = {e: nc.alloc_semaphore(name=f"{name}_{e}") for e in ENGS}
        self.cnt = {e: 0 for e in ENGS}
        self.seen = {e: {} for e in ENGS}
        self.chans = {}
        self.nalloc = 0

    def sb(self, name, shape, dtype):
        return self.stack.enter_context(self.nc.sbuf_tensor(f"{self.name}_{name}", list(shape), dtype))

    def ps(self, name, shape, dtype=F32):
        return self.stack.enter_context(self.nc.psum_tensor(f"{self.name}_{name}", list(shape), dtype))

    def chan(self, name):
        if name not in self.chans:
            s = self.nc.alloc_semaphore(name=f"{self.name}_c_{name}")
            self.chans[name] = [s, 0]
        return name

    def _waits(self, eng, waits):
        wl = []
        for tok in waits:
            if tok is None:
                continue
            if isinstance(tok, list):
                wl += self._waits(eng, tok)
                continue
            kind, key, val = tok
            if kind == "e" and key == eng and eng == "tensor":
                continue
            k = (kind, key)
            if self.seen[eng].get(k, 0) >= val:
                continue
            self.seen[eng][k] = val
            sem = self.sem[key] if kind == "e" else self.chans[key][0]
            wl.append((sem, val))
        return wl

    def op(self, eng, fn, waits=(), sig=True):
        wl = self._waits(eng, waits)
        tok = None
        if sig:
            self.cnt[eng] += 1
            tok = ("e", eng, self.cnt[eng])
        self.q[eng].append((fn, wl, self.sem[eng] if sig else None, 1))
        return tok

    def dma(self, eng, chan, out, in_, waits=(), **kw):
        wl = self._waits(eng, waits)
        c = self.chans[chan]
        c[1] += 16
        self.q[eng].append((lambda e: e.dma_start(out=out, in_=in_, **kw), wl, c[0], 16))
        return ("d", chan, c[1])

    def chan_tok(self, chan):
        c = self.chans[chan]
        return ("d", chan, c[1]) if c[1] else None

    def emit(self, final_waits=()):
        fin = []
        for e in ENGS:
            if self.cnt[e]:
                fin.append(("e", e, self.cnt[e]))
        for cname, c in self.chans.items():
            if c[1]:
                fin.append(("d", cname, c[1]))
        endw = self._waits("gpsimd", fin)
        with self.nc.Block() as block:
            for eng in ENGS:
                def f(e, eng=eng):
                    for fn, wl, sem, inc in self.q[eng]:
                        for (s, v) in wl:
                            e.wait_ge(s, v)
                        ins = fn(e)
                        if sem is not None:
                            ins.then_inc(sem, inc)
                    if eng == "gpsimd":
                        for (s, v) in endw:
                            e.wait_ge(s, v)
                getattr(block, eng)(f)
        self.nc.all_engine_barrier()
        self.nc.clear_and_free_semaphores(list(self.sem.values()) + [c[0] for c in self.chans.values()])
        self.nc.all_engine_barrier()
        self.stack.close()


D = 1024
DFF = 2816
NFC = DFF // 128
EPS = 1e-6


def load_consts(P, ident_bf=True):
    nc = P.nc
    idf = P.sb("idf", [128, 128], F32)
    idb = P.sb("idb", [128, 128], BF16)
    t0 = P.op("gpsimd", lambda e: e.memset(idf[:], 0.0))
    t = P.op("gpsimd", lambda e: e.affine_select(out=idf[:], in_=idf[:], pattern=[[-1, 128]],
                                                 compare_op=ALU.not_equal, fill=1.0, base=0,
                                                 channel_multiplier=1), waits=[t0])
    t2 = P.op("vector", lambda e: e.tensor_copy(out=idb[:], in_=idf[:]), waits=[t])
    return idf, idb, [t, t2]


def rms_tile(P, xt, gb, ut, ss, sd, rs, waits):
    t1 = P.op("scalar", lambda e: e.activation(out=ut, in_=xt, func=AF.Square, accum_out=ss), waits=waits)
    t2 = P.op("scalar", lambda e: e.activation(out=sd, in_=ss, func=AF.Sqrt, scale=1.0 / D, bias=EPS), waits=[t1])
    t3 = P.op("vector", lambda e: e.reciprocal(out=rs, in_=sd), waits=[t2])
    t4 = P.op("vector", lambda e: e.scalar_tensor_tensor(out=ut, in0=xt, scalar=rs, in1=gb, op0=ALU.mult, op1=ALU.mult),
              waits=[t3])
    return t4


def ffn_phase(nc, name, src, dst, gain, w_in, w_out, ntok):
    P = Phase(nc, name)
    NT = ntok // 128
    NG = ntok // 512
    NX = 6
    win = P.sb("win", [128, 8, 2 * DFF], BF16)
    wout = P.sb("wout", [128, NFC, D], BF16)
    gb = P.sb("gb", [128, D], F32)
    xs = [P.sb(f"x{i}", [128, D], F32) for i in range(NX)]
    us = [P.sb(f"u{i}", [128, D], BF16) for i in range(2)]
    uT = P.sb("uT", [128, 8, 512], BF16)
    gT = P.sb("gT", [128, NFC, 512], BF16)
    sg = [P.sb(f"sg{i}", [128, 512], F32) for i in range(2)]
    st = P.sb("st", [128, 3 * 8], F32)
    idf, idb, tid = load_consts(P)
    pT = P.ps("pT", [128, D], BF16)
    pg = [P.ps(f"pg{i}", [128, 512]) for i in range(2)]
    pu = [P.ps(f"pu{i}", [128, 512]) for i in range(2)]
    po = [P.ps(f"po{i}", [128, 512]) for i in range(2)]

    P.chan("wg")
    twg = P.dma("gpsimd", "wg", gb[:], gain.partition_broadcast(128))
    w_in_v = w_in.rearrange("(kc p) f -> p kc f", p=128)
    GR = [(0, 6), (6, 12), (12, 17), (17, NFC)]
    tgrp = {}
    for gi_, (f0, f1) in enumerate(GR):
        P.chan(f"w{gi_}")
        for half in range(2):
            c0, c1 = half * DFF + f0 * 128, half * DFF + f1 * 128
            t_ = P.dma("gpsimd", f"w{gi_}", win[:, :, c0:c1], w_in_v[:, :, c0:c1])
        for fc in range(f0, f1):
            tgrp[fc] = t_
    P.chan("wo")
    w_out_v = w_out.rearrange("(fc p) d -> p fc d", p=128)
    for fc in range(NFC):
        two = P.dma("gpsimd", "wo", wout[:, fc, :], w_out_v[:, fc, :])
    tw = twg

    for i in range(NX):
        P.chan(f"ld{i}")
        P.chan(f"st{i}")

    ld_tok = {}
    x_free = {}
    uT_tok = {}
    pT_free = [None]
    u_free = [None, None]
    pg_free = [[None, None], [None, None]]
    po_free = [None, None]
    gT_toks = {}
    mm2_last = [None]
    state = {"fcn": 0, "on": 0}

    def load(T):
        s = T % NX
        ld_tok[T] = P.dma("sync", f"ld{s}", xs[s][:], src[T * 128:(T + 1) * 128, :], waits=[x_free.get(s)])

    def prep(T, mm1_done):
        s = T % NX
        tt = T % 4
        ui = T % 2
        c = (T % 8) * 3
        t4 = rms_tile(P, xs[s][:], gb[:], us[ui][:], st[:, c:c + 1], st[:, c + 1:c + 2], st[:, c + 2:c + 3],
                      waits=[ld_tok[T], tw, u_free[ui]])
        tp = None
        for kc in range(8):
            tp = P.op("tensor", lambda e, kc=kc: e.transpose(pT[:, kc * 128:(kc + 1) * 128],
                                                              us[ui][:, kc * 128:(kc + 1) * 128], idb[:]),
                      waits=[t4, pT_free[0]] + tid, sig=(kc == 7))
        u_free[ui] = tp
        te = P.op("scalar", lambda e: e.copy(out=uT[:, :, tt * 128:(tt + 1) * 128],
                                             in_=pT[:].rearrange("p (k t) -> p k t", k=8)),
                  waits=[tp, mm1_done])
        pT_free[0] = te
        uT_tok[T] = te

    for T in range(min(NX, NT)):
        load(T)
    for T in range(4):
        prep(T, None)
    next_load = min(NX, NT)

    for g in range(NG):
        uw = [uT_tok[4 * g + tt] for tt in range(4)]
        last_mm1 = None
        for fc in range(NFC):
            b = fc % 2
            for half, pp in ((0, pg[b]), (1, pu[b])):
                col = half * DFF + fc * 128
                for kc in range(8):
                    t = P.op("tensor", lambda e, pp=pp, kc=kc, col=col: e.matmul(
                        pp[:], lhsT=win[:, kc, col:col + 128], rhs=uT[:, kc, :], start=(kc == 0), stop=(kc == 7)),
                        waits=uw + [tgrp[fc], pg_free[b][half], mm2_last[0]], sig=(kc == 7))
                if half == 0:
                    tg = t
                else:
                    tu = t
            ta = P.op("scalar", lambda e, b=b: e.activation(out=sg[b][:], in_=pg[b][:], func=AF.Silu),
                      waits=[tg, gT_toks.get(("sgfree", b))])
            td = P.op("vector", lambda e, b=b, fc=fc: e.tensor_tensor(out=gT[:, fc, :], in0=sg[b][:], in1=pu[b][:],
                                                                      op=ALU.mult), waits=[ta, tu])
            pg_free[b][0] = ta
            pg_free[b][1] = td
            gT_toks[("sgfree", b)] = td
            gT_toks[fc] = td
            last_mm1 = tu
        gw = [gT_toks[fc] for fc in range(NFC)]
        for tt in range(4):
            T = 4 * g + tt
            s = T % NX
            tds = []
            for dh in range(2):
                b = (2 * tt + dh) % 2
                for fc in range(NFC):
                    t = P.op("tensor", lambda e, b=b, fc=fc, tt=tt, dh=dh: e.matmul(
                        po[b][:], lhsT=gT[:, fc, tt * 128:(tt + 1) * 128], rhs=wout[:, fc, dh * 512:(dh + 1) * 512],
                        start=(fc == 0), stop=(fc == NFC - 1)), waits=gw + [two, po_free[b]], sig=(fc == NFC - 1))
                mm2_last[0] = t
                td = P.op("vector", lambda e, b=b, s=s, dh=dh: e.scalar_tensor_tensor(
                    out=xs[s][:, dh * 512:(dh + 1) * 512], in0=po[b][:], scalar=0.5,
                    in1=xs[s][:, dh * 512:(dh + 1) * 512], op0=ALU.mult, op1=ALU.add), waits=[t])
                po_free[b] = td
                tds.append(td)
            x_free[s] = P.dma("gpsimd", f"st{s}", dst[T * 128:(T + 1) * 128, :], xs[s][:], waits=tds)
            if next_load < NT and next_load % NX == s:
                load(next_load)
                next_load += 1
            Tn = 4 * (g + 1) + tt
            if Tn < NT:
                while next_load <= Tn:
                    load(next_load)
                    next_load += 1
                prep(Tn, last_mm1)
    P.emit()


HD = 64


def host_tables():
    t = np.arange(8192)
    freqs = 10000.0 ** (-np.arange(16, dtype=np.float32) / 16)
    ar = (t // 64).astype(np.float32)[:, None] * freqs[None, :]
    ac = (t % 64).astype(np.float32)[:, None] * freqs[None, :]
    cos4 = np.concatenate([np.cos(ar), np.cos(ar), np.cos(ac), np.cos(ac)], -1).astype(np.float32)
    sin4 = np.concatenate([-np.sin(ar), np.sin(ar), -np.sin(ac), np.sin(ac)], -1).astype(np.float32)
    cc = np.arange(64)
    ang = 2 * np.pi * np.outer(cc, cc) / 64
    C, S = np.cos(ang), np.sin(ang)
    bd = np.zeros((128, 2, 512), np.float32)
    for c in range(2):
        for gl in range(2):
            g = 2 * c + gl
            bd[gl * 64:(gl + 1) * 64, c, g * 64:(g + 1) * 64] = C
            bd[gl * 64:(gl + 1) * 64, c, 256 + g * 64:256 + (g + 1) * 64] = S
    return {"cos4": cos4, "sin4": sin4, "bd": bd}


def a0_phase(nc, name, src, gain, w_in, qn_g, kn_g, tabs, AB, QT, KT, V, seqs):
    P = Phase(nc, name)
    ntok = sum(seqs)
    NT = ntok // 128
    pos0 = []
    for S in seqs:
        pos0 += list(range(0, S, 128))
    wq = P.sb("wq", [128, 8, 1536], BF16)
    bd = P.sb("bd", [128, 2, 512], BF16)
    gb = P.sb("gb", [128, D], F32)
    gqk = P.sb("gqk", [128, 16, HD], F32)
    xs = [P.sb(f"x{i}", [128, D], F32) for i in range(2)]
    us = [P.sb(f"u{i}", [128, D], BF16) for i in range(2)]
    uT = [P.sb(f"uT{i}", [128, 8, 128], BF16) for i in range(2)]
    fTs = P.sb("fTs", [128, 2, 128], BF16)
    abs_ = [P.sb(f"abs{i}", [128, 512], BF16) for i in range(2)]
    sq = P.sb("sq", [128, 16, HD], F32)
    qn = P.sb("qn", [128, 16, HD], F32)
    t1 = P.sb("t1", [128, 16, HD], F32)
    t2 = P.sb("t2", [128, 16, HD], F32)
    qr = [P.sb(f"qr{i}", [128, 16, HD], BF16) for i in range(2)]
    kd = [P.sb(f"kd{i}", [128, 4, 2, HD], BF16) for i in range(2)]
    vs = [P.sb(f"vs{i}", [128, 256], BF16) for i in range(2)]
    cs = [P.sb(f"cs{i}", [128, 2, HD], F32) for i in range(2)]
    qTs = [P.sb(f"qTs{i}", [128, 6, 128], BF16) for i in range(2)]
    kTs = [P.sb(f"kTs{i}", [128, 4, 128], BF16) for i in range(2)]
    st = P.sb("st", [128, 3 * 8], F32)
    sh = P.sb("sh", [128, 3 * 16], F32)
    idf, idb, tid = load_consts(P)
    pT = P.ps("pT", [128, D], BF16)
    pF = P.ps("pF", [128, 2, 128])
    pAB = P.ps("pAB", [128, 512])
    pQ = P.ps("pQ", [128, 1536])
    pTq = P.ps("pTq", [128, 6, 128], BF16)
    pTk = P.ps("pTk", [128, 4, 128], BF16)

    P.chan("w")
    w_v = w_in.rearrange("(kc p) f -> p kc f", p=128)
    for kc in range(8):
        P.dma("gpsimd", "w", wq[:, kc, :], w_v[:, kc, :])
    P.dma("gpsimd", "w", bd[:], tabs["bd"])
    P.dma("gpsimd", "w", gb[:], gain.partition_broadcast(128))
    for h in range(16):
        P.dma("gpsimd", "w", gqk[:, h, :], (qn_g if h < 12 else kn_g).partition_broadcast(128))
    tw = P.chan_tok("w")
    for i in range(2):
        P.chan(f"ld{i}")
        P.chan(f"so{i}")

    QTv = QT.rearrange("a p t -> p a t")
    KTv = KT.rearrange("a p t -> p a t")
    free = {}

    def F(k):
        return free.get(k)

    for T in range(NT):
        i = T % 2
        c = (T % 8) * 3
        tl = P.dma("sync", f"ld{i}", xs[i][:], src[T * 128:(T + 1) * 128, :], waits=[F(("x", i))])
        P.dma("sync", f"ld{i}", cs[i][:, 0, :], tabs["cos4"][pos0[T]:pos0[T] + 128, :], waits=[F(("cs", i))])
        tl = P.dma("sync", f"ld{i}", cs[i][:, 1, :], tabs["sin4"][pos0[T]:pos0[T] + 128, :])
        t4 = rms_tile(P, xs[i][:], gb[:], us[i][:], st[:, c:c + 1], st[:, c + 1:c + 2], st[:, c + 2:c + 3],
                      waits=[tl, tw, F(("u", i))])
        free[("x", i)] = t4
        for kc in range(8):
            tp = P.op("tensor", lambda e, kc=kc, i=i: e.transpose(pT[:, kc * 128:(kc + 1) * 128],
                                                                   us[i][:, kc * 128:(kc + 1) * 128], idb[:]),
                      waits=[t4, F("pT")] + tid, sig=(kc == 7))
        free[("u", i)] = tp
        te = P.op("scalar", lambda e, i=i: e.copy(out=uT[i][:], in_=pT[:].rearrange("p (k t) -> p k t", k=8)),
                  waits=[tp, F(("uT", i))])
        free["pT"] = te
        for cch in range(2):
            for kc in range(8):
                tf = P.op("tensor", lambda e, cch=cch, kc=kc, i=i: e.matmul(
                    pF[:, cch, :], lhsT=wq[:, kc, cch * 128:(cch + 1) * 128], rhs=uT[i][:, kc, :],
                    start=(kc == 0), stop=(kc == 7)), waits=[te, tw, F("pF")], sig=(cch == 1 and kc == 7))
        tfe = P.op("vector", lambda e: e.tensor_copy(out=fTs[:], in_=pF[:]), waits=[tf, F("fTs")])
        free["pF"] = tfe
        for cch in range(2):
            tab = P.op("tensor", lambda e, cch=cch: e.matmul(pAB[:], lhsT=fTs[:, cch, :], rhs=bd[:, cch, :],
                                                             start=(cch == 0), stop=(cch == 1)),
                       waits=[tfe, F("pAB")], sig=(cch == 1))
        free["fTs"] = tab
        tabe = P.op("scalar", lambda e, i=i: e.copy(out=abs_[i][:], in_=pAB[:]), waits=[tab, F(("abs", i))])
        free["pAB"] = tabe
        P.dma("gpsimd", f"so{i}", AB[T * 128:(T + 1) * 128, :], abs_[i][:], waits=[tabe])
        for (c0, c1, o0) in ((256, 768, 0), (768, 1280, 512), (1280, 1536, 1024)):
            for kc in range(8):
                tq = P.op("tensor", lambda e, c0=c0, c1=c1, o0=o0, kc=kc, i=i: e.matmul(
                    pQ[:, o0:o0 + (c1 - c0)], lhsT=uT[i][:, kc, :], rhs=wq[:, kc, c0:c1],
                    start=(kc == 0), stop=(kc == 7)), waits=[te, tw, F("pQ")], sig=(c0 == 1280 and kc == 7))
        free[("uT", i)] = tq
        pqk = pQ[:, 0:1024].rearrange("p (h d) -> p h d", h=16)
        s1 = P.op("scalar", lambda e: e.activation(out=sq[:], in_=pqk, func=AF.Square), waits=[tq, F("sq")])
        hc = (T % 2) * 48 // 2
        ssh, sdh, rsh = sh[:, 0:16], sh[:, 16:32], sh[:, 32:48]
        s2 = P.op("vector", lambda e: e.tensor_reduce(out=ssh, in_=sq[:], axis=AX.X, op=ALU.add), waits=[s1])
        free["sq"] = s2
        s3 = P.op("scalar", lambda e: e.activation(out=sdh, in_=ssh, func=AF.Sqrt, scale=1.0 / HD, bias=EPS), waits=[s2])
        s4 = P.op("vector", lambda e: e.reciprocal(out=rsh, in_=sdh), waits=[s3])
        s5 = P.op("vector", lambda e: e.tensor_tensor(out=qn[:], in0=pqk, in1=rsh.unsqueeze(2).to_broadcast([128, 16, HD]),
                                                      op=ALU.mult), waits=[s4, F("qn")])
        tv = P.op("scalar", lambda e, i=i: e.copy(out=vs[i][:], in_=pQ[:, 1024:1280]), waits=[tq, F(("vs", i))])
        s6 = P.op("gpsimd", lambda e: e.tensor_tensor(out=qn[:], in0=qn[:], in1=gqk[:], op=ALU.mult), waits=[s5, tw])
        free["pQ"] = [s5, tv]
        P.dma("gpsimd", f"so{i}", V[T * 128:(T + 1) * 128, :], vs[i][:], waits=[tv])
        cosb = cs[i][:, 0:1, :].to_broadcast([128, 16, HD])
        r1 = P.op("vector", lambda e, cosb=cosb: e.tensor_tensor(out=t1[:], in0=qn[:], in1=cosb, op=ALU.mult),
                  waits=[s6, tl, F("t1")])
        qn5 = qn[:].rearrange("p h (a x f) -> p h a x f", a=2, x=2)
        t25 = t2[:].rearrange("p h (a x f) -> p h a x f", a=2, x=2)
        sn5 = cs[i][:, 1:2, :].rearrange("p o (a x f) -> p o a x f", a=2, x=2)
        r2 = P.op("gpsimd", lambda e, sn5=sn5: e.tensor_tensor(out=t25[:, :, :, 0, :], in0=qn5[:, :, :, 1, :],
                                                              in1=sn5[:, :, :, 0, :].to_broadcast([128, 16, 2, 16]),
                                                              op=ALU.mult), waits=[s6, tl, F("t2")])
        r3 = P.op("gpsimd", lambda e, sn5=sn5: e.tensor_tensor(out=t25[:, :, :, 1, :], in0=qn5[:, :, :, 0, :],
                                                              in1=sn5[:, :, :, 1, :].to_broadcast([128, 16, 2, 16]),
                                                              op=ALU.mult), waits=[s6, tl])
        free["qn"] = [r1, r3]
        free[("cs", i)] = [r1, r3]
        r4 = P.op("vector", lambda e, i=i: e.tensor_tensor(out=qr[i][:], in0=t1[:], in1=t2[:], op=ALU.add),
                  waits=[r1, r3, F(("qr", i))])
        free["t1"] = r4
        free["t2"] = r4
        r5 = P.op("vector", lambda e, i=i: e.tensor_copy(
            out=kd[i][:], in_=qr[i][:, 12:16, :].unsqueeze(2).to_broadcast([128, 4, 2, HD])),
            waits=[r4, F(("kd", i))])
        for a in range(6):
            tq2 = P.op("tensor", lambda e, a=a, i=i: e.transpose(
                pTq[:, a, :], qr[i][:, 2 * a:2 * a + 2, :].rearrange("p h d -> p (h d)"), idb[:]),
                waits=[r4, F("pTq")], sig=(a == 5))
        for g in range(4):
            tk2 = P.op("tensor", lambda e, g=g, i=i: e.transpose(
                pTk[:, g, :], kd[i][:, g, :, :].rearrange("p h d -> p (h d)"), idb[:]),
                waits=[r5, F("pTk")], sig=(g == 3))
        free[("qr", i)] = tk2
        free[("kd", i)] = tk2
        e1 = P.op("scalar", lambda e, i=i: e.copy(out=qTs[i][:], in_=pTq[:]), waits=[tq2, F(("qTs", i))])
        e2 = P.op("vector", lambda e, i=i: e.tensor_copy(out=kTs[i][:], in_=pTk[:]), waits=[tk2, F(("kTs", i))])
        free["pTq"] = e1
        free["pTk"] = e2
        P.dma("gpsimd", f"so{i}", QTv[:, :, T * 128:(T + 1) * 128], qTs[i][:], waits=[e1])
        tso = P.dma("gpsimd", f"so{i}", KTv[:, :, T * 128:(T + 1) * 128], kTs[i][:], waits=[e2])
        for k in ("abs", "vs", "qTs", "kTs"):
            free[(k, i)] = tso
    P.emit()


def a0_phase2(nc, name, src, gain, w_in, qn_g, kn_g, tabs, AB, QT, KT, V, seqs):
    P = Phase(nc, name)
    ntok = sum(seqs)
    NT = ntok // 128
    pos0 = []
    for S in seqs:
        pos0 += list(range(0, S, 128))
    wq = P.sb("wq", [128, 8, 1536], BF16)
    bd = P.sb("bd", [128, 2, 512], BF16)
    gb = P.sb("gb", [128, D], F32)
    gqk = P.sb("gqk", [128, 16, HD], F32)
    xs = [P.sb(f"x{i}", [128, D], F32) for i in range(2)]
    us = [P.sb(f"u{i}", [128, D], BF16) for i in range(2)]
    uT = [P.sb(f"uT{i}", [128, 8, 128], BF16) for i in range(2)]
    fTs = P.sb("fTs", [128, 2, 128], BF16)
    abs_ = [P.sb(f"abs{i}", [128, 512], BF16) for i in range(2)]
    sq = P.sb("sq", [128, 16, HD], F32)
    qn = P.sb("qn", [128, 16, HD], F32)
    t1 = P.sb("t1", [128, 16, HD], F32)
    t2 = P.sb("t2", [128, 16, HD], F32)
    qr = [P.sb(f"qr{i}", [128, 16, HD], BF16) for i in range(2)]
    kd = [P.sb(f"kd{i}", [128, 4, 2, HD], BF16) for i in range(2)]
    vs = [P.sb(f"vs{i}", [128, 256], BF16) for i in range(2)]
    cs = [P.sb(f"cs{i}", [128, 2, HD], F32) for i in range(2)]
    qTs = [P.sb(f"qTs{i}", [128, 6, 128], BF16) for i in range(2)]
    kTs = [P.sb(f"kTs{i}", [128, 4, 128], BF16) for i in range(2)]
    st = P.sb("st", [128, 3 * 8], F32)
    sh = P.sb("sh", [128, 3 * 16], F32)
    idf, idb, tid = load_consts(P)
    pT = P.ps("pT", [128, D], BF16)
    pF = P.ps("pF", [128, 2, 128])
    pAB = P.ps("pAB", [128, 512])
    pQ = P.ps("pQ", [128, 1536])
    pTq = P.ps("pTq", [128, 6, 128], BF16)
    pTk = P.ps("pTk", [128, 4, 128], BF16)

    P.chan("w")
    w_v = w_in.rearrange("(kc p) f -> p kc f", p=128)
    for kc in range(8):
        P.dma("gpsimd", "w", wq[:, kc, :], w_v[:, kc, :])
    P.dma("gpsimd", "w", bd[:], tabs["bd"])
    P.dma("gpsimd", "w", gb[:], gain.partition_broadcast(128))
    for h in range(16):
        P.dma("gpsimd", "w", gqk[:, h, :], (qn_g if h < 12 else kn_g).partition_broadcast(128))
    tw = P.chan_tok("w")
    for i in range(2):
        P.chan(f"ld{i}")
        P.chan(f"so{i}")
        P.chan(f"sa{i}")

    QTv = QT.rearrange("a p t -> p a t")
    KTv = KT.rearrange("a p t -> p a t")
    free = {}

    def F(k):
        return free.get(k)

    A = {}

    def stage_a(T):
        i = T % 2
        c = (T % 8) * 3
        tl = P.dma("sync", f"ld{i}", xs[i][:], src[T * 128:(T + 1) * 128, :], waits=[F(("x", i))])
        P.dma("sync", f"ld{i}", cs[i][:, 0, :], tabs["cos4"][pos0[T]:pos0[T] + 128, :], waits=[F(("cs", i))])
        tl = P.dma("sync", f"ld{i}", cs[i][:, 1, :], tabs["sin4"][pos0[T]:pos0[T] + 128, :])
        t4 = rms_tile(P, xs[i][:], gb[:], us[i][:], st[:, c:c + 1], st[:, c + 1:c + 2], st[:, c + 2:c + 3],
                      waits=[tl, tw, F(("u", i))])
        free[("x", i)] = t4
        for kc in range(8):
            tp = P.op("tensor", lambda e, kc=kc, i=i: e.transpose(pT[:, kc * 128:(kc + 1) * 128],
                                                                   us[i][:, kc * 128:(kc + 1) * 128], idb[:]),
                      waits=[t4, F("pT")] + tid, sig=(kc == 7))
        free[("u", i)] = tp
        te = P.op("scalar", lambda e, i=i: e.copy(out=uT[i][:], in_=pT[:].rearrange("p (k t) -> p k t", k=8)),
                  waits=[tp, F(("uT", i))])
        free["pT"] = te
        for cch in range(2):
            for kc in range(8):
                tf = P.op("tensor", lambda e, cch=cch, kc=kc, i=i: e.matmul(
                    pF[:, cch, :], lhsT=wq[:, kc, cch * 128:(cch + 1) * 128], rhs=uT[i][:, kc, :],
                    start=(kc == 0), stop=(kc == 7)), waits=[te, tw, F("pF")], sig=(cch == 1 and kc == 7))
        tfe = P.op("vector", lambda e: e.tensor_copy(out=fTs[:], in_=pF[:]), waits=[tf, F("fTs")])
        free["pF"] = tfe
        for cch in range(2):
            tab = P.op("tensor", lambda e, cch=cch: e.matmul(pAB[:], lhsT=fTs[:, cch, :], rhs=bd[:, cch, :],
                                                             start=(cch == 0), stop=(cch == 1)),
                       waits=[tfe, F("pAB")], sig=(cch == 1))
        free["fTs"] = tab
        tabe = P.op("scalar", lambda e, i=i: e.copy(out=abs_[i][:], in_=pAB[:]), waits=[tab, F(("abs", i))])
        free["pAB"] = tabe
        free[("abs", i)] = P.dma("gpsimd", f"sa{i}", AB[T * 128:(T + 1) * 128, :], abs_[i][:], waits=[tabe])
        A[T] = (te, tl)

    def stage_a2(T):
        i = T % 2
        te, tl = A.pop(T)
        for (c0, c1, o0) in ((256, 768, 0), (768, 1280, 512), (1280, 1536, 1024)):
            for kc in range(8):
                tq = P.op("tensor", lambda e, c0=c0, c1=c1, o0=o0, kc=kc, i=i: e.matmul(
                    pQ[:, o0:o0 + (c1 - c0)], lhsT=uT[i][:, kc, :], rhs=wq[:, kc, c0:c1],
                    start=(kc == 0), stop=(kc == 7)), waits=[te, tw, F("pQ")], sig=(c0 == 1280 and kc == 7))
        free[("uT", i)] = tq
        A[T] = (tq, tl)

    def stage_b(T):
        i = T % 2
        tq, tl = A.pop(T)
        pqk = pQ[:, 0:1024].rearrange("p (h d) -> p h d", h=16)
        s1 = P.op("scalar", lambda e: e.activation(out=sq[:], in_=pqk, func=AF.Square), waits=[tq, F("sq")])
        hc = (T % 2) * 48 // 2
        ssh, sdh, rsh = sh[:, 0:16], sh[:, 16:32], sh[:, 32:48]
        s2 = P.op("vector", lambda e: e.tensor_reduce(out=ssh, in_=sq[:], axis=AX.X, op=ALU.add), waits=[s1])
        free["sq"] = s2
        s3 = P.op("scalar", lambda e: e.activation(out=sdh, in_=ssh, func=AF.Sqrt, scale=1.0 / HD, bias=EPS), waits=[s2])
        s4 = P.op("vector", lambda e: e.reciprocal(out=rsh, in_=sdh), waits=[s3])
        s5 = P.op("vector", lambda e: e.tensor_tensor(out=qn[:], in0=pqk, in1=rsh.unsqueeze(2).to_broadcast([128, 16, HD]),
                                                      op=ALU.mult), waits=[s4, F("qn")])
        tv = P.op("scalar", lambda e, i=i: e.copy(out=vs[i][:], in_=pQ[:, 1024:1280]), waits=[tq, F(("vs", i))])
        free["pQ"] = [s5, tv]
        P.dma("gpsimd", f"so{i}", V[T * 128:(T + 1) * 128, :], vs[i][:], waits=[tv])
        A[T] = (tq, tl, s5, tv)

    def stage_b2(T):
        i = T % 2
        tq, tl, s5, tv = A.pop(T)
        s6 = P.op("gpsimd", lambda e: e.tensor_tensor(out=qn[:], in0=qn[:], in1=gqk[:], op=ALU.mult), waits=[s5, tw])
        cosb = cs[i][:, 0:1, :].to_broadcast([128, 16, HD])
        r1 = P.op("vector", lambda e, cosb=cosb: e.tensor_tensor(out=t1[:], in0=qn[:], in1=cosb, op=ALU.mult),
                  waits=[s6, tl, F("t1")])
        qn5 = qn[:].rearrange("p h (a x f) -> p h a x f", a=2, x=2)
        t25 = t2[:].rearrange("p h (a x f) -> p h a x f", a=2, x=2)
        sn5 = cs[i][:, 1:2, :].rearrange("p o (a x f) -> p o a x f", a=2, x=2)
        r2 = P.op("gpsimd", lambda e, sn5=sn5: e.tensor_tensor(out=t25[:, :, :, 0, :], in0=qn5[:, :, :, 1, :],
                                                              in1=sn5[:, :, :, 0, :].to_broadcast([128, 16, 2, 16]),
                                                              op=ALU.mult), waits=[s6, tl, F("t2")])
        r3 = P.op("gpsimd", lambda e, sn5=sn5: e.tensor_tensor(out=t25[:, :, :, 1, :], in0=qn5[:, :, :, 0, :],
                                                              in1=sn5[:, :, :, 1, :].to_broadcast([128, 16, 2, 16]),
                                                              op=ALU.mult), waits=[s6, tl])
        free["qn"] = [r1, r3]
        free[("cs", i)] = [r1, r3]
        r4 = P.op("vector", lambda e, i=i: e.tensor_tensor(out=qr[i][:], in0=t1[:], in1=t2[:], op=ALU.add),
                  waits=[r1, r3, F(("qr", i))])
        free["t1"] = r4
        free["t2"] = r4
        r5 = P.op("vector", lambda e, i=i: e.tensor_copy(
            out=kd[i][:], in_=qr[i][:, 12:16, :].unsqueeze(2).to_broadcast([128, 4, 2, HD])),
            waits=[r4, F(("kd", i))])
        for a in range(6):
            tq2 = P.op("tensor", lambda e, a=a, i=i: e.transpose(
                pTq[:, a, :], qr[i][:, 2 * a:2 * a + 2, :].rearrange("p h d -> p (h d)"), idb[:]),
                waits=[r4, F("pTq")], sig=(a == 5))
        for g in range(4):
            tk2 = P.op("tensor", lambda e, g=g, i=i: e.transpose(
                pTk[:, g, :], kd[i][:, g, :, :].rearrange("p h d -> p (h d)"), idb[:]),
                waits=[r5, F("pTk")], sig=(g == 3))
        free[("qr", i)] = tk2
        free[("kd", i)] = tk2
        e1 = P.op("scalar", lambda e, i=i: e.copy(out=qTs[i][:], in_=pTq[:]), waits=[tq2, F(("qTs", i))])
        e2 = P.op("vector", lambda e, i=i: e.tensor_copy(out=kTs[i][:], in_=pTk[:]), waits=[tk2, F(("kTs", i))])
        free["pTq"] = e1
        free["pTk"] = e2
        P.dma("gpsimd", f"so{i}", QTv[:, :, T * 128:(T + 1) * 128], qTs[i][:], waits=[e1])
        tso = P.dma("gpsimd", f"so{i}", KTv[:, :, T * 128:(T + 1) * 128], kTs[i][:], waits=[e2])
        for k in ("vs", "qTs", "kTs"):
            free[(k, i)] = tso

    stage_a(0)
    stage_a2(0)
    for T in range(NT):
        if T + 1 < NT:
            stage_a(T + 1)
        stage_b(T)
        if T + 1 < NT:
            stage_a2(T + 1)
        stage_b2(T)
    P.emit()


def att_phase(nc, name, QT, KT, V, mT, off, S):
    P = Phase(nc, name)
    NKC = S // 128
    NQG = S // 512
    scale = HD ** -0.5
    KTs = [P.sb(f"KT{i}", [128, S], BF16) for i in range(2)]
    Va = [P.sb(f"Va{i}", [128, NKC, HD + 1], BF16) for i in range(2)]
    QTs = [P.sb(f"QT{i}", [128, 512], BF16) for i in range(2)]
    PT = [P.sb(f"PT{i}", [128, 512], BF16) for i in range(3)]
    onesf = P.sb("onesf", [128, 64], F32)
    rden = P.sb("rden", [128, 512], F32)
    oc = P.sb("oc", [64, 512], F32)
    oT = [P.sb(f"oT{i}", [64, 512], BF16) for i in range(2)]
    pS = [P.ps(f"pS{i}", [128, 512]) for i in range(3)]
    pO = [P.ps(f"pO{i}", [128, 512]) for i in range(2)]
    pB = P.ps("pB", [64, 512])
    free = {}
    F = free.get
    t_ones = P.op("vector", lambda e: e.memset(onesf[:], 1.0))
    tm = [P.op("gpsimd", lambda e, i=i: e.memset(Va[i][:, :, HD:HD + 1], 1.0)) for i in range(2)]
    for i in range(2):
        P.chan(f"kv{i}")
        P.chan(f"q{i}")
        P.chan(f"o{i}")
    it = 0
    sidx = 0
    for g in range(4):
        gi = g % 2
        for c0 in range(0, S, 2048):
            c1 = min(S, c0 + 2048)
            P.dma("sync", f"kv{gi}", KTs[gi][:, c0:c1], KT[g, :, off + c0:off + c1], waits=[F(("kv", gi))])
        Vv = V[off:off + S, g * HD:(g + 1) * HD].rearrange("(kc p) d -> p kc d", p=128)
        for c0 in range(0, NKC, 16):
            c1 = min(NKC, c0 + 16)
            tkv = P.dma("sync", f"kv{gi}", Va[gi][:, c0:c1, 0:HD], Vv[:, c0:c1, :], waits=[tm[gi], F(("kv", gi))])
        for j in range(3):
            h = 3 * g + j
            a, half = h // 2, h % 2
            pb = 64 * half
            for qg in range(NQG):
                qi = it % 2
                tq = P.dma("sync", f"q{qi}", QTs[qi][:], QT[a, :, off + qg * 512: off + (qg + 1) * 512],
                           waits=[F(("q", qi))])
                po = pO[it % 2]
                ts_tok = {}
                te_tok = {}

                def st_mm(kc):
                    nonlocal sidx
                    b = sidx % 3
                    ts_tok[kc] = (P.op("tensor", lambda e, b=b, kc=kc, gi=gi, qi=qi, pb=pb: e.matmul(
                        pS[b][:], lhsT=KTs[gi][pb:pb + 64, kc * 128:(kc + 1) * 128], rhs=QTs[qi][pb:pb + 64, :],
                        start=True, stop=True), waits=[tkv, tq, F(("pS", b))]), b)
                    sidx += 1

                def ex(kc):
                    tok, b = ts_tok[kc]
                    t = P.op("scalar", lambda e, b=b, kc=kc: e.activation(out=PT[kc % 3][:], in_=pS[b][:], func=AF.Exp,
                                                                          scale=scale), waits=[tok])
                    free[("pS", b)] = t
                    te_tok[kc] = t

                def pv(kc):
                    return P.op("tensor", lambda e, kc=kc, po=po, gi=gi: e.matmul(
                        po[0:HD + 1, :], lhsT=Va[gi][:, kc, :], rhs=PT[kc % 3][:], start=(kc == 0), stop=(kc == NKC - 1)),
                        waits=[te_tok[kc], F(("pO", it % 2))], sig=(kc == NKC - 1))

                st_mm(0)
                if NKC > 1:
                    st_mm(1)
                tl = None
                for kc in range(NKC):
                    ex(kc)
                    tl = pv(kc)
                    if kc + 2 < NKC:
                        st_mm(kc + 2)
                free[("q", qi)] = tl
                if j == 2 and qg == NQG - 1:
                    free[("kv", gi)] = tl
                e1 = P.op("vector", lambda e, po=po: e.reciprocal(out=rden[64:65, :], in_=po[64:65, :]),
                          waits=[tl, F("rden")])
                e2 = P.op("tensor", lambda e: e.matmul(pB[:], lhsT=onesf[64:65, 0:64], rhs=rden[64:65, :],
                                                       start=True, stop=True), waits=[e1, t_ones, F("pB")])
                free["rden"] = e2
                e3 = P.op("scalar", lambda e, po=po: e.copy(out=oc[:], in_=po[0:64, :]), waits=[tl, F("oc")])
                oi = it % 2
                e4 = P.op("vector", lambda e, oi=oi: e.tensor_tensor(out=oT[oi][:], in0=oc[:], in1=pB[:], op=ALU.mult),
                          waits=[e2, e3, F(("oT", oi))])
                free["pB"] = e4
                free["oc"] = e4
                free[("pO", it % 2)] = [e1, e3]
                free[("oT", oi)] = P.dma("gpsimd", f"o{oi}", mT[256 + 64 * h:256 + 64 * (h + 1),
                                                                off + qg * 512: off + (qg + 1) * 512], oT[oi][:], waits=[e4])
                it += 1
    P.emit()


def fft_tables(S):
    N2 = S // 128
    n = np.arange(128)
    a1 = 2 * np.pi * np.outer(n, n) / 128
    k1 = np.arange(128)[:, None]; n2 = np.arange(N2)[None, :]
    at = 2 * np.pi * k1 * n2 / S
    m = np.arange(N2)
    a2 = 2 * np.pi * np.outer(m, m) / N2
    w1 = np.stack([np.cos(a1), np.sin(a1), -np.sin(a1)], 1).astype(np.float32)
    tw = np.stack([np.cos(at), np.sin(at)], 1).astype(np.float32)
    w2 = np.stack([np.cos(a2), -np.sin(a2)], 1).astype(np.float32)
    return {f"w1": w1, f"tw{S}": tw, f"w2_{S}": w2}


def fft_phase(nc, name, AB, YS, FM, tabs, off, S):
    P = Phase(nc, name)
    N2 = S // 128
    KB = 16
    NB = 128 // KB
    ABs = P.sb("ABs", [128, N2, 512], BF16)
    Ypr = P.sb("Ypr", [128, N2, 256], BF16)
    Ypq = P.sb("Ypq", [128, N2, 256], BF16)
    w1 = P.sb("w1", [128, 3, 128], BF16)
    tw = P.sb("tw", [128, 2, N2], F32)
    w2 = P.sb("w2", [N2, 2, N2], BF16)
    tmp = [P.sb(f"tmp{i}", [128, 2, 256], F32) for i in range(2)]
    Yb = [P.sb(f"Yb{i}", [N2, 2, KB, 256], BF16) for i in range(2)]
    fo = [P.sb(f"fo{i}", [N2, KB, 256], BF16) for i in range(2)]
    pY = [P.ps(f"pY{i}", [128, 2, 256]) for i in range(2)]
    pX = [P.ps(f"pX{i}", [N2, 512]) for i in range(2)]
    free = {}
    F = free.get
    P.chan("w")
    P.dma("gpsimd", "w", w1[:], tabs["w1"])
    P.dma("gpsimd", "w", tw[:], tabs[f"tw{S}"])
    P.dma("gpsimd", "w", w2[:], tabs[f"w2_{S}"])
    P.chan("ab")
    ABv = AB[off:off + S, :].rearrange("(p n) c -> p n c", p=128)
    CH = min(8, N2)
    for c0 in range(0, N2, CH):
        t_ab = P.dma("sync", "ab", ABs[:, c0:c0 + CH, :], ABv[:, c0:c0 + CH, :])
    t_w = [P.chan_tok("w"), t_ab]
    for n2 in range(N2):
        b = n2 % 2
        A = ABs[:, n2, 0:256]
        B = ABs[:, n2, 256:512]
        P.op("tensor", lambda e, b=b, A=A: e.matmul(pY[b][:, 0, :], lhsT=w1[:, 0, :], rhs=A, start=True, stop=False),
             waits=[t_w, F(("pY", b))], sig=False)
        P.op("tensor", lambda e, b=b, B=B: e.matmul(pY[b][:, 0, :], lhsT=w1[:, 2, :], rhs=B, start=False, stop=True), sig=False)
        P.op("tensor", lambda e, b=b, A=A: e.matmul(pY[b][:, 1, :], lhsT=w1[:, 1, :], rhs=A, start=True, stop=False), sig=False)
        t1 = P.op("tensor", lambda e, b=b, B=B: e.matmul(pY[b][:, 1, :], lhsT=w1[:, 0, :], rhs=B, start=False, stop=True))
        tc = tw[:, 0, n2:n2 + 1]
        ts = tw[:, 1, n2:n2 + 1]
        a1 = P.op("vector", lambda e, b=b, ts=ts: e.tensor_scalar(out=tmp[b][:, 0, :], in0=pY[b][:, 1, :], scalar1=ts,
                                                                   scalar2=None, op0=ALU.mult), waits=[t1, F(("tmp", b))])
        a2 = P.op("vector", lambda e, b=b, tc=tc: e.tensor_scalar(out=tmp[b][:, 1, :], in0=pY[b][:, 1, :], scalar1=tc,
                                                                   scalar2=None, op0=ALU.mult), waits=[t1, F(("tmp", b))])
        a3 = P.op("vector", lambda e, b=b, tc=tc, n2=n2: e.scalar_tensor_tensor(
            out=Ypr[:, n2, :], in0=pY[b][:, 0, :], scalar=tc, in1=tmp[b][:, 0, :], op0=ALU.mult, op1=ALU.subtract),
            waits=[a1])
        a4 = P.op("vector", lambda e, b=b, ts=ts, n2=n2: e.scalar_tensor_tensor(
            out=Ypq[:, n2, :], in0=pY[b][:, 0, :], scalar=ts, in1=tmp[b][:, 1, :], op0=ALU.mult, op1=ALU.add),
            waits=[a2])
        free[("pY", b)] = [a2, a4]
        free[("tmp", b)] = a4
    P.chan("ys")
    for c0 in range(0, N2, CH):
        P.dma("gpsimd", "ys", YS[0, :, c0:c0 + CH, :], Ypr[:, c0:c0 + CH, :], waits=[a4])
        tys = P.dma("gpsimd", "ys", YS[1, :, c0:c0 + CH, :], Ypq[:, c0:c0 + CH, :], waits=[a4])
    YSv = YS.rearrange("r k n c -> n r k c")
    norm = float(1.0 / np.sqrt(64.0 * S))
    for i in range(2):
        P.chan(f"yb{i}")
        P.chan(f"fo{i}")
    FMv = FM[off:off + S, :].rearrange("(k2 k1) c -> k2 k1 c", k1=128)
    gi = 0
    for blk in range(NB):
        bi = blk % 2
        for r in range(2):
            tyb = P.dma("sync", f"yb{bi}", Yb[bi][:, r, :, :], YSv[:, r, blk * KB:(blk + 1) * KB, :],
                        waits=[tys, F(("Yb", bi))])
        for grp in range(KB // 2):
            pb = gi % 2
            P.op("tensor", lambda e, pb=pb, bi=bi, grp=grp: e.matmul(
                pX[pb][:], lhsT=w2[:, 0, :], rhs=Yb[bi][:, 0, 2 * grp:2 * grp + 2, :].rearrange("n k c -> n (k c)"),
                start=True, stop=False), waits=[tyb, t_w, F(("pX", pb))], sig=False)
            t2 = P.op("tensor", lambda e, pb=pb, bi=bi, grp=grp: e.matmul(
                pX[pb][:], lhsT=w2[:, 1, :], rhs=Yb[bi][:, 1, 2 * grp:2 * grp + 2, :].rearrange("n k c -> n (k c)"),
                start=False, stop=True))
            eng = "scalar" if gi % 2 == 0 else "vector"
            if eng == "scalar":
                t3 = P.op("scalar", lambda e, pb=pb, bi=bi, grp=grp: e.activation(
                    out=fo[bi][:, 2 * grp:2 * grp + 2, :].rearrange("n k c -> n (k c)"), in_=pX[pb][:], func=AF.Copy,
                    scale=norm), waits=[t2, F(("fo", bi))])
            else:
                t3 = P.op("vector", lambda e, pb=pb, bi=bi, grp=grp: e.tensor_scalar(
                    out=fo[bi][:, 2 * grp:2 * grp + 2, :].rearrange("n k c -> n (k c)"), in0=pX[pb][:], scalar1=norm,
                    scalar2=None, op0=ALU.mult), waits=[t2, F(("fo", bi))])
            free[("pX", pb)] = t3
            gi += 1
            last2 = [t3] + ([last2[0]] if grp else [])
        free[("Yb", bi)] = t2
        free[("fo", bi)] = P.dma("gpsimd", f"fo{bi}", FMv[:, blk * KB:(blk + 1) * KB, :], fo[bi][:], waits=last2)
    P.emit()


def o0_phase(nc, name, src, dst, mT, FM, w_out, ntok):
    P = Phase(nc, name)
    NT = ntok // 128
    wo = P.sb("wo", [128, 8, D], BF16)
    xs = [P.sb(f"x{i}", [128, D], F32) for i in range(2)]
    ms = [P.sb(f"m{i}", [128, 6, 128], BF16) for i in range(2)]
    fm = [P.sb(f"fm{i}", [128, 256], BF16) for i in range(2)]
    fT = [P.sb(f"fT{i}", [128, 2, 128], BF16) for i in range(2)]
    idf, idb, tid = load_consts(P)
    pT = P.ps("pT", [128, 2, 128], BF16)
    pYo = [P.ps(f"pYo{i}", [128, 512]) for i in range(2)]
    P.chan("w")
    wv = w_out.rearrange("(kc p) d -> p kc d", p=128)
    for kc in range(8):
        P.dma("gpsimd", "w", wo[:, kc, :], wv[:, kc, :])
    tw = P.chan_tok("w")
    for i in range(2):
        P.chan(f"ld{i}")
        P.chan(f"st{i}")
    mTv = mT[256:1024, :].rearrange("(c p) t -> p c t", p=128)
    free = {}
    F = free.get
    k = 0
    for T in range(NT):
        i = T % 2
        P.dma("sync", f"ld{i}", xs[i][:], src[T * 128:(T + 1) * 128, :], waits=[F(("st", i))])
        P.dma("sync", f"ld{i}", ms[i][:], mTv[:, :, T * 128:(T + 1) * 128], waits=[F(("mm", i))])
        tl = P.dma("sync", f"ld{i}", fm[i][:], FM[T * 128:(T + 1) * 128, :], waits=[F(("tp", i))])
        for c in range(2):
            tp = P.op("tensor", lambda e, c=c, i=i: e.transpose(pT[:, c, :], fm[i][:, c * 128:(c + 1) * 128], idb[:]),
                      waits=[tl, F("pT")] + tid, sig=(c == 1))
        free[("tp", i)] = tp
        te = P.op("scalar", lambda e, i=i: e.copy(out=fT[i][:], in_=pT[:]), waits=[tp, F(("mm", i))])
        free["pT"] = te
        tds = []
        for dh in range(2):
            pb = k % 2
            k += 1
            for kc in range(8):
                lhs = (lambda i=i, kc=kc: fT[i][:, kc, :]) if kc < 2 else (lambda i=i, kc=kc: ms[i][:, kc - 2, :])
                tm = P.op("tensor", lambda e, lhs=lhs, pb=pb, kc=kc, dh=dh: e.matmul(
                    pYo[pb][:], lhsT=lhs(), rhs=wo[:, kc, dh * 512:(dh + 1) * 512], start=(kc == 0), stop=(kc == 7)),
                    waits=[te, tl, tw, F(("pYo", pb))], sig=(kc == 7))
            td = P.op("vector", lambda e, pb=pb, i=i, dh=dh: e.tensor_tensor(
                out=xs[i][:, dh * 512:(dh + 1) * 512], in0=xs[i][:, dh * 512:(dh + 1) * 512], in1=pYo[pb][:], op=ALU.add),
                waits=[tm, tl])
            free[("pYo", pb)] = td
            tds.append(td)
        free[("mm", i)] = tm
        free[("st", i)] = P.dma("gpsimd", f"st{i}", dst[T * 128:(T + 1) * 128, :], xs[i][:], waits=tds)
    P.emit()


def att_phase2(nc, name, QT, KT, V, mT, off, S):
    P = Phase(nc, name)
    NKC = S // 128
    NQG = S // 512
    scale = HD ** -0.5
    KTp = [P.sb(f"KTp{i}", [128, S], BF16) for i in range(2)]
    Va = [P.sb(f"Va{g}", [128, NKC, HD + 1], BF16) for g in range(4)]
    QTs = [P.sb(f"QT{i}", [128, 512], BF16) for i in range(2)]
    PT = [P.sb(f"PT{i}", [128, 2, 512], BF16) for i in range(3)]
    onesf = P.sb("onesf", [128, 64], F32)
    rden = P.sb("rden", [128, 2, 512], F32)
    oc = [P.sb(f"oc{i}", [HD + 1, 512], F32) for i in range(2)]
    oT = [P.sb(f"oT{i}", [64, 512], BF16) for i in range(2)]
    pS = [P.ps(f"pS{i}", [128, 2, 512]) for i in range(2)]
    pO = P.ps("pO", [128, 2, 512])
    pB = P.ps("pB", [64, 512])
    free = {}
    F = free.get
    t_ones = P.op("vector", lambda e: e.memset(onesf[:], 1.0))
    P.chan("v")
    for g in range(4):
        tm = P.op("gpsimd", lambda e, g=g: e.memset(Va[g][:, :, HD:HD + 1], 1.0))
        Vv = V[off:off + S, g * HD:(g + 1) * HD].rearrange("(kc p) d -> p kc d", p=128)
        for c0 in range(0, NKC, 16):
            c1 = min(NKC, c0 + 16)
            tv = P.dma("sync", "v", Va[g][:, c0:c1, 0:HD], Vv[:, c0:c1, :], waits=[tm])
    for i in range(2):
        P.chan(f"k{i}")
        P.chan(f"q{i}")
        P.chan(f"o{i}")
    it = 0
    sidx = 0
    pending = [None]
    oidx = [0]

    def make_finish(a, qg, e1s, e3s):
        def finish():
            for hh in range(2):
                h = 2 * a + hh
                e2 = P.op("tensor", lambda e, hh=hh: e.matmul(pB[:], lhsT=onesf[64:65, 0:64], rhs=rden[64:65, hh, :],
                                                              start=True, stop=True), waits=[e1s[hh], t_ones, F("pB")])
                oi = oidx[0] % 2
                oidx[0] += 1
                e4 = P.op("vector", lambda e, oi=oi, hh=hh: e.tensor_tensor(out=oT[oi][:], in0=oc[hh][0:64, :], in1=pB[:],
                                                                           op=ALU.mult), waits=[e2, e3s[hh], F(("oT", oi))])
                free["pB"] = e4
                free[("oc", hh)] = e4
                free[("rden", hh)] = e2
                free[("oT", oi)] = P.dma("gpsimd", f"o{oi}", mT[256 + 64 * h:256 + 64 * (h + 1),
                                                                off + qg * 512: off + (qg + 1) * 512], oT[oi][:], waits=[e4])
        return finish

    for a in range(6):
        ki = a % 2
        g0, g1 = (2 * a) // 3, (2 * a + 1) // 3
        for c0 in range(0, S, 2048):
            c1 = min(S, c0 + 2048)
            P.dma("sync", f"k{ki}", KTp[ki][0:64, c0:c1], KT[g0, 0:64, off + c0:off + c1], waits=[F(("k", ki))])
            tk = P.dma("sync", f"k{ki}", KTp[ki][64:128, c0:c1], KT[g1, 64:128, off + c0:off + c1], waits=[F(("k", ki))])
        for qg in range(NQG):
            qi = it % 2
            tq = P.dma("sync", f"q{qi}", QTs[qi][:], QT[a, :, off + qg * 512: off + (qg + 1) * 512], waits=[F(("q", qi))])
            ts_tok = {}
            te_tok = {}

            def st_mm(kc, ki=ki, qi=qi, tq=tq):
                nonlocal sidx
                b = sidx % 2
                P.op("tensor", lambda e, b=b, kc=kc: e.matmul(
                    pS[b][:, 0, :], lhsT=KTp[ki][0:64, kc * 128:(kc + 1) * 128], rhs=QTs[qi][0:64, :], start=True, stop=True),
                    waits=[tk, tq, F(("pS", b))], sig=False)
                ts_tok[kc] = (P.op("tensor", lambda e, b=b, kc=kc: e.matmul(
                    pS[b][:, 1, :], lhsT=KTp[ki][64:128, kc * 128:(kc + 1) * 128], rhs=QTs[qi][64:128, :],
                    start=True, stop=True)), b)
                sidx += 1

            def ex(kc):
                tok, b = ts_tok[kc]
                t = P.op("scalar", lambda e, b=b, kc=kc: e.activation(out=PT[kc % 3][:], in_=pS[b][:], func=AF.Exp,
                                                                      scale=scale), waits=[tok])
                free[("pS", b)] = t
                te_tok[kc] = t

            def pv(kc, g0=g0, g1=g1):
                P.op("tensor", lambda e, kc=kc: e.matmul(
                    pO[0:HD + 1, 0, :], lhsT=Va[g0][:, kc, :], rhs=PT[kc % 3][:, 0, :], start=(kc == 0), stop=(kc == NKC - 1)),
                    waits=[te_tok[kc], tv, F("pO")], sig=False)
                return P.op("tensor", lambda e, kc=kc: e.matmul(
                    pO[0:HD + 1, 1, :], lhsT=Va[g1][:, kc, :], rhs=PT[kc % 3][:, 1, :], start=(kc == 0), stop=(kc == NKC - 1)),
                    sig=(kc == NKC - 1))

            st_mm(0)
            if NKC > 1:
                st_mm(1)
            tl = None
            for kc in range(NKC):
                ex(kc)
                tl = pv(kc)
                if kc + 2 < NKC:
                    st_mm(kc + 2)
                if kc == min(3, NKC - 1) and pending[0] is not None:
                    pending[0]()
                    pending[0] = None
            free[("q", qi)] = tl
            if qg == NQG - 1:
                free[("k", ki)] = tl
            e1s, e3s = [], []
            for hh in range(2):
                e3 = P.op("scalar", lambda e, hh=hh: e.copy(out=oc[hh][:], in_=pO[0:HD + 1, hh, :]),
                          waits=[tl, F(("oc", hh))])
                e1 = P.op("vector", lambda e, hh=hh: e.reciprocal(out=rden[64:65, hh, :], in_=oc[hh][64:65, :]),
                          waits=[e3, F(("rden", hh))])
                e1s.append(e1)
                e3s.append(e3)
            free["pO"] = e3s
            pending[0] = make_finish(a, qg, e1s, e3s)
            it += 1
    if pending[0] is not None:
        pending[0]()
    P.emit()


MH = 4
MD = 256
LN16 = math.log(16.0)


def mlstm_tables():
    t = np.arange(128)
    uf = (t[:, None] <= t[None, :]).astype(np.float32)
    ub = (t[:, None] >= t[None, :]).astype(np.float32)
    return {"um": np.stack([uf, ub, np.ones((128, 128), np.float32)], 1)}


def a1_phase(nc, name, src, gain, w_in, gate_bias, ZT, V1, O1, G, ntok):
    P = Phase(nc, name)
    NT = ntok // 128
    wc = P.sb("wc", [128, 8, 4112], BF16)
    gb = P.sb("gb", [128, D], F32)
    bias = P.sb("bias", [128, 16], F32)
    xs = [P.sb(f"x{i}", [128, D], F32) for i in range(2)]
    us = [P.sb(f"u{i}", [128, D], BF16) for i in range(2)]
    uT = [P.sb(f"uT{i}", [128, 8, 128], BF16) for i in range(2)]
    zs = [P.sb(f"zs{i}", [128, 16, 128], F32) for i in range(2)]
    vs = [P.sb(f"vs{i}", [128, D], BF16) for i in range(2)]
    os_ = [P.sb(f"os{i}", [128, D], BF16) for i in range(2)]
    gs = [P.sb(f"gs{i}", [128, 16], F32) for i in range(2)]
    st = P.sb("st", [128, 3 * 8], F32)
    idf, idb, tid = load_consts(P)
    pT = P.ps("pT", [128, D], BF16)
    pZ = P.ps("pZ", [128, 8, 128])
    pVO = P.ps("pVO", [128, 2048])
    pG = P.ps("pG", [128, 16])
    P.chan("w")
    w_v = w_in.rearrange("(kc p) f -> p kc f", p=128)
    for kc in range(8):
        P.dma("gpsimd", "w", wc[:, kc, :], w_v[:, kc, :])
    P.dma("gpsimd", "w", gb[:], gain.partition_broadcast(128))
    P.dma("gpsimd", "w", bias[:], gate_bias.partition_broadcast(128))
    tw = P.chan_tok("w")
    for i in range(2):
        P.chan(f"ld{i}")
        P.chan(f"so{i}")
    ZTv = ZT.rearrange("(c p) t -> p c t", p=128)
    free = {}
    F = free.get
    for T in range(NT):
        i = T % 2
        c = (T % 8) * 3
        tl = P.dma("sync", f"ld{i}", xs[i][:], src[T * 128:(T + 1) * 128, :], waits=[F(("x", i))])
        t4 = rms_tile(P, xs[i][:], gb[:], us[i][:], st[:, c:c + 1], st[:, c + 1:c + 2], st[:, c + 2:c + 3],
                      waits=[tl, tw, F(("u", i))])
        free[("x", i)] = t4
        for kc in range(8):
            tp = P.op("tensor", lambda e, kc=kc, i=i: e.transpose(pT[:, kc * 128:(kc + 1) * 128],
                                                                   us[i][:, kc * 128:(kc + 1) * 128], idb[:]),
                      waits=[t4, F("pT")] + tid, sig=(kc == 7))
        free[("u", i)] = tp
        te = P.op("scalar", lambda e, i=i: e.copy(out=uT[i][:], in_=pT[:].rearrange("p (k t) -> p k t", k=8)),
                  waits=[tp, F(("uT", i))])
        free["pT"] = te
        for hf in range(2):
            for cc in range(8):
                ch = hf * 8 + cc
                for kc in range(8):
                    tz = P.op("tensor", lambda e, cc=cc, ch=ch, kc=kc, i=i: e.matmul(
                        pZ[:, cc, :], lhsT=wc[:, kc, ch * 128:(ch + 1) * 128], rhs=uT[i][:, kc, :],
                        start=(kc == 0), stop=(kc == 7)), waits=[te, tw, F("pZ")], sig=(cc == 7 and kc == 7))
            eng = "vector" if hf == 0 else "scalar"
            if eng == "vector":
                tze = P.op("vector", lambda e, hf=hf, i=i: e.tensor_copy(out=zs[i][:, hf * 8:(hf + 1) * 8, :], in_=pZ[:]),
                           waits=[tz, F(("zs", i))])
            else:
                tze = P.op("scalar", lambda e, hf=hf, i=i: e.copy(out=zs[i][:, hf * 8:(hf + 1) * 8, :], in_=pZ[:]),
                           waits=[tz, F(("zs", i))])
            free["pZ"] = tze
            P.dma("gpsimd", f"so{i}", ZTv[:, hf * 8:(hf + 1) * 8, T * 128:(T + 1) * 128], zs[i][:, hf * 8:(hf + 1) * 8, :],
                  waits=[tze])
        for gq in range(4):
            for kc in range(8):
                tvo = P.op("tensor", lambda e, gq=gq, kc=kc, i=i: e.matmul(
                    pVO[:, gq * 512:(gq + 1) * 512], lhsT=uT[i][:, kc, :], rhs=wc[:, kc, 2048 + gq * 512:2048 + (gq + 1) * 512],
                    start=(kc == 0), stop=(kc == 7)), waits=[te, tw, F("pVO")], sig=(gq == 3 and kc == 7))
        for kc in range(8):
            tg = P.op("tensor", lambda e, kc=kc, i=i: e.matmul(pG[:], lhsT=uT[i][:, kc, :], rhs=wc[:, kc, 4096:4112],
                                                               start=(kc == 0), stop=(kc == 7)),
                      waits=[te, tw, F("pG")], sig=(kc == 7))
        free[("uT", i)] = tg
        ev = P.op("vector", lambda e, i=i: e.tensor_copy(out=vs[i][:], in_=pVO[:, 0:1024]), waits=[tvo, F(("so", i))])
        eo = P.op("scalar", lambda e, i=i: e.activation(out=os_[i][:], in_=pVO[:, 1024:2048], func=AF.Sigmoid),
                  waits=[tvo, F(("so", i))])
        free["pVO"] = [ev, eo]
        eg = P.op("vector", lambda e, i=i: e.tensor_tensor(out=gs[i][:], in0=pG[:], in1=bias[:], op=ALU.add),
                  waits=[tg, tw, F(("so", i))])
        free["pG"] = eg
        P.dma("gpsimd", f"so{i}", V1[T * 128:(T + 1) * 128, :], vs[i][:], waits=[ev])
        P.dma("gpsimd", f"so{i}", O1[T * 128:(T + 1) * 128, :], os_[i][:], waits=[eo])
        tso = P.dma("gpsimd", f"so{i}", G[T * 128:(T + 1) * 128, :], gs[i][:], waits=[eg])
        free[("so", i)] = tso
        free[("zs", i)] = tso
    P.emit()


def sweep_phase(nc, name, direction, ZT, QK, V1, O1, G, HB, conv_w, tabs, seqs, fin=None):
    P = Phase(nc, name)
    fwd = direction == "f"
    d8 = 0 if fwd else 8
    zs = [P.sb(f"zs{i}", [128, 16, 130], F32) for i in range(2)]
    acc = P.sb("acc", [128, 16, 128], F32)
    tmpc = P.sb("tmpc", [128, 16, 128], F32)
    qkT = [P.sb(f"qkT{i}", [128, 16, 128], BF16) for i in range(2)]
    va = [P.sb(f"va{i}", [128, MH, MD + 1], BF16) for i in range(2)]
    gt = [P.sb(f"gt{i}", [128, 8], F32) for i in range(2)]
    kt = [P.sb(f"kt{i}", [128, MH, MD], BF16) for i in range(2)]
    sqk = [P.sb(f"sqk{i}", [128, 128], BF16) for i in range(2)]
    Cst = P.sb("Cst", [128, MH, 2, MD + 1], F32)
    Cbf = P.sb("Cbf", [128, MH, 2, MD + 1], BF16)
    hbuf = [P.sb(f"hbuf{i}", [128, D], F32) for i in range(2)]
    um = P.sb("um", [128, 3, 128], F32)
    cw = P.sb("cw", [128, 3, 16], F32)
    gsm = [P.sb(f"gsm{i}", [128, 40], F32) for i in range(2)]
    dsm = [P.sb(f"dsm{i}", [128, 16], F32) for i in range(2)]
    idf, idb, tid = load_consts(P)
    pS = P.ps("pS", [128, 2, 128])
    pA = P.ps("pA", [128, 2, 512])
    pC = P.ps("pC", [128, 2, 512])
    pT = P.ps("pT", [128, 8, 128], BF16)
    pG = P.ps("pG", [128, 8])
    P.chan("w")
    P.dma("gpsimd", "w", um[:], tabs["um"])
    cwT = P.sb("cwT", [48, 128], F32)
    P.dma("gpsimd", "w", cwT[:], conv_w.rearrange("j (c p) -> (j c) p", p=128))
    if fwd:
        wo = P.sb("wo", [128, 8, D], BF16)
        gh = P.sb("gh", [128, D], F32)
        hbt = [P.sb(f"hbt{i}", [128, D], F32) for i in range(2)]
        og = [P.sb(f"og{i}", [128, D], BF16) for i in range(2)]
        xs = [P.sb(f"x{i}", [128, D], F32) for i in range(2)]
        sq = P.sb("sq", [128, D], F32)
        hn = P.sb("hn", [128, D], F32)
        mb = P.sb("mb", [128, D], BF16)
        mTs = P.sb("mTs", [128, 8, 128], BF16)
        hsm = P.sb("hsm", [128, 12], F32)
        pY = P.ps("pY", [128, 512])
        wv = fin["w_out"].rearrange("(kc p) d -> p kc d", p=128)
        for kc in range(8):
            P.dma("gpsimd", "w", wo[:, kc, :], wv[:, kc, :])
        P.dma("gpsimd", "w", gh[:], fin["head_norm"].partition_broadcast(128))
    tw0 = P.chan_tok("w")
    tcw1 = P.op("tensor", lambda e: e.transpose(pC[:, 0, 0:48], cwT[:], idf[0:48, 0:48]), waits=[tw0] + tid)
    tcw2 = P.op("vector", lambda e: e.tensor_copy(out=cw[:].rearrange("p j c -> p (j c)"), in_=pC[:, 0, 0:48]), waits=[tcw1])
    tw = [tw0, tcw2]
    tm = [P.op("gpsimd", lambda e, i=i: e.memset(va[i][:, :, MD:MD + 1], 1.0)) for i in range(2)]
    for i in range(2):
        P.chan(f"ld{i}")
        P.chan(f"st{i}")
    ZTv = ZT.rearrange("(c p) t -> p c t", p=128)
    QKv = QK.rearrange("(c p) t -> p c t", p=128)
    free = {"pC": tcw2}
    F = free.get
    umask = um[:, 0 if fwd else 1, :]
    uones = um[:, 2, :]

    sched = []
    off = 0
    for S in seqs:
        n = S // 128
        order = range(n) if fwd else range(n - 1, -1, -1)
        for k, ci in enumerate(order):
            sched.append((off + ci * 128, ci == 0, ci == n - 1, k == 0))
        off += S

    cstate = [None]
    for it, (t0, seq_first, seq_last, reset) in enumerate(sched):
        i = it % 2
        lo = 1 if seq_first else 0
        hi = 129 if seq_last else 130
        lw = [F(("zs", i))]
        if not fwd:
            P.dma("sync", f"ld{i}", zs[i][:, :, lo:hi], ZTv[:, :, t0 - 1 + lo:t0 - 1 + hi], waits=lw)
        else:
            P.dma("sync", f"ld{i}", qkT[i][:], QKv[:, :, t0:t0 + 128], waits=[F(("qkT", i))])
        P.dma("sync", f"ld{i}", va[i][:, :, 0:MD], V1[t0:t0 + 128, :].rearrange("p (h d) -> p h d", h=MH),
              waits=[tm[i], F(("va", i))])
        tl = P.dma("sync", f"ld{i}", gt[i][:], G[t0:t0 + 128, d8:d8 + 8], waits=[F(("gt", i))])
        if fwd:
            P.dma("sync", f"ld{i}", hbt[i][:], HB[t0:t0 + 128, :], waits=[F(("fin", i))])
            P.dma("sync", f"ld{i}", og[i][:], O1[t0:t0 + 128, :])
            tl = P.dma("sync", f"ld{i}", xs[i][:], fin["src"][t0:t0 + 128, :], waits=[F(("xst", i))])
        tz = []
        if seq_first and not fwd:
            tz.append(P.op("gpsimd", lambda e, i=i: e.memset(zs[i][:, :, 0:1], 0.0), waits=lw))
        if seq_last and not fwd:
            tz.append(P.op("gpsimd", lambda e, i=i: e.memset(zs[i][:, :, 129:130], 0.0), waits=lw))
        if reset:
            rw = [cstate[0], F("Cbf_r")] + [F(("Cst", h)) for h in range(MH)]
            r1 = P.op("gpsimd", lambda e: e.memset(Cst[:], 0.0), waits=rw)
            r2 = P.op("gpsimd", lambda e: e.memset(Cbf[:], 0.0), waits=rw)
            cstate[0] = [r1, r2]
        def cwb(j):
            return cw[:, j, :].unsqueeze(2).to_broadcast([128, 16, 128])
        if not fwd:
            c1 = P.op("vector", lambda e, i=i: e.tensor_tensor(out=acc[:], in0=zs[i][:, :, 0:128], in1=cwb(0), op=ALU.mult),
                      waits=[tl, tw, F("acc")] + tz)
            c2 = P.op("gpsimd", lambda e, i=i: e.tensor_tensor(out=tmpc[:], in0=zs[i][:, :, 1:129], in1=cwb(1), op=ALU.mult),
                      waits=[tl, tw, F("tmpc")] + tz)
            c3 = P.op("vector", lambda e: e.tensor_tensor(out=acc[:], in0=acc[:], in1=tmpc[:], op=ALU.add), waits=[c1, c2])
            c4 = P.op("gpsimd", lambda e, i=i: e.tensor_tensor(out=tmpc[:], in0=zs[i][:, :, 2:130], in1=cwb(2), op=ALU.mult),
                      waits=[c3, tl] + tz)
            free[("zs", i)] = c4
            c5 = P.op("vector", lambda e: e.tensor_tensor(out=acc[:], in0=acc[:], in1=tmpc[:], op=ALU.add), waits=[c4])
            free["tmpc"] = c5
            c6 = P.op("scalar", lambda e, i=i: e.activation(out=qkT[i][:], in_=acc[:], func=AF.Silu),
                      waits=[c5, F(("qkT", i))])
            free["acc"] = c6
            tqk = P.dma("gpsimd", f"st{i}", QKv[:, :, t0:t0 + 128], qkT[i][:], waits=[c6])
        else:
            c6 = tl
            tqk = None
        g_ = gsm[i]
        e1, l1, tmpg, al, ej, eL = (g_[:, 0:4], g_[:, 4:8], g_[:, 8:12], g_[:, 12:16], g_[:, 16:20], g_[:, 20:24])
        ig = gt[i][:, 0:4]
        fg = gt[i][:, 4:8]
        g1 = P.op("scalar", lambda e, e1=e1, fg=fg: e.activation(out=e1, in_=fg, func=AF.Exp, scale=-1.0),
                  waits=[tl, F(("gsm", i))])
        g2 = P.op("scalar", lambda e, e1=e1, l1=l1: e.activation(out=l1, in_=e1, func=AF.Ln, bias=1.0), waits=[g1])
        P.op("tensor", lambda e, l1=l1: e.matmul(pG[:, 0:4], lhsT=umask, rhs=l1, start=True, stop=True),
             waits=[g2, tw, F("pG")], sig=False)
        g3 = P.op("tensor", lambda e, l1=l1: e.matmul(pG[:, 4:8], lhsT=uones, rhs=l1, start=True, stop=True))
        g4 = P.op("vector", lambda e, tmpg=tmpg, ig=ig: e.tensor_tensor(out=tmpg, in0=ig, in1=pG[:, 0:4], op=ALU.add),
                  waits=[g3, tl])
        free[("gt", i)] = g4
        g5 = P.op("scalar", lambda e, al=al, tmpg=tmpg: e.activation(out=al, in_=tmpg, func=AF.Exp, bias=-LN16), waits=[g4])
        g6 = P.op("scalar", lambda e, ej=ej: e.activation(out=ej, in_=pG[:, 0:4], func=AF.Exp, scale=-1.0), waits=[g3])
        g7 = P.op("scalar", lambda e, eL=eL: e.activation(out=eL, in_=pG[:, 4:8], func=AF.Exp, scale=-1.0), waits=[g3])
        free["pG"] = [g4, g7]
        for c in range(8):
            tk = P.op("tensor", lambda e, c=c, i=i: e.transpose(pT[:, c, :], qkT[i][:, 8 + c, :], idb[:]),
                      waits=[c6, F("pT")] + tid, sig=(c == 7))
        k1 = P.op("vector", lambda e, i=i, al=al: e.tensor_tensor(
            out=kt[i][:], in0=pT[:].rearrange("p (h c) t -> p h (c t)", h=MH),
            in1=al.unsqueeze(2).to_broadcast([128, MH, MD]), op=ALU.mult), waits=[tk, g5, F(("kt", i))])
        free["pT"] = k1
        hts = []
        last_pe = None
        for h in range(MH):
            b = h % 2
            for dc in range(2):
                ts_ = P.op("tensor", lambda e, b=b, h=h, dc=dc, i=i: e.matmul(
                    pS[:, b, :], lhsT=qkT[i][:, 8 + 2 * h + dc, :], rhs=qkT[i][:, 2 * h + dc, :],
                    start=(dc == 0), stop=(dc == 1)), waits=[c6, F(("pS", b))], sig=(dc == 1))
            s1 = P.op("vector", lambda e, b=b, h=h, al=al: e.scalar_tensor_tensor(
                out=sqk[b][:], in0=pS[:, b, :], scalar=al[:, h:h + 1], in1=umask, op0=ALU.mult, op1=ALU.mult),
                waits=[ts_, g5, tw, F(("sqk", b))])
            free[("pS", b)] = s1
            P.op("tensor", lambda e, b=b, h=h, i=i: e.matmul(pA[:, b, 0:MD + 1], lhsT=sqk[b][:], rhs=va[i][:, h, :],
                                                             start=True, stop=False),
                 waits=[s1, tl, F(("pA", b))], sig=False)
            for dc in range(2):
                ta = P.op("tensor", lambda e, b=b, h=h, dc=dc, i=i: e.matmul(
                    pA[:, b, 0:MD + 1], lhsT=qkT[i][:, 2 * h + dc, :], rhs=Cbf[:, h, dc, :], start=False, stop=(dc == 1)),
                    waits=[cstate[0], F(("Cbf", h))], sig=(dc == 1))
            free[("sqk", b)] = ta
            for dc in range(2):
                tc_ = P.op("tensor", lambda e, h=h, dc=dc, i=i: e.matmul(
                    pC[:, dc, 0:MD + 1], lhsT=kt[i][:, h, dc * 128:(dc + 1) * 128], rhs=va[i][:, h, :],
                    start=True, stop=True), waits=[k1, tl, F("pC")], sig=(dc == 1))
            last_pe = tc_
            dd = dsm[i]
            d1, d2, rr, sc = (dd[:, 4 * h:4 * h + 1], dd[:, 4 * h + 1:4 * h + 2], dd[:, 4 * h + 2:4 * h + 3],
                              dd[:, 4 * h + 3:4 * h + 4])
            o0 = P.op("vector", lambda e, b=b, h=h, d1=d1, ej=ej: e.tensor_scalar(
                out=d1, in0=pA[:, b, MD:MD + 1], scalar1=ej[:, h:h + 1], scalar2=None, op0=ALU.mult),
                waits=[ta, g6, F(("dsm", i))])
            o1 = P.op("vector", lambda e, d1=d1: e.scalar_tensor_tensor(out=d1, in0=d1, scalar=-1.0, in1=d1,
                                                                       op0=ALU.mult, op1=ALU.max), waits=[o0])
            o2 = P.op("vector", lambda e, d1=d1, d2=d2: e.tensor_scalar(out=d2, in0=d1, scalar1=1.0, scalar2=None,
                                                                         op0=ALU.max), waits=[o1])
            o3 = P.op("vector", lambda e, d2=d2, rr=rr: e.reciprocal(out=rr, in_=d2), waits=[o2])
            o4 = P.op("vector", lambda e, rr=rr, sc=sc, ej=ej, h=h: e.tensor_tensor(out=sc, in0=rr, in1=ej[:, h:h + 1],
                                                                                     op=ALU.mult), waits=[o3])
            o5 = P.op("vector", lambda e, b=b, h=h, i=i, sc=sc: e.tensor_scalar(
                out=hbuf[i][:, h * MD:(h + 1) * MD], in0=pA[:, b, 0:MD], scalar1=sc, scalar2=None, op0=ALU.mult),
                waits=[o4, F(("hbuf", i))])
            free[("pA", b)] = o5
            hts.append(o5)
            uA = P.op("vector", lambda e, h=h, eL=eL: e.tensor_scalar(out=Cst[:, h, :, :], in0=Cst[:, h, :, :],
                                                                      scalar1=eL[:, h:h + 1], scalar2=None, op0=ALU.mult),
                      waits=[g7, cstate[0], F(("Cst", h))])
            u2 = P.op("vector", lambda e, h=h, eL=eL: e.scalar_tensor_tensor(
                out=Cst[:, h, :, :], in0=pC[:, :, 0:MD + 1], scalar=eL[:, h:h + 1], in1=Cst[:, h, :, :],
                op0=ALU.mult, op1=ALU.add), waits=[tc_, uA])
            free["pC"] = u2
            u3 = P.op("scalar", lambda e, h=h: e.copy(out=Cbf[:, h, :, :], in_=Cst[:, h, :, :]), waits=[u2, ta])
            free[("Cst", h)] = u3
            free[("Cbf", h)] = u3
        free["Cbf_r"] = ta
        free[("qkT", i)] = [last_pe, tqk]
        free[("kt", i)] = last_pe
        free[("va", i)] = last_pe
        free[("gsm", i)] = [u2, o4, k1]
        free[("dsm", i)] = hts[-1]
        if not fwd:
            tst = P.dma("gpsimd", f"st{i}", HB[t0:t0 + 128, :], hbuf[i][:], waits=hts)
            free[("hbuf", i)] = tst
            free[("qkT", i)] = [last_pe, tst]
        else:
            f1 = P.op("gpsimd", lambda e, i=i: e.tensor_tensor(out=hbuf[i][:], in0=hbuf[i][:], in1=hbt[i][:], op=ALU.add),
                      waits=hts + [tl])
            f2 = P.op("scalar", lambda e, i=i: e.activation(out=sq[:], in_=hbuf[i][:], func=AF.Square),
                      waits=[f1, F("sq")])
            ssh, sdh, rsh = hsm[:, 0:4], hsm[:, 4:8], hsm[:, 8:12]
            f3 = P.op("vector", lambda e: e.tensor_reduce(out=ssh, in_=sq[:].rearrange("p (h d) -> p h d", h=MH),
                                                          axis=AX.X, op=ALU.add), waits=[f2])
            free["sq"] = f3
            f4 = P.op("scalar", lambda e: e.activation(out=sdh, in_=ssh, func=AF.Sqrt, scale=1.0 / MD, bias=EPS), waits=[f3])
            f5 = P.op("vector", lambda e: e.reciprocal(out=rsh, in_=sdh), waits=[f4])
            f6 = P.op("vector", lambda e, i=i: e.tensor_tensor(
                out=hn[:].rearrange("p (h d) -> p h d", h=MH), in0=hbuf[i][:].rearrange("p (h d) -> p h d", h=MH),
                in1=rsh.unsqueeze(2).to_broadcast([128, MH, MD]), op=ALU.mult), waits=[f5, F("hn")])
            free[("hbuf", i)] = f6
            f7 = P.op("gpsimd", lambda e: e.tensor_tensor(out=hn[:], in0=hn[:], in1=gh[:], op=ALU.mult), waits=[f6, tw])
            f8 = P.op("vector", lambda e, i=i: e.tensor_tensor(out=mb[:], in0=hn[:], in1=og[i][:], op=ALU.mult),
                      waits=[f7, tl, F("mb")])
            free["hn"] = f8
            for kc in range(8):
                f9 = P.op("tensor", lambda e, kc=kc: e.transpose(pT[:, kc, :], mb[:, kc * 128:(kc + 1) * 128], idb[:]),
                          waits=[f8, F("pT")], sig=(kc == 7))
            free["mb"] = f9
            f10 = P.op("scalar", lambda e: e.copy(out=mTs[:], in_=pT[:]), waits=[f9, F("mTs")])
            free["pT"] = f10
            tds = []
            for dh in range(2):
                for kc in range(8):
                    f11 = P.op("tensor", lambda e, kc=kc, dh=dh: e.matmul(
                        pY[:], lhsT=mTs[:, kc, :], rhs=wo[:, kc, dh * 512:(dh + 1) * 512], start=(kc == 0), stop=(kc == 7)),
                        waits=[f10, tw, F("pY")], sig=(kc == 7))
                f12 = P.op("vector", lambda e, i=i, dh=dh: e.tensor_tensor(
                    out=xs[i][:, dh * 512:(dh + 1) * 512], in0=xs[i][:, dh * 512:(dh + 1) * 512], in1=pY[:], op=ALU.add),
                    waits=[f11, tl])
                free["pY"] = f12
                tds.append(f12)
            free["mTs"] = f11
            free[("fin", i)] = [f1, f8]
            free[("xst", i)] = P.dma("gpsimd", f"st{i}", fin["dst"][t0:t0 + 128, :], xs[i][:], waits=tds)
    P.emit()


def sweep_phase2(nc, name, direction, ZT, QK, V1, O1, G, HB, conv_w, tabs, seqs, fin=None):
    P = Phase(nc, name)
    fwd = direction == "f"
    d8 = 0 if fwd else 8
    ND = 10
    qkT = [P.sb(f"qkT{i}", [128, 16, 128], BF16) for i in range(2)]
    va = [P.sb(f"va{i}", [128, MH, MD + 1], BF16) for i in range(2)]
    gt = [P.sb(f"gt{i}", [128, 8], F32) for i in range(2)]
    kt = [P.sb(f"kt{i}", [128, MH, MD], BF16) for i in range(2)]
    sqk = [[P.sb(f"sqk{i}_{h}", [128, 128], BF16) for h in range(MH)] for i in range(2)]
    CstL = [P.sb(f"Cst{k}", [128, MH, 2, MD + 1], F32) for k in range(2)]
    CbfL = [P.sb(f"Cbf{k}", [128, MH, 2, MD + 1], BF16) for k in range(2)]
    hbuf = [P.sb(f"hbuf{i}", [128, D], F32) for i in range(2)]
    um = P.sb("um", [128, 3, 128], F32)
    gsm = [P.sb(f"gsm{i}", [128, 32], F32) for i in range(2)]
    dsm = P.sb("dsm", [128, 16], F32)
    idf, idb, tid = load_consts(P)
    pS = P.ps("pS", [128, 2, 512])
    pA = P.ps("pA", [128, 2, 512])
    pC = P.ps("pC", [128, 2, 512])
    pT = P.ps("pT", [128, 8, 128], BF16)
    pG = P.ps("pG", [128, 8])
    P.chan("w")
    P.dma("gpsimd", "w", um[:], tabs["um"])
    if not fwd:
        zs = [P.sb(f"zs{i}", [128, 16, 130], F32) for i in range(2)]
        acc = P.sb("acc", [128, 16, 128], F32)
        tmpc = P.sb("tmpc", [128, 16, 128], F32)
        cw = P.sb("cw", [128, 3, 16], F32)
        cwT = P.sb("cwT", [48, 128], F32)
        P.dma("gpsimd", "w", cwT[:], conv_w.rearrange("j (c p) -> (j c) p", p=128))
    else:
        wo = P.sb("wo", [128, 8, D], BF16)
        gh = P.sb("gh", [128, D], F32)
        hbt = [P.sb(f"hbt{i}", [128, D], F32) for i in range(2)]
        og = [P.sb(f"og{i}", [128, D], BF16) for i in range(2)]
        xs = [P.sb(f"x{i}", [128, D], F32) for i in range(2)]
        ogh = [P.sb(f"ogh{i}", [128, D], F32) for i in range(2)]
        sq = P.sb("sq", [128, D], F32)
        hn = P.sb("hn", [128, D], F32)
        mb = P.sb("mb", [128, D], BF16)
        mTs = P.sb("mTs", [128, 8, 128], BF16)
        hsm = P.sb("hsm", [128, 12], F32)
        wv = fin["w_out"].rearrange("(kc p) d -> p kc d", p=128)
        for kc in range(8):
            P.dma("gpsimd", "w", wo[:, kc, :], wv[:, kc, :])
        P.dma("gpsimd", "w", gh[:], fin["head_norm"].partition_broadcast(128))
    tw0 = P.chan_tok("w")
    free = {}
    F = free.get
    if not fwd:
        tcw1 = P.op("tensor", lambda e: e.transpose(pC[:, 0, 0:48], cwT[:], idf[0:48, 0:48]), waits=[tw0] + tid)
        tcw2 = P.op("vector", lambda e: e.tensor_copy(out=cw[:].rearrange("p j c -> p (j c)"), in_=pC[:, 0, 0:48]),
                    waits=[tcw1])
        tw = [tw0, tcw2]
        free["pC"] = tcw2
    else:
        tw = [tw0]
    tm = [P.op("gpsimd", lambda e, i=i: e.memset(va[i][:, :, MD:MD + 1], 1.0)) for i in range(2)]
    for i in range(2):
        P.chan(f"ld{i}")
        P.chan(f"st{i}")
    ZTv = ZT.rearrange("(c p) t -> p c t", p=128)
    QKv = QK.rearrange("(c p) t -> p c t", p=128)
    umask = um[:, 0 if fwd else 1, :]
    uones = um[:, 2, :]

    def seq_chunks(off, S):
        n = S // 128
        order = range(n) if fwd else range(n - 1, -1, -1)
        return [(off + ci * 128, ci == 0, ci == n - 1, k == 0) for k, ci in enumerate(order)]
    sched = []
    offs = []
    off = 0
    for S in seqs:
        offs.append(off)
        off += S
    k = 0
    while k < len(seqs):
        if k + 1 < len(seqs) and seqs[k] == seqs[k + 1]:
            c0s, c1s = seq_chunks(offs[k], seqs[k]), seq_chunks(offs[k + 1], seqs[k + 1])
            assert len(sched) % 2 == 0
            for x0, x1 in zip(c0s, c1s):
                sched.append(x0 + (0,))
                sched.append(x1 + (1,))
            k += 2
        else:
            sched += [x + (0,) for x in seq_chunks(offs[k], seqs[k])]
            k += 1
    A = {}

    def stage_a(it):
        t0, seq_first, seq_last, reset, sid = sched[it]
        i = it % 2
        if not fwd:
            lo = 1 if seq_first else 0
            hi = 129 if seq_last else 130
            lw = [F(("zs", i))]
            P.dma("sync", f"ld{i}", zs[i][:, :, lo:hi], ZTv[:, :, t0 - 1 + lo:t0 - 1 + hi], waits=lw)
        else:
            P.dma("sync", f"ld{i}", qkT[i][:], QKv[:, :, t0:t0 + 128], waits=[F(("qkT", i))])
        P.dma("sync", f"ld{i}", va[i][:, :, 0:MD], V1[t0:t0 + 128, :].rearrange("p (h d) -> p h d", h=MH),
              waits=[tm[i], F(("va", i))])
        tl = P.dma("sync", f"ld{i}", gt[i][:], G[t0:t0 + 128, d8:d8 + 8], waits=[F(("gt", i))])
        if fwd:
            P.dma("sync", f"ld{i}", hbt[i][:], HB[t0:t0 + 128, :], waits=[F(("fin", i))])
            P.dma("sync", f"ld{i}", og[i][:], O1[t0:t0 + 128, :])
            tl = P.dma("sync", f"ld{i}", xs[i][:], fin["src"][t0:t0 + 128, :], waits=[F(("xst", i))])
        tqk = None
        togh = None
        if fwd:
            togh = P.op("gpsimd", lambda e: e.tensor_tensor(out=ogh[i][:], in0=og[i][:], in1=gh[:], op=ALU.mult),
                        waits=[tl, tw, F(("ogh", i))])
        if not fwd:
            tz = []
            if seq_first:
                tz.append(P.op("gpsimd", lambda e: e.memset(zs[i][:, :, 0:1], 0.0), waits=lw))
            if seq_last:
                tz.append(P.op("gpsimd", lambda e: e.memset(zs[i][:, :, 129:130], 0.0), waits=lw))
            cend = []
            for eng, c0, c1 in (("vector", 0, ND), ("gpsimd", ND, 16)):
                nch = c1 - c0

                def cwb(j, c0=c0, c1=c1, nch=nch):
                    return cw[:, j, c0:c1].unsqueeze(2).to_broadcast([128, nch, 128])
                k1_ = P.op(eng, lambda e, c0=c0, c1=c1, cwb=cwb: e.tensor_tensor(
                    out=acc[:, c0:c1, :], in0=zs[i][:, c0:c1, 0:128], in1=cwb(0), op=ALU.mult),
                    waits=[tl, tw, F("acc")] + tz)
                k2_ = P.op(eng, lambda e, c0=c0, c1=c1, cwb=cwb: e.tensor_tensor(
                    out=tmpc[:, c0:c1, :], in0=zs[i][:, c0:c1, 1:129], in1=cwb(1), op=ALU.mult), waits=[k1_])
                k3_ = P.op(eng, lambda e, c0=c0, c1=c1: e.tensor_tensor(
                    out=acc[:, c0:c1, :], in0=acc[:, c0:c1, :], in1=tmpc[:, c0:c1, :], op=ALU.add), waits=[k2_])
                k4_ = P.op(eng, lambda e, c0=c0, c1=c1, cwb=cwb: e.tensor_tensor(
                    out=tmpc[:, c0:c1, :], in0=zs[i][:, c0:c1, 2:130], in1=cwb(2), op=ALU.mult), waits=[k3_])
                k5_ = P.op(eng, lambda e, c0=c0, c1=c1: e.tensor_tensor(
                    out=acc[:, c0:c1, :], in0=acc[:, c0:c1, :], in1=tmpc[:, c0:c1, :], op=ALU.add), waits=[k4_])
                cend.append(k5_)
            free[("zs", i)] = cend
            c6 = P.op("scalar", lambda e: e.activation(out=qkT[i][:], in_=acc[:], func=AF.Silu),
                      waits=cend + [F(("qkT", i))])
            free["acc"] = c6
            tqk = P.dma("gpsimd", f"st{i}", QKv[:, :, t0:t0 + 128], qkT[i][:], waits=[c6])
        else:
            c6 = tl
        g_ = gsm[i]
        e1, l1, tmpg, al, einv, eL = (g_[:, 0:4], g_[:, 4:8], g_[:, 8:12], g_[:, 12:16], g_[:, 16:20], g_[:, 20:24])
        ig = gt[i][:, 0:4]
        fg = gt[i][:, 4:8]
        g1 = P.op("scalar", lambda e: e.activation(out=e1, in_=fg, func=AF.Exp, scale=-1.0), waits=[tl, F(("gsm", i))])
        g2 = P.op("scalar", lambda e: e.activation(out=l1, in_=e1, func=AF.Ln, bias=1.0), waits=[g1])
        P.op("tensor", lambda e: e.matmul(pG[:, 0:4], lhsT=umask, rhs=l1, start=True, stop=True),
             waits=[g2, tw, F("pG")], sig=False)
        g3 = P.op("tensor", lambda e: e.matmul(pG[:, 4:8], lhsT=uones, rhs=l1, start=True, stop=True))
        g4 = P.op("vector", lambda e: e.tensor_tensor(out=tmpg, in0=ig, in1=pG[:, 0:4], op=ALU.add), waits=[g3, tl])
        free[("gt", i)] = g4
        g5 = P.op("scalar", lambda e: e.activation(out=al, in_=tmpg, func=AF.Exp, bias=-LN16), waits=[g4])
        g6 = P.op("scalar", lambda e: e.activation(out=einv, in_=pG[:, 0:4], func=AF.Exp), waits=[g3])
        g7 = P.op("scalar", lambda e: e.activation(out=eL, in_=pG[:, 4:8], func=AF.Exp, scale=-1.0), waits=[g3])
        free["pG"] = [g4, g7]
        for c in range(8):
            tk = P.op("tensor", lambda e, c=c: e.transpose(pT[:, c, :], qkT[i][:, 8 + c, :], idb[:]),
                      waits=[c6, F("pT")] + tid, sig=(c == 7))
        k1 = P.op("vector", lambda e: e.tensor_tensor(
            out=kt[i][:], in0=pT[:].rearrange("p (h c) t -> p h (c t)", h=MH),
            in1=al.unsqueeze(2).to_broadcast([128, MH, MD]), op=ALU.mult), waits=[tk, g5, F(("kt", i))])
        free["pT"] = k1
        s1s = []
        for h in range(MH):
            b = h % 2
            for dc in range(2):
                ts_ = P.op("tensor", lambda e, b=b, h=h, dc=dc: e.matmul(
                    pS[:, b, 0:128], lhsT=qkT[i][:, 8 + 2 * h + dc, :], rhs=qkT[i][:, 2 * h + dc, :],
                    start=(dc == 0), stop=(dc == 1)), waits=[c6, F(("pS", b))], sig=(dc == 1))
            s1 = P.op("vector", lambda e, b=b, h=h: e.scalar_tensor_tensor(
                out=sqk[i][h][:], in0=pS[:, b, 0:128], scalar=al[:, h:h + 1], in1=umask, op0=ALU.mult, op1=ALU.mult),
                waits=[ts_, g5, tw, F(("sqk", i, h))])
            free[("pS", b)] = s1
            s1s.append(s1)
        A[it] = dict(tl=tl, c6=c6, k1=k1, g6=g6, g7=g7, s1s=s1s, tqk=tqk, einv=einv, eL=eL, togh=togh)

    def stage_b(it):
        t0, seq_first, seq_last, reset, sid = sched[it]
        i = it % 2
        Cst, Cbf = CstL[sid], CbfL[sid]
        a = A.pop(it)
        tl, c6, k1, g6, g7, s1s, tqk, einv, eL = (a["tl"], a["c6"], a["k1"], a["g6"], a["g7"], a["s1s"], a["tqk"],
                                                  a["einv"], a["eL"])
        togh = a["togh"]
        sqs = []
        rtok = None
        if reset:
            rw = [F(("Cbf_r", sid))] + [F(("Cst", sid, h)) for h in range(MH)]
            r1 = P.op("gpsimd", lambda e: e.memset(Cst[:], 0.0), waits=rw)
            r2 = P.op("gpsimd", lambda e: e.memset(Cbf[:], 0.0), waits=rw)
            rtok = [r1, r2]
        hts = []
        last_pe = None
        gl = []
        for h in range(MH):
            b = h % 2
            for dc in range(2):
                tc_ = P.op("tensor", lambda e, h=h, dc=dc: e.matmul(
                    pC[:, dc, 0:MD + 1], lhsT=kt[i][:, h, dc * 128:(dc + 1) * 128], rhs=va[i][:, h, :],
                    start=True, stop=True), waits=[k1, tl, F("pC")], sig=(dc == 1))
            last_pe = tc_
            uA = P.op("gpsimd", lambda e, h=h: e.tensor_tensor(
                out=Cst[:, h, :, :], in0=Cst[:, h, :, :], in1=eL[:, h:h + 1].unsqueeze(2).to_broadcast([128, 2, MD + 1]),
                op=ALU.mult), waits=[g7, rtok, F(("Cst", sid, h))])
            u2 = P.op("vector", lambda e, h=h: e.scalar_tensor_tensor(
                out=Cst[:, h, :, :], in0=pC[:, :, 0:MD + 1], scalar=eL[:, h:h + 1], in1=Cst[:, h, :, :],
                op0=ALU.mult, op1=ALU.add), waits=[tc_, uA, g7])
            free["pC"] = u2
            P.op("tensor", lambda e, b=b, h=h: e.matmul(pA[:, b, 0:MD + 1], lhsT=sqk[i][h][:], rhs=va[i][:, h, :],
                                                        start=True, stop=False),
                 waits=[s1s[h], tl, F(("pA", b))], sig=False)
            for dc in range(2):
                ta = P.op("tensor", lambda e, b=b, h=h, dc=dc: e.matmul(
                    pA[:, b, 0:MD + 1], lhsT=qkT[i][:, 2 * h + dc, :], rhs=Cbf[:, h, dc, :], start=False, stop=(dc == 1)),
                    waits=[c6, rtok, F(("Cbf", sid, h))], sig=(dc == 1))
            free[("sqk", i, h)] = ta
            u3 = P.op("scalar", lambda e, h=h: e.copy(out=Cbf[:, h, :, :], in_=Cst[:, h, :, :]), waits=[u2, ta])
            free[("Cst", sid, h)] = u3
            free[("Cbf", sid, h)] = u3
            ad, dmx, rr = dsm[:, 4 * h:4 * h + 1], dsm[:, 4 * h + 1:4 * h + 2], dsm[:, 4 * h + 2:4 * h + 3]
            o0 = P.op("scalar", lambda e, b=b, ad=ad: e.activation(out=ad, in_=pA[:, b, MD:MD + 1], func=AF.Abs),
                      waits=[ta, F(("dsm", h))])
            o1 = P.op("vector", lambda e, h=h, ad=ad, dmx=dmx: e.tensor_tensor(out=dmx, in0=ad, in1=einv[:, h:h + 1],
                                                                               op=ALU.max), waits=[o0, g6])
            o3 = P.op("vector", lambda e, dmx=dmx, rr=rr: e.reciprocal(out=rr, in_=dmx), waits=[o1])
            if not fwd:
                o5 = P.op("vector", lambda e, b=b, h=h, rr=rr: e.tensor_scalar(
                    out=hbuf[i][:, h * MD:(h + 1) * MD], in0=pA[:, b, 0:MD], scalar1=rr, scalar2=None, op0=ALU.mult),
                    waits=[o3, F(("hbuf", i))])
            else:
                o5 = P.op("vector", lambda e, b=b, h=h, rr=rr: e.scalar_tensor_tensor(
                    out=hbuf[i][:, h * MD:(h + 1) * MD], in0=pA[:, b, 0:MD], scalar=rr,
                    in1=hbt[i][:, h * MD:(h + 1) * MD], op0=ALU.mult, op1=ALU.add), waits=[o3, tl, F(("hbuf", i))])
                sqs.append(P.op("scalar", lambda e, h=h: e.activation(
                    out=sq[:, h * MD:(h + 1) * MD], in_=hbuf[i][:, h * MD:(h + 1) * MD], func=AF.Square,
                    accum_out=hsm[:, h:h + 1]), waits=[o5, F("hsm")]))
            free[("pA", b)] = o5
            free[("dsm", h)] = o5
            hts.append(o5)
            gl += [u2, o1]
        free[("Cbf_r", sid)] = ta
        last_pe = ta
        free[("kt", i)] = last_pe
        free[("va", i)] = last_pe
        free[("gsm", i)] = gl
        if not fwd:
            tst = P.dma("gpsimd", f"st{i}", HB[t0:t0 + 128, :], hbuf[i][:], waits=hts)
            free[("hbuf", i)] = tst
            free[("qkT", i)] = [last_pe, tst]
        else:
            free[("qkT", i)] = last_pe
            ssh, sdh, rsh = hsm[:, 0:4], hsm[:, 4:8], hsm[:, 8:12]
            f4 = P.op("scalar", lambda e: e.activation(out=sdh, in_=ssh, func=AF.Sqrt, scale=1.0 / MD, bias=EPS), waits=sqs)
            f5 = P.op("vector", lambda e: e.reciprocal(out=rsh, in_=sdh), waits=[f4])
            for h in range(MH):
                f8 = P.op("vector", lambda e, h=h: e.scalar_tensor_tensor(
                    out=mb[:, h * MD:(h + 1) * MD], in0=hbuf[i][:, h * MD:(h + 1) * MD], scalar=rsh[:, h:h + 1],
                    in1=ogh[i][:, h * MD:(h + 1) * MD], op0=ALU.mult, op1=ALU.mult), waits=[f5, togh, F("mb")])
            free[("hbuf", i)] = f8
            free["hsm"] = f8
            free[("ogh", i)] = f8
            f1 = sqs[-1]
            for kc in range(8):
                f9 = P.op("tensor", lambda e, kc=kc: e.transpose(pT[:, kc, :], mb[:, kc * 128:(kc + 1) * 128], idb[:]),
                          waits=[f8, F("pT")], sig=(kc == 7))
            free["mb"] = f9
            f10 = P.op("scalar", lambda e: e.copy(out=mTs[:], in_=pT[:]), waits=[f9, F("mTs")])
            free["pT"] = f10
            tds = []
            for dh in range(2):
                for kc in range(8):
                    f11 = P.op("tensor", lambda e, kc=kc, dh=dh: e.matmul(
                        pC[:, 0, :], lhsT=mTs[:, kc, :], rhs=wo[:, kc, dh * 512:(dh + 1) * 512], start=(kc == 0), stop=(kc == 7)),
                        waits=[f10, tw, F("pC")], sig=(kc == 7))
                f12 = P.op("vector", lambda e, dh=dh: e.tensor_tensor(
                    out=xs[i][:, dh * 512:(dh + 1) * 512], in0=xs[i][:, dh * 512:(dh + 1) * 512], in1=pC[:, 0, :], op=ALU.add),
                    waits=[f11, tl])
                free["pC"] = f12
                tds.append(f12)
            free["mTs"] = f11
            free[("fin", i)] = [f8, togh]
            free[("xst", i)] = P.dma("gpsimd", f"st{i}", fin["dst"][t0:t0 + 128, :], xs[i][:], waits=tds)

    n = len(sched)
    stage_a(0)
    for it in range(n):
        if it + 1 < n:
            stage_a(it + 1)
        stage_b(it)
    P.emit()


def a1_phase2(nc, name, src, gain, w_in, gate_bias, ZT, V1, O1, G, ntok):
    P = Phase(nc, name)
    NT = ntok // 128
    NG = ntok // 512
    wc = P.sb("wc", [128, 8, 4112], BF16)
    gb = P.sb("gb", [128, D], F32)
    bias = P.sb("bias", [128, 16], F32)
    xs = [P.sb(f"x{i}", [128, D], F32) for i in range(2)]
    us = [P.sb(f"u{i}", [128, D], BF16) for i in range(2)]
    uT = [P.sb(f"uT{i}", [128, 8, 512], BF16) for i in range(2)]
    zs = [P.sb(f"zs{i}", [128, 16, 512], F32) for i in range(2)]
    vs = [P.sb(f"vs{i}", [128, D], BF16) for i in range(2)]
    os_ = [P.sb(f"os{i}", [128, D], BF16) for i in range(2)]
    gs = [P.sb(f"gs{i}", [128, 16], F32) for i in range(2)]
    st = P.sb("st", [128, 3 * 8], F32)
    idf, idb, tid = load_consts(P)
    pT = P.ps("pT", [128, D], BF16)
    pZ = [P.ps(f"pZ{i}", [128, 512]) for i in range(2)]
    pVO = P.ps("pVO", [128, 2048])
    pG = P.ps("pG", [128, 16])
    P.chan("w")
    w_v = w_in.rearrange("(kc p) f -> p kc f", p=128)
    for kc in range(8):
        P.dma("gpsimd", "w", wc[:, kc, :], w_v[:, kc, :])
    P.dma("gpsimd", "w", gb[:], gain.partition_broadcast(128))
    P.dma("gpsimd", "w", bias[:], gate_bias.partition_broadcast(128))
    tw = P.chan_tok("w")
    for i in range(2):
        P.chan(f"ld{i}")
        P.chan(f"so{i}")
        P.chan(f"sz{i}")
    ZTv = ZT.rearrange("(c p) t -> p c t", p=128)
    free = {}
    F = free.get
    zc = 0
    for g in range(NG):
        gi = g % 2
        tes = []
        for tt in range(4):
            T = 4 * g + tt
            i = T % 2
            c = (T % 8) * 3
            tl = P.dma("sync", f"ld{i}", xs[i][:], src[T * 128:(T + 1) * 128, :], waits=[F(("x", i))])
            t4 = rms_tile(P, xs[i][:], gb[:], us[i][:], st[:, c:c + 1], st[:, c + 1:c + 2], st[:, c + 2:c + 3],
                          waits=[tl, tw, F(("u", i))])
            free[("x", i)] = t4
            for kc in range(8):
                tp = P.op("tensor", lambda e, kc=kc, i=i: e.transpose(pT[:, kc * 128:(kc + 1) * 128],
                                                                       us[i][:, kc * 128:(kc + 1) * 128], idb[:]),
                          waits=[t4, F("pT")] + tid, sig=(kc == 7))
            free[("u", i)] = tp
            te = P.op("scalar", lambda e, gi=gi, tt=tt: e.copy(out=uT[gi][:, :, tt * 128:(tt + 1) * 128],
                                                               in_=pT[:].rearrange("p (k t) -> p k t", k=8)),
                      waits=[tp, F(("uT", gi))])
            free["pT"] = te
            tes.append(te)
            for gq in range(4):
                for kc in range(8):
                    tvo = P.op("tensor", lambda e, gq=gq, kc=kc, gi=gi, tt=tt: e.matmul(
                        pVO[:, gq * 512:(gq + 1) * 512], lhsT=uT[gi][:, kc, tt * 128:(tt + 1) * 128],
                        rhs=wc[:, kc, 2048 + gq * 512:2048 + (gq + 1) * 512],
                        start=(kc == 0), stop=(kc == 7)), waits=[te, tw, F("pVO")], sig=(gq == 3 and kc == 7))
            for kc in range(8):
                tg = P.op("tensor", lambda e, kc=kc, gi=gi, tt=tt: e.matmul(
                    pG[:], lhsT=uT[gi][:, kc, tt * 128:(tt + 1) * 128], rhs=wc[:, kc, 4096:4112],
                    start=(kc == 0), stop=(kc == 7)), waits=[te, tw, F("pG")], sig=(kc == 7))
            ev = P.op("vector", lambda e, i=i: e.tensor_copy(out=vs[i][:], in_=pVO[:, 0:1024]), waits=[tvo, F(("so", i))])
            eo = P.op("scalar", lambda e, i=i: e.activation(out=os_[i][:], in_=pVO[:, 1024:2048], func=AF.Sigmoid),
                      waits=[tvo, F(("so", i))])
            free["pVO"] = [ev, eo]
            eg = P.op("vector", lambda e, i=i: e.tensor_tensor(out=gs[i][:], in0=pG[:], in1=bias[:], op=ALU.add),
                      waits=[tg, tw, F(("so", i))])
            free["pG"] = eg
            P.dma("gpsimd", f"so{i}", V1[T * 128:(T + 1) * 128, :], vs[i][:], waits=[ev])
            P.dma("gpsimd", f"so{i}", O1[T * 128:(T + 1) * 128, :], os_[i][:], waits=[eo])
            free[("so", i)] = P.dma("gpsimd", f"so{i}", G[T * 128:(T + 1) * 128, :], gs[i][:], waits=[eg])
        tzes = []
        for ch in range(16):
            pb = zc % 2
            zc += 1
            for kc in range(8):
                tz = P.op("tensor", lambda e, pb=pb, ch=ch, kc=kc, gi=gi: e.matmul(
                    pZ[pb][:], lhsT=wc[:, kc, ch * 128:(ch + 1) * 128], rhs=uT[gi][:, kc, :],
                    start=(kc == 0), stop=(kc == 7)), waits=tes + [tw, F(("pZ", pb))], sig=(kc == 7))
            if ch % 2 == 0:
                tze = P.op("vector", lambda e, pb=pb, ch=ch, gi=gi: e.tensor_copy(out=zs[gi][:, ch, :], in_=pZ[pb][:]),
                           waits=[tz, F(("zs", gi))])
            else:
                tze = P.op("scalar", lambda e, pb=pb, ch=ch, gi=gi: e.copy(out=zs[gi][:, ch, :], in_=pZ[pb][:]),
                           waits=[tz, F(("zs", gi))])
            free[("pZ", pb)] = tze
            tzes.append(tze)
            if ch % 4 == 3:
                tsz = P.dma("gpsimd", f"sz{gi}", ZTv[:, ch - 3:ch + 1, g * 512:(g + 1) * 512], zs[gi][:, ch - 3:ch + 1, :],
                            waits=tzes[-4:])
        free[("uT", gi)] = tz
        free[("zs", gi)] = tsz
    P.emit()


SEQS = [4096, 4096, 8192]


def build_program(seqs=SEQS):
    nc = bass.Bass("TRN2", target_bir_lowering=False)
    ntok = sum(seqs)

    def inp(n, s, dt=F32):
        return nc.dram_tensor(n, list(s), dt, kind="ExternalInput").ap()

    def scr(n, s, dt=F32):
        return nc.dram_tensor(n, list(s), dt).ap()

    x = inp("x", [ntok, D])
    y = nc.dram_tensor("y", [ntok, D], F32, kind="ExternalOutput").ap()
    W = {}
    for l in range(2):
        for k in (1, 2):
            W[f"f{k}n{l}"] = inp(f"f{k}n{l}", [D])
            W[f"f{k}i{l}"] = inp(f"f{k}i{l}", [D, 2 * DFF])
            W[f"f{k}o{l}"] = inp(f"f{k}o{l}", [DFF, D])
        W[f"mn{l}"] = inp(f"mn{l}", [D])
    W["abi"] = inp("abi", [D, 1536]); W["abq"] = inp("abq", [64]); W["abk"] = inp("abk", [64]); W["abo"] = inp("abo", [D, D])
    W["ci"] = inp("ci", [D, 4112]); W["cgb"] = inp("cgb", [16]); W["cc"] = inp("cc", [3, 2048]); W["chn"] = inp("chn", [D])
    W["co"] = inp("co", [D, D])
    tabs_np = const_tables(seqs)
    tabs = {k: inp(k, v.shape) for k, v in tabs_np.items()}
    xa = scr("xa", [ntok, D]); xb = scr("xb", [ntok, D]); xc = scr("xc", [ntok, D]); xd = scr("xd", [ntok, D])
    AB = scr("AB", [ntok, 512], BF16); QT = scr("QT", [6, 128, ntok], BF16); KT = scr("KT", [4, 128, ntok], BF16)
    V = scr("V", [ntok, 256], BF16); mT = scr("mT", [1024, ntok], BF16); FM = scr("FM", [ntok, 256], BF16)
    ffn_phase(nc, "fa", x, xa, W["f1n0"], W["f1i0"], W["f1o0"], ntok)
    a0_phase2(nc, "a0", xa, W["mn0"], W["abi"], W["abq"], W["abk"], tabs, AB, QT, KT, V, seqs)
    off = 0
    for si, S in enumerate(seqs):
        YS = scr(f"YS{si}", [2, 128, S // 128, 256], BF16)
        fft_phase(nc, f"ff{si}", AB, YS, FM, tabs, off, S)
        att_phase2(nc, f"at{si}", QT, KT, V, mT, off, S)
        off += S
    o0_phase(nc, "o0", xa, xb, mT, FM, W["abo"], ntok)
    ffn_phase(nc, "fb", xb, xc, W["f2n0"], W["f2i0"], W["f2o0"], ntok)
    ffn_phase(nc, "fc", xc, xd, W["f1n1"], W["f1i1"], W["f1o1"], ntok)
    xe = mlstm_layer(nc, xd, W, tabs, seqs)
    ffn_phase(nc, "fd", xe, y, W["f2n1"], W["f2i1"], W["f2o1"], ntok)
    return nc, tabs_np


def const_tables(seqs):
    t = host_tables()
    for S in sorted(set(seqs)):
        t.update(fft_tables(S))
    t.update(mlstm_tables())
    return t


def mlstm_layer(nc, xd, W, tabs, seqs):
    ntok = sum(seqs)
    ZT = nc.dram_tensor("ZT", [2048, ntok], F32).ap()
    V1 = nc.dram_tensor("V1", [ntok, D], BF16).ap()
    O1 = nc.dram_tensor("O1", [ntok, D], BF16).ap()
    G = nc.dram_tensor("G", [ntok, 16], F32).ap()
    QK = nc.dram_tensor("QK", [2048, ntok], BF16).ap()
    HB = nc.dram_tensor("HB", [ntok, D], F32).ap()
    xe = nc.dram_tensor("xe", [ntok, D], F32).ap()
    a1_phase2(nc, "a1", xd, W["mn1"], W["ci"], W["cgb"], ZT, V1, O1, G, ntok)
    sweep_phase2(nc, "sb", "b", ZT, QK, V1, O1, G, HB, W["cc"], tabs, seqs)
    sweep_phase2(nc, "sf", "f", ZT, QK, V1, O1, G, HB, W["cc"], tabs, seqs,
                fin={"w_out": W["co"], "head_norm": W["chn"], "src": xd, "dst": xe})
    return xe


_CACHE = {}


def kernel(x_prompt, x_sample, ffn1_norm, ffn1_w_in, ffn1_w_out, mix_norm, ab_w_in, ab_q_norm, ab_k_norm, ab_w_out,
           c_w_in, c_gate_bias, c_conv, c_head_norm, c_w_out, ffn2_norm, ffn2_w_in, ffn2_w_out):
    f = lambda a: np.ascontiguousarray(np.asarray(a, dtype=np.float32))
    if "nc" not in _CACHE:
        _CACHE["nc"] = build_program()
    nc, tabs_np = _CACHE["nc"]
    xp, xs = f(x_prompt), f(x_sample)
    shared = {}
    for l in range(2):
        shared[f"f1n{l}"] = f(ffn1_norm[l]); shared[f"f1i{l}"] = f(ffn1_w_in[l]); shared[f"f1o{l}"] = f(ffn1_w_out[l])
        shared[f"f2n{l}"] = f(ffn2_norm[l]); shared[f"f2i{l}"] = f(ffn2_w_in[l]); shared[f"f2o{l}"] = f(ffn2_w_out[l])
        shared[f"mn{l}"] = f(mix_norm[l])
    shared["abi"] = f(ab_w_in[0]); shared["abq"] = f(ab_q_norm[0]); shared["abk"] = f(ab_k_norm[0]); shared["abo"] = f(ab_w_out[0])
    shared["ci"] = f(c_w_in[0]); shared["cgb"] = f(c_gate_bias[0]); shared["cc"] = f(c_conv[0]); shared["chn"] = f(c_head_norm[0])
    shared["co"] = f(c_w_out[0])
    shared.update(tabs_np)
    in_maps = []
    for c in range(8):
        xc = np.concatenate([xp[2 * c], xp[2 * c + 1], xs[c]], axis=0)
        in_maps.append({"x": xc, **shared})
    res = run_bass_kernel_spmd(nc, in_maps, core_ids=list(range(8)))
    yp = np.empty_like(xp)
    ys = np.empty_like(xs)
    for c in range(8):
        yc = res.results[c]["y"]
        yp[2 * c] = yc[0:4096]
        yp[2 * c + 1] = yc[4096:8192]
        ys[c] = yc[8192:16384]
    return (yp, ys)
```

```python
from contextlib import ExitStack
import math
import numpy as np
import concourse.bass as bass
import concourse.mybir as mybir
from concourse.bass_utils import run_bass_kernel_spmd

F32 = mybir.dt.float32
BF16 = mybir.dt.bfloat16
AF = mybir.ActivationFunctionType
ALU = mybir.AluOpType
AX = mybir.AxisListType

ENGS = ("tensor", "vector", "scalar", "gpsimd", "sync")


class Phase:
    def __init__(self, nc, name):
        self.nc = nc
        self.name = name
        self.stack = ExitStack()
        self.q = {e: [] for e in ENGS}
        self.sem = {e: nc.alloc_semaphore(name=f"{name}_{e}") for e in ENGS}
        self.cnt = {e: 0 for e in ENGS}
        self.seen = {e: {} for e in ENGS}
        self.chans = {}
        self.nalloc = 0

    def sb(self, name, shape, dtype):
        return self.stack.enter_context(self.nc.sbuf_tensor(f"{self.name}_{name}", list(shape), dtype))

    def ps(self, name, shape, dtype=F32):
        return self.stack.enter_context(self.nc.psum_tensor(f"{self.name}_{name}", list(shape), dtype))

    def chan(self, name):
        if name not in self.chans:
            s = self.nc.alloc_semaphore(name=f"{self.name}_c_{name}")
            self.chans[name] = [s, 0]
        return name

    def _waits(self, eng, waits):
        wl = []
        for tok in waits:
            if tok is None:
                continue
            if isinstance(tok, list):
                wl += self._waits(eng, tok)
                continue
            kind, key, val = tok
            if kind == "e" and key == eng and eng == "tensor":
                continue
            k = (kind, key)
            if self.seen[eng].get(k, 0) >= val:
                continue
            self.seen[eng][k] = val
            sem = self.sem[key] if kind == "e" else self.chans[key][0]
            wl.append((sem, val))
        return wl

    def op(self, eng, fn, waits=(), sig=True):
        wl = self._waits(eng, waits)
        tok = None
        if sig:
            self.cnt[eng] += 1
            tok = ("e", eng, self.cnt[eng])
        self.q[eng].append((fn, wl, self.sem[eng] if sig else None, 1))
        return tok

    def dma(self, eng, chan, out, in_, waits=(), **kw):
        wl = self._waits(eng, waits)
        c = self.chans[chan]
        c[1] += 16
        self.q[eng].append((lambda e: e.dma_start(out=out, in_=in_, **kw), wl, c[0], 16))
        return ("d", chan, c[1])

    def chan_tok(self, chan):
        c = self.chans[chan]
        return ("d", chan, c[1]) if c[1] else None

    def emit(self, final_waits=()):
        fin = []
        for e in ENGS:
            if self.cnt[e]:
                fin.append(("e", e, self.cnt[e]))
        for cname, c in self.chans.items():
            if c[1]:
                fin.append(("d", cname, c[1]))
        endw = self._waits("gpsimd", fin)
        with self.nc.Block() as block:
            for eng in ENGS:
                def f(e, eng=eng):
                    for fn, wl, sem, inc in self.q[eng]:
                        for (s, v) in wl:
                            e.wait_ge(s, v)
                        ins = fn(e)
                        if sem is not None:
                            ins.then_inc(sem, inc)
                    if eng == "gpsimd":
                        for (s, v) in endw:
                            e.wait_ge(s, v)
                getattr(block, eng)(f)
        self.nc.all_engine_barrier()
        self.nc.clear_and_free_semaphores(list(self.sem.values()) + [c[0] for c in self.chans.values()])
        self.nc.all_engine_barrier()
        self.stack.close()


D = 1024
DFF = 2816
NFC = DFF // 128
EPS = 1e-6


def load_consts(P, ident_bf=True):
    nc = P.nc
    idf = P.sb("idf", [128, 128], F32)
    idb = P.sb("idb", [128, 128], BF16)
    t0 = P.op("gpsimd", lambda e: e.memset(idf[:], 0.0))
    t = P.op("gpsimd", lambda e: e.affine_select(out=idf[:], in_=idf[:], pattern=[[-1, 128]],
                                                 compare_op=ALU.not_equal, fill=1.0, base=0,
                                                 channel_multiplier=1), waits=[t0])
    t2 = P.op("vector", lambda e: e.tensor_copy(out=idb[:], in_=idf[:]), waits=[t])
    return idf, idb, [t, t2]


def rms_tile(P, xt, gb, ut, ss, sd, rs, waits):
    t1 = P.op("scalar", lambda e: e.activation(out=ut, in_=xt, func=AF.Square, accum_out=ss), waits=waits)
    t2 = P.op("scalar", lambda e: e.activation(out=sd, in_=ss, func=AF.Sqrt, scale=1.0 / D, bias=EPS), waits=[t1])
    t3 = P.op("vector", lambda e: e.reciprocal(out=rs, in_=sd), waits=[t2])
    t4 = P.op("vector", lambda e: e.scalar_tensor_tensor(out=ut, in0=xt, scalar=rs, in1=gb, op0=ALU.mult, op1=ALU.mult),
              waits=[t3])
    return t4


def ffn_phase(nc, name, src, dst, gain, w_in, w_out, ntok):
    P = Phase(nc, name)
    NT = ntok // 128
    NG = ntok // 512
    NX = 6
    win = P.sb("win", [128, 8, 2 * DFF], BF16)
    wout = P.sb("wout", [128, NFC, D], BF16)
    gb = P.sb("gb", [128, D], F32)
    xs = [P.sb(f"x{i}", [128, D], F32) for i in range(NX)]
    us = [P.sb(f"u{i}", [128, D], BF16) for i in range(2)]
    uT = P.sb("uT", [128, 8, 512], BF16)
    gT = P.sb("gT", [128, NFC, 512], BF16)
    sg = [P.sb(f"sg{i}", [128, 512], F32) for i in range(2)]
    st = P.sb("st", [128, 3 * 8], F32)
    idf, idb, tid = load_consts(P)
    pT = P.ps("pT", [128, D], BF16)
    pg = [P.ps(f"pg{i}", [128, 512]) for i in range(2)]
    pu = [P.ps(f"pu{i}", [128, 512]) for i in range(2)]
    po = [P.ps(f"po{i}", [128, 512]) for i in range(2)]

    P.chan("wg")
    twg = P.dma("gpsimd", "wg", gb[:], gain.partition_broadcast(128))
    w_in_v = w_in.rearrange("(kc p) f -> p kc f", p=128)
    GR = [(0, 6), (6, 12), (12, 17), (17, NFC)]
    tgrp = {}
    for gi_, (f0, f1) in enumerate(GR):
        P.chan(f"w{gi_}")
        for half in range(2):
            c0, c1 = half * DFF + f0 * 128, half * DFF + f1 * 128
            t_ = P.dma("gpsimd", f"w{gi_}", win[:, :, c0:c1], w_in_v[:, :, c0:c1])
        for fc in range(f0, f1):
            tgrp[fc] = t_
    P.chan("wo")
    w_out_v = w_out.rearrange("(fc p) d -> p fc d", p=128)
    for fc in range(NFC):
        two = P.dma("gpsimd", "wo", wout[:, fc, :], w_out_v[:, fc, :])
    tw = twg

    for i in range(NX):
        P.chan(f"ld{i}")
        P.chan(f"st{i}")

    ld_tok = {}
    x_free = {}
    uT_tok = {}
    pT_free = [None]
    u_free = [None, None]
    pg_free = [[None, None], [None, None]]
    po_free = [None, None]
    gT_toks = {}
    mm2_last = [None]
    state = {"fcn": 0, "on": 0}

    def load(T):
        s = T % NX
        ld_tok[T] = P.dma("sync", f"ld{s}", xs[s][:], src[T * 128:(T + 1) * 128, :], waits=[x_free.get(s)])

    def prep(T, mm1_done):
        s = T % NX
        tt = T % 4
        ui = T % 2
        c = (T % 8) * 3
        t4 = rms_tile(P, xs[s][:], gb[:], us[ui][:], st[:, c:c + 1], st[:, c + 1:c + 2], st[:, c + 2:c + 3],
                      waits=[ld_tok[T], tw, u_free[ui]])
        tp = None
        for kc in range(8):
            tp = P.op("tensor", lambda e, kc=kc: e.transpose(pT[:, kc * 128:(kc + 1) * 128],
                                                              us[ui][:, kc * 128:(kc + 1) * 128], idb[:]),
                      waits=[t4, pT_free[0]] + tid, sig=(kc == 7))
        u_free[ui] = tp
        te = P.op("scalar", lambda e: e.copy(out=uT[:, :, tt * 128:(tt + 1) * 128],
                                             in_=pT[:].rearrange("p (k t) -> p k t", k=8)),
                  waits=[tp, mm1_done])
        pT_free[0] = te
        uT_tok[T] = te

    for T in range(min(NX, NT)):
        load(T)
    for T in range(4):
        prep(T, None)
    next_load = min(NX, NT)

    for g in range(NG):
        uw = [uT_tok[4 * g + tt] for tt in range(4)]
        last_mm1 = None
        for fc in range(NFC):
            b = fc % 2
            for half, pp in ((0, pg[b]), (1, pu[b])):
                col = half * DFF + fc * 128
                for kc in range(8):
                    t = P.op("tensor", lambda e, pp=pp, kc=kc, col=col: e.matmul(
                        pp[:], lhsT=win[:, kc, col:col + 128], rhs=uT[:, kc, :], start=(kc == 0), stop=(kc == 7)),
                        waits=uw + [tgrp[fc], pg_free[b][half], mm2_last[0]], sig=(kc == 7))
                if half == 0:
                    tg = t
                else:
                    tu = t
            ta = P.op("scalar", lambda e, b=b: e.activation(out=sg[b][:], in_=pg[b][:], func=AF.Silu),
                      waits=[tg, gT_toks.get(("sgfree", b))])
            td = P.op("vector", lambda e, b=b, fc=fc: e.tensor_tensor(out=gT[:, fc, :], in0=sg[b][:], in1=pu[b][:],
                                                                      op=ALU.mult), waits=[ta, tu])
            pg_free[b][0] = ta
            pg_free[b][1] = td
            gT_toks[("sgfree", b)] = td
            gT_toks[fc] = td
            last_mm1 = tu
        gw = [gT_toks[fc] for fc in range(NFC)]
        for tt in range(4):
            T = 4 * g + tt
            s = T % NX
            tds = []
            for dh in range(2):
                b = (2 * tt + dh) % 2
                for fc in range(NFC):
                    t = P.op("tensor", lambda e, b=b, fc=fc, tt=tt, dh=dh: e.matmul(
                        po[b][:], lhsT=gT[:, fc, tt * 128:(tt + 1) * 128], rhs=wout[:, fc, dh * 512:(dh + 1) * 512],
                        start=(fc == 0), stop=(fc == NFC - 1)), waits=gw + [two, po_free[b]], sig=(fc == NFC - 1))
                mm2_last[0] = t
                td = P.op("vector", lambda e, b=b, s=s, dh=dh: e.scalar_tensor_tensor(
                    out=xs[s][:, dh * 512:(dh + 1) * 512], in0=po[b][:], scalar=0.5,
                    in1=xs[s][:, dh * 512:(dh + 1) * 512], op0=ALU.mult, op1=ALU.add), waits=[t])
                po_free[b] = td
                tds.append(td)
            x_free[s] = P.dma("gpsimd", f"st{s}", dst[T * 128:(T + 1) * 128, :], xs[s][:], waits=tds)
            if next_load < NT and next_load % NX == s:
                load(next_load)
                next_load += 1
            Tn = 4 * (g + 1) + tt
            if Tn < NT:
                while next_load <= Tn:
                    load(next_load)
                    next_load += 1
                prep(Tn, last_mm1)
    P.emit()


HD = 64


def host_tables():
    t = np.arange(8192)
    freqs = 10000.0 ** (-np.arange(16, dtype=np.float32) / 16)
    ar = (t // 64).astype(np.float32)[:, None] * freqs[None, :]
    ac = (t % 64).astype(np.float32)[:, None] * freqs[None, :]
    cos4 = np.concatenate([np.cos(ar), np.cos(ar), np.cos(ac), np.cos(ac)], -1).astype(np.float32)
    sin4 = np.concatenate([-np.sin(ar), np.sin(ar), -np.sin(ac), np.sin(ac)], -1).astype(np.float32)
    cc = np.arange(64)
    ang = 2 * np.pi * np.outer(cc, cc) / 64
    C, S = np.cos(ang), np.sin(ang)
    bd = np.zeros((128, 2, 512), np.float32)
    for c in range(2):
        for gl in range(2):
            g = 2 * c + gl
            bd[gl * 64:(gl + 1) * 64, c, g * 64:(g + 1) * 64] = C
            bd[gl * 64:(gl + 1) * 64, c, 256 + g * 64:256 + (g + 1) * 64] = S
    return {"cos4": cos4, "sin4": sin4, "bd": bd}


def a0_phase(nc, name, src, gain, w_in, qn_g, kn_g, tabs, AB, QT, KT, V, seqs):
    P = Phase(nc, name)
    ntok = sum(seqs)
    NT = ntok // 128
    pos0 = []
    for S in seqs:
        pos0 += list(range(0, S, 128))
    wq = P.sb("wq", [128, 8, 1536], BF16)
    bd = P.sb("bd", [128, 2, 512], BF16)
    gb = P.sb("gb", [128, D], F32)
    gqk = P.sb("gqk", [128, 16, HD], F32)
    xs = [P.sb(f"x{i}", [128, D], F32) for i in range(2)]
    us = [P.sb(f"u{i}", [128, D], BF16) for i in range(2)]
    uT = [P.sb(f"uT{i}", [128, 8, 128], BF16) for i in range(2)]
    fTs = P.sb("fTs", [128, 2, 128], BF16)
    abs_ = [P.sb(f"abs{i}", [128, 512], BF16) for i in range(2)]
    sq = P.sb("sq", [128, 16, HD], F32)
    qn = P.sb("qn", [128, 16, HD], F32)
    t1 = P.sb("t1", [128, 16, HD], F32)
    t2 = P.sb("t2", [128, 16, HD], F32)
    qr = [P.sb(f"qr{i}", [128, 16, HD], BF16) for i in range(2)]
    kd = [P.sb(f"kd{i}", [128, 4, 2, HD], BF16) for i in range(2)]
    vs = [P.sb(f"vs{i}", [128, 256], BF16) for i in range(2)]
    cs = [P.sb(f"cs{i}", [128, 2, HD], F32) for i in range(2)]
    qTs = [P.sb(f"qTs{i}", [128, 6, 128], BF16) for i in range(2)]
    kTs = [P.sb(f"kTs{i}", [128, 4, 128], BF16) for i in range(2)]
    st = P.sb("st", [128, 3 * 8], F32)
    sh = P.sb("sh", [128, 3 * 16], F32)
    idf, idb, tid = load_consts(P)
    pT = P.ps("pT", [128, D], BF16)
    pF = P.ps("pF", [128, 2, 128])
    pAB = P.ps("pAB", [128, 512])
    pQ = P.ps("pQ", [128, 1536])
    pTq = P.ps("pTq", [128, 6, 128], BF16)
    pTk = P.ps("pTk", [128, 4, 128], BF16)

    P.chan("w")
    w_v = w_in.rearrange("(kc p) f -> p kc f", p=128)
    for kc in range(8):
        P.dma("gpsimd", "w", wq[:, kc, :], w_v[:, kc, :])
    P.dma("gpsimd", "w", bd[:], tabs["bd"])
    P.dma("gpsimd", "w", gb[:], gain.partition_broadcast(128))
    for h in range(16):
        P.dma("gpsimd", "w", gqk[:, h, :], (qn_g if h < 12 else kn_g).partition_broadcast(128))
    tw = P.chan_tok("w")
    for i in range(2):
        P.chan(f"ld{i}")
        P.chan(f"so{i}")

    QTv = QT.rearrange("a p t -> p a t")
    KTv = KT.rearrange("a p t -> p a t")
    free = {}

    def F(k):
        return free.get(k)

    for T in range(NT):
        i = T % 2
        c = (T % 8) * 3
        tl = P.dma("sync", f"ld{i}", xs[i][:], src[T * 128:(T + 1) * 128, :], waits=[F(("x", i))])
        P.dma("sync", f"ld{i}", cs[i][:, 0, :], tabs["cos4"][pos0[T]:pos0[T] + 128, :], waits=[F(("cs", i))])
        tl = P.dma("sync", f"ld{i}", cs[i][:, 1, :], tabs["sin4"][pos0[T]:pos0[T] + 128, :])
        t4 = rms_tile(P, xs[i][:], gb[:], us[i][:], st[:, c:c + 1], st[:, c + 1:c + 2], st[:, c + 2:c + 3],
                      waits=[tl, tw, F(("u", i))])
        free[("x", i)] = t4
        for kc in range(8):
            tp = P.op("tensor", lambda e, kc=kc, i=i: e.transpose(pT[:, kc * 128:(kc + 1) * 128],
                                                                   us[i][:, kc * 128:(kc + 1) * 128], idb[:]),
                      waits=[t4, F("pT")] + tid, sig=(kc == 7))
        free[("u", i)] = tp
        te = P.op("scalar", lambda e, i=i: e.copy(out=uT[i][:], in_=pT[:].rearrange("p (k t) -> p k t", k=8)),
                  waits=[tp, F(("uT", i))])
        free["pT"] = te
        for cch in range(2):
            for kc in range(8):
                tf = P.op("tensor", lambda e, cch=cch, kc=kc, i=i: e.matmul(
                    pF[:, cch, :], lhsT=wq[:, kc, cch * 128:(cch + 1) * 128], rhs=uT[i][:, kc, :],
                    start=(kc == 0), stop=(kc == 7)), waits=[te, tw, F("pF")], sig=(cch == 1 and kc == 7))
        tfe = P.op("vector", lambda e: e.tensor_copy(out=fTs[:], in_=pF[:]), waits=[tf, F("fTs")])
        free["pF"] = tfe
        for cch in range(2):
            tab = P.op("tensor", lambda e, cch=cch: e.matmul(pAB[:], lhsT=fTs[:, cch, :], rhs=bd[:, cch, :],
                                                             start=(cch == 0), stop=(cch == 1)),
                       waits=[tfe, F("pAB")], sig=(cch == 1))
        free["fTs"] = tab
        tabe = P.op("scalar", lambda e, i=i: e.copy(out=abs_[i][:], in_=pAB[:]), waits=[tab, F(("abs", i))])
        free["pAB"] = tabe
        P.dma("gpsimd", f"so{i}", AB[T * 128:(T + 1) * 128, :], abs_[i][:], waits=[tabe])
        for (c0, c1, o0) in ((256, 768, 0), (768, 1280, 512), (1280, 1536, 1024)):
            for kc in range(8):
                tq = P.op("tensor", lambda e, c0=c0, c1=c1, o0=o0, kc=kc, i=i: e.matmul(
                    pQ[:, o0:o0 + (c1 - c0)], lhsT=uT[i][:, kc, :], rhs=wq[:, kc, c0:c1],
                    start=(kc == 0), stop=(kc == 7)), waits=[te, tw, F("pQ")], sig=(c0 == 1280 and kc == 7))
        free[("uT", i)] = tq
        pqk = pQ[:, 0:1024].rearrange("p (h d) -> p h d", h=16)
        s1 = P.op("scalar", lambda e: e.activation(out=sq[:], in_=pqk, func=AF.Square), waits=[tq, F("sq")])
        hc = (T % 2) * 48 // 2
        ssh, sdh, rsh = sh[:, 0:16], sh[:, 16:32], sh[:, 32:48]
        s2 = P.op("vector", lambda e: e.tensor_reduce(out=ssh, in_=sq[:], axis=AX.X, op=ALU.add), waits=[s1])
        free["sq"] = s2
        s3 = P.op("scalar", lambda e: e.activation(out=sdh, in_=ssh, func=AF.Sqrt, scale=1.0 / HD, bias=EPS), waits=[s2])
        s4 = P.op("vector", lambda e: e.reciprocal(out=rsh, in_=sdh), waits=[s3])
        s5 = P.op("vector", lambda e: e.tensor_tensor(out=qn[:], in0=pqk, in1=rsh.unsqueeze(2).to_broadcast([128, 16, HD]),
                                                      op=ALU.mult), waits=[s4, F("qn")])
        tv = P.op("scalar", lambda e, i=i: e.copy(out=vs[i][:], in_=pQ[:, 1024:1280]), waits=[tq, F(("vs", i))])
        s6 = P.op("gpsimd", lambda e: e.tensor_tensor(out=qn[:], in0=qn[:], in1=gqk[:], op=ALU.mult), waits=[s5, tw])
        free["pQ"] = [s5, tv]
        P.dma("gpsimd", f"so{i}", V[T * 128:(T + 1) * 128, :], vs[i][:], waits=[tv])
        cosb = cs[i][:, 0:1, :].to_broadcast([128, 16, HD])
        r1 = P.op("vector", lambda e, cosb=cosb: e.tensor_tensor(out=t1[:], in0=qn[:], in1=cosb, op=ALU.mult),
                  waits=[s6, tl, F("t1")])
        qn5 = qn[:].rearrange("p h (a x f) -> p h a x f", a=2, x=2)
        t25 = t2[:].rearrange("p h (a x f) -> p h a x f", a=2, x=2)
        sn5 = cs[i][:, 1:2, :].rearrange("p o (a x f) -> p o a x f", a=2, x=2)
        r2 = P.op("gpsimd", lambda e, sn5=sn5: e.tensor_tensor(out=t25[:, :, :, 0, :], in0=qn5[:, :, :, 1, :],
                                                              in1=sn5[:, :, :, 0, :].to_broadcast([128, 16, 2, 16]),
                                                              op=ALU.mult), waits=[s6, tl, F("t2")])
        r3 = P.op("gpsimd", lambda e, sn5=sn5: e.tensor_tensor(out=t25[:, :, :, 1, :], in0=qn5[:, :, :, 0, :],
                                                              in1=sn5[:, :, :, 1, :].to_broadcast([128, 16, 2, 16]),
                                                              op=ALU.mult), waits=[s6, tl])
        free["qn"] = [r1, r3]
        free[("cs", i)] = [r1, r3]
        r4 = P.op("vector", lambda e, i=i: e.tensor_tensor(out=qr[i][:], in0=t1[:], in1=t2[:], op=ALU.add),
                  waits=[r1, r3, F(("qr", i))])
        free["t1"] = r4
        free["t2"] = r4
        r5 = P.op("vector", lambda e, i=i: e.tensor_copy(
            out=kd[i][:], in_=qr[i][:, 12:16, :].unsqueeze(2).to_broadcast([128, 4, 2, HD])),
            waits=[r4, F(("kd", i))])
        for a in range(6):
            tq2 = P.op("tensor", lambda e, a=a, i=i: e.transpose(
                pTq[:, a, :], qr[i][:, 2 * a:2 * a + 2, :].rearrange("p h d -> p (h d)"), idb[:]),
                waits=[r4, F("pTq")], sig=(a == 5))
        for g in range(4):
            tk2 = P.op("tensor", lambda e, g=g, i=i: e.transpose(
                pTk[:, g, :], kd[i][:, g, :, :].rearrange("p h d -> p (h d)"), idb[:]),
                waits=[r5, F("pTk")], sig=(g == 3))
        free[("qr", i)] = tk2
        free[("kd", i)] = tk2
        e1 = P.op("scalar", lambda e, i=i: e.copy(out=qTs[i][:], in_=pTq[:]), waits=[tq2, F(("qTs", i))])
        e2 = P.op("vector", lambda e, i=i: e.tensor_copy(out=kTs[i][:], in_=pTk[:]), waits=[tk2, F(("kTs", i))])
        free["pTq"] = e1
        free["pTk"] = e2
        P.dma("gpsimd", f"so{i}", QTv[:, :, T * 128:(T + 1) * 128], qTs[i][:], waits=[e1])
        tso = P.dma("gpsimd", f"so{i}", KTv[:, :, T * 128:(T + 1) * 128], kTs[i][:], waits=[e2])
        for k in ("abs", "vs", "qTs", "kTs"):
            free[(k, i)] = tso
    P.emit()


def a0_phase2(nc, name, src, gain, w_in, qn_g, kn_g, tabs, AB, QT, KT, V, seqs):
    P = Phase(nc, name)
    ntok = sum(seqs)
    NT = ntok // 128
    pos0 = []
    for S in seqs:
        pos0 += list(range(0, S, 128))
    wq = P.sb("wq", [128, 8, 1536], BF16)
    bd = P.sb("bd", [128, 2, 512], BF16)
    gb = P.sb("gb", [128, D], F32)
    gqk = P.sb("gqk", [128, 16, HD], F32)
    xs = [P.sb(f"x{i}", [128, D], F32) for i in range(2)]
    us = [P.sb(f"u{i}", [128, D], BF16) for i in range(2)]
    uT = [P.sb(f"uT{i}", [128, 8, 128], BF16) for i in range(2)]
    fTs = P.sb("fTs", [128, 2, 128], BF16)
    abs_ = [P.sb(f"abs{i}", [128, 512], BF16) for i in range(2)]
    sq = P.sb("sq", [128, 16, HD], F32)
    qn = P.sb("qn", [128, 16, HD], F32)
    t1 = P.sb("t1", [128, 16, HD], F32)
    t2 = P.sb("t2", [128, 16, HD], F32)
    qr = [P.sb(f"qr{i}", [128, 16, HD], BF16) for i in range(2)]
    kd = [P.sb(f"kd{i}", [128, 4, 2, HD], BF16) for i in range(2)]
    vs = [P.sb(f"vs{i}", [128, 256], BF16) for i in range(2)]
    cs = [P.sb(f"cs{i}", [128, 2, HD], F32) for i in range(2)]
    qTs = [P.sb(f"qTs{i}", [128, 6, 128], BF16) for i in range(2)]
    kTs = [P.sb(f"kTs{i}", [128, 4, 128], BF16) for i in range(2)]
    st = P.sb("st", [128, 3 * 8], F32)
    sh = P.sb("sh", [128, 3 * 16], F32)
    idf, idb, tid = load_consts(P)
    pT = P.ps("pT", [128, D], BF16)
    pF = P.ps("pF", [128, 2, 128])
    pAB = P.ps("pAB", [128, 512])
    pQ = P.ps("pQ", [128, 1536])
    pTq = P.ps("pTq", [128, 6, 128], BF16)
    pTk = P.ps("pTk", [128, 4, 128], BF16)

    P.chan("w")
    w_v = w_in.rearrange("(kc p) f -> p kc f", p=128)
    for kc in range(8):
        P.dma("gpsimd", "w", wq[:, kc, :], w_v[:, kc, :])
    P.dma("gpsimd", "w", bd[:], tabs["bd"])
    P.dma("gpsimd", "w", gb[:], gain.partition_broadcast(128))
    for h in range(16):
        P.dma("gpsimd", "w", gqk[:, h, :], (qn_g if h < 12 else kn_g).partition_broadcast(128))
    tw = P.chan_tok("w")
    for i in range(2):
        P.chan(f"ld{i}")
        P.chan(f"so{i}")
        P.chan(f"sa{i}")

    QTv = QT.rearrange("a p t -> p a t")
    KTv = KT.rearrange("a p t -> p a t")
    free = {}

    def F(k):
        return free.get(k)

    A = {}

    def stage_a(T):
        i = T % 2
        c = (T % 8) * 3
        tl = P.dma("sync", f"ld{i}", xs[i][:], src[T * 128:(T + 1) * 128, :], waits=[F(("x", i))])
        P.dma("sync", f"ld{i}", cs[i][:, 0, :], tabs["cos4"][pos0[T]:pos0[T] + 128, :], waits=[F(("cs", i))])
        tl = P.dma("sync", f"ld{i}", cs[i][:, 1, :], tabs["sin4"][pos0[T]:pos0[T] + 128, :])
        t4 = rms_tile(P, xs[i][:], gb[:], us[i][:], st[:, c:c + 1], st[:, c + 1:c + 2], st[:, c + 2:c + 3],
                      waits=[tl, tw, F(("u", i))])
        free[("x", i)] = t4
        for kc in range(8):
            tp = P.op("tensor", lambda e, kc=kc, i=i: e.transpose(pT[:, kc * 128:(kc + 1) * 128],
                                                                   us[i][:, kc * 128:(kc + 1) * 128], idb[:]),
                      waits=[t4, F("pT")] + tid, sig=(kc == 7))
        free[("u", i)] = tp
        te = P.op("scalar", lambda e, i=i: e.copy(out=uT[i][:], in_=pT[:].rearrange("p (k t) -> p k t", k=8)),
                  waits=[tp, F(("uT", i))])
        free["pT"] = te
        for cch in range(2):
            for kc in range(8):
                tf = P.op("tensor", lambda e, cch=cch, kc=kc, i=i: e.matmul(
                    pF[:, cch, :], lhsT=wq[:, kc, cch * 128:(cch + 1) * 128], rhs=uT[i][:, kc, :],
                    start=(kc == 0), stop=(kc == 7)), waits=[te, tw, F("pF")], sig=(cch == 1 and kc == 7))
        tfe = P.op("vector", lambda e: e.tensor_copy(out=fTs[:], in_=pF[:]), waits=[tf, F("fTs")])
        free["pF"] = tfe
        for cch in range(2):
            tab = P.op("tensor", lambda e, cch=cch: e.matmul(pAB[:], lhsT=fTs[:, cch, :], rhs=bd[:, cch, :],
                                                             start=(cch == 0), stop=(cch == 1)),
                       waits=[tfe, F("pAB")], sig=(cch == 1))
        free["fTs"] = tab
        tabe = P.op("scalar", lambda e, i=i: e.copy(out=abs_[i][:], in_=pAB[:]), waits=[tab, F(("abs", i))])
        free["pAB"] = tabe
        free[("abs", i)] = P.dma("gpsimd", f"sa{i}", AB[T * 128:(T + 1) * 128, :], abs_[i][:], waits=[tabe])
        A[T] = (te, tl)

    def stage_a2(T):
        i = T % 2
        te, tl = A.pop(T)
        for (c0, c1, o0) in ((256, 768, 0), (768, 1280, 512), (1280, 1536, 1024)):
            for kc in range(8):
                tq = P.op("tensor", lambda e, c0=c0, c1=c1, o0=o0, kc=kc, i=i: e.matmul(
                    pQ[:, o0:o0 + (c1 - c0)], lhsT=uT[i][:, kc, :], rhs=wq[:, kc, c0:c1],
                    start=(kc == 0), stop=(kc == 7)), waits=[te, tw, F("pQ")], sig=(c0 == 1280 and kc == 7))
        free[("uT", i)] = tq
        A[T] = (tq, tl)

    def stage_b(T):
        i = T % 2
        tq, tl = A.pop(T)
        pqk = pQ[:, 0:1024].rearrange("p (h d) -> p h d", h=16)
        s1 = P.op("scalar", lambda e: e.activation(out=sq[:], in_=pqk, func=AF.Square), waits=[tq, F("sq")])
        hc = (T % 2) * 48 // 2
        ssh, sdh, rsh = sh[:, 0:16], sh[:, 16:32], sh[:, 32:48]
        s2 = P.op("vector", lambda e: e.tensor_reduce(out=ssh, in_=sq[:], axis=AX.X, op=ALU.add), waits=[s1])
        free["sq"] = s2
        s3 = P.op("scalar", lambda e: e.activation(out=sdh, in_=ssh, func=AF.Sqrt, scale=1.0 / HD, bias=EPS), waits=[s2])
        s4 = P.op("vector", lambda e: e.reciprocal(out=rsh, in_=sdh), waits=[s3])
        s5 = P.op("vector", lambda e: e.tensor_tensor(out=qn[:], in0=pqk, in1=rsh.unsqueeze(2).to_broadcast([128, 16, HD]),
                                                      op=ALU.mult), waits=[s4, F("qn")])
        tv = P.op("scalar", lambda e, i=i: e.copy(out=vs[i][:], in_=pQ[:, 1024:1280]), waits=[tq, F(("vs", i))])
        free["pQ"] = [s5, tv]
        P.dma("gpsimd", f"so{i}", V[T * 128:(T + 1) * 128, :], vs[i][:], waits=[tv])
        A[T] = (tq, tl, s5, tv)

    def stage_b2(T):
        i = T % 2
        tq, tl, s5, tv = A.pop(T)
        s6 = P.op("gpsimd", lambda e: e.tensor_tensor(out=qn[:], in0=qn[:], in1=gqk[:], op=ALU.mult), waits=[s5, tw])
        cosb = cs[i][:, 0:1, :].to_broadcast([128, 16, HD])
        r1 = P.op("vector", lambda e, cosb=cosb: e.tensor_tensor(out=t1[:], in0=qn[:], in1=cosb, op=ALU.mult),
                  waits=[s6, tl, F("t1")])
        qn5 = qn[:].rearrange("p h (a x f) -> p h a x f", a=2, x=2)
        t25 = t2[:].rearrange("p h (a x f) -> p h a x f", a=2, x=2)
        sn5 = cs[i][:, 1:2, :].rearrange("p o (a x f) -> p o a x f", a=2, x=2)
        r2 = P.op("gpsimd", lambda e, sn5=sn5: e.tensor_tensor(out=t25[:, :, :, 0, :], in0=qn5[:, :, :, 1, :],
                                                              in1=sn5[:, :, :, 0, :].to_broadcast([128, 16, 2, 16]),
                                                              op=ALU.mult), waits=[s6, tl, F("t2")])
        r3 = P.op("gpsimd", lambda e, sn5=sn5: e.tensor_tensor(out=t25[:, :, :, 1, :], in0=qn5[:, :, :, 0, :],
                                                              in1=sn5[:, :, :, 1, :].to_broadcast([128, 16, 2, 16]),
                                                              op=ALU.mult), waits=[s6, tl])
        free["qn"] = [r1, r3]
        free[("cs", i)] = [r1, r3]
        r4 = P.op("vector", lambda e, i=i: e.tensor_tensor(out=qr[i][:], in0=t1[:], in1=t2[:], op=ALU.add),
                  waits=[r1, r3, F(("qr", i))])
        free["t1"] = r4
        free["t2"] = r4
        r5 = P.op("vector", lambda e, i=i: e.tensor_copy(
            out=kd[i][:], in_=qr[i][:, 12:16, :].unsqueeze(2).to_broadcast([128, 4, 2, HD])),
            waits=[r4, F(("kd", i))])
        for a in range(6):
            tq2 = P.op("tensor", lambda e, a=a, i=i: e.transpose(
                pTq[:, a, :], qr[i][:, 2 * a:2 * a + 2, :].rearrange("p h d -> p (h d)"), idb[:]),
                waits=[r4, F("pTq")], sig=(a == 5))
        for g in range(4):
            tk2 = P.op("tensor", lambda e, g=g, i=i: e.transpose(
                pTk[:, g, :], kd[i][:, g, :, :].rearrange("p h d -> p (h d)"), idb[:]),
                waits=[r5, F("pTk")], sig=(g == 3))
        free[("qr", i)] = tk2
        free[("kd", i)] = tk2
        e1 = P.op("scalar", lambda e, i=i: e.copy(out=qTs[i][:], in_=pTq[:]), waits=[tq2, F(("qTs", i))])
        e2 = P.op("vector", lambda e, i=i: e.tensor_copy(out=kTs[i][:], in_=pTk[:]), waits=[tk2, F(("kTs", i))])
        free["pTq"] = e1
        free["pTk"] = e2
        P.dma("gpsimd", f"so{i}", QTv[:, :, T * 128:(T + 1) * 128], qTs[i][:], waits=[e1])
        tso = P.dma("gpsimd", f"so{i}", KTv[:, :, T * 128:(T + 1) * 128], kTs[i][:], waits=[e2])
        for k in ("vs", "qTs", "kTs"):
            free[(k, i)] = tso

    stage_a(0)
    stage_a2(0)
    for T in range(NT):
        if T + 1 < NT:
            stage_a(T + 1)
        stage_b(T)
        if T + 1 < NT:
            stage_a2(T + 1)
        stage_b2(T)
    P.emit()


def att_phase(nc, name, QT, KT, V, mT, off, S):
    P = Phase(nc, name)
    NKC = S // 128
    NQG = S // 512
    scale = HD ** -0.5
    KTs = [P.sb(f"KT{i}", [128, S], BF16) for i in range(2)]
    Va = [P.sb(f"Va{i}", [128, NKC, HD + 1], BF16) for i in range(2)]
    QTs = [P.sb(f"QT{i}", [128, 512], BF16) for i in range(2)]
    PT = [P.sb(f"PT{i}", [128, 512], BF16) for i in range(3)]
    onesf = P.sb("onesf", [128, 64], F32)
    rden = P.sb("rden", [128, 512], F32)
    oc = P.sb("oc", [64, 512], F32)
    oT = [P.sb(f"oT{i}", [64, 512], BF16) for i in range(2)]
    pS = [P.ps(f"pS{i}", [128, 512]) for i in range(3)]
    pO = [P.ps(f"pO{i}", [128, 512]) for i in range(2)]
    pB = P.ps("pB", [64, 512])
    free = {}
    F = free.get
    t_ones = P.op("vector", lambda e: e.memset(onesf[:], 1.0))
    tm = [P.op("gpsimd", lambda e, i=i: e.memset(Va[i][:, :, HD:HD + 1], 1.0)) for i in range(2)]
    for i in range(2):
        P.chan(f"kv{i}")
        P.chan(f"q{i}")
        P.chan(f"o{i}")
    it = 0
    sidx = 0
    for g in range(4):
        gi = g % 2
        for c0 in range(0, S, 2048):
            c1 = min(S, c0 + 2048)
            P.dma("sync", f"kv{gi}", KTs[gi][:, c0:c1], KT[g, :, off + c0:off + c1], waits=[F(("kv", gi))])
        Vv = V[off:off + S, g * HD:(g + 1) * HD].rearrange("(kc p) d -> p kc d", p=128)
        for c0 in range(0, NKC, 16):
            c1 = min(NKC, c0 + 16)
            tkv = P.dma("sync", f"kv{gi}", Va[gi][:, c0:c1, 0:HD], Vv[:, c0:c1, :], waits=[tm[gi], F(("kv", gi))])
        for j in range(3):
            h = 3 * g + j
            a, half = h // 2, h % 2
            pb = 64 * half
            for qg in range(NQG):
                qi = it % 2
                tq = P.dma("sync", f"q{qi}", QTs[qi][:], QT[a, :, off + qg * 512: off + (qg + 1) * 512],
                           waits=[F(("q", qi))])
                po = pO[it % 2]
                ts_tok = {}
                te_tok = {}

                def st_mm(kc):
                    nonlocal sidx
                    b = sidx % 3
                    ts_tok[kc] = (P.op("tensor", lambda e, b=b, kc=kc, gi=gi, qi=qi, pb=pb: e.matmul(
                        pS[b][:], lhsT=KTs[gi][pb:pb + 64, kc * 128:(kc + 1) * 128], rhs=QTs[qi][pb:pb + 64, :],
                        start=True, stop=True), waits=[tkv, tq, F(("pS", b))]), b)
                    sidx += 1

                def ex(kc):
                    tok, b = ts_tok[kc]
                    t = P.op("scalar", lambda e, b=b, kc=kc: e.activation(out=PT[kc % 3][:], in_=pS[b][:], func=AF.Exp,
                                                                          scale=scale), waits=[tok])
                    free[("pS", b)] = t
                    te_tok[kc] = t

                def pv(kc):
                    return P.op("tensor", lambda e, kc=kc, po=po, gi=gi: e.matmul(
                        po[0:HD + 1, :], lhsT=Va[gi][:, kc, :], rhs=PT[kc % 3][:], start=(kc == 0), stop=(kc == NKC - 1)),
                        waits=[te_tok[kc], F(("pO", it % 2))], sig=(kc == NKC - 1))

                st_mm(0)
                if NKC > 1:
                    st_mm(1)
                tl = None
                for kc in range(NKC):
                    ex(kc)
                    tl = pv(kc)
                    if kc + 2 < NKC:
                        st_mm(kc + 2)
                free[("q", qi)] = tl
                if j == 2 and qg == NQG - 1:
                    free[("kv", gi)] = tl
                e1 = P.op("vector", lambda e, po=po: e.reciprocal(out=rden[64:65, :], in_=po[64:65, :]),
                          waits=[tl, F("rden")])
                e2 = P.op("tensor", lambda e: e.matmul(pB[:], lhsT=onesf[64:65, 0:64], rhs=rden[64:65, :],
                                                       start=True, stop=True), waits=[e1, t_ones, F("pB")])
                free["rden"] = e2
                e3 = P.op("scalar", lambda e, po=po: e.copy(out=oc[:], in_=po[0:64, :]), waits=[tl, F("oc")])
                oi = it % 2
                e4 = P.op("vector", lambda e, oi=oi: e.tensor_tensor(out=oT[oi][:], in0=oc[:], in1=pB[:], op=ALU.mult),
                          waits=[e2, e3, F(("oT", oi))])
                free["pB"] = e4
                free["oc"] = e4
                free[("pO", it % 2)] = [e1, e3]
                free[("oT", oi)] = P.dma("gpsimd", f"o{oi}", mT[256 + 64 * h:256 + 64 * (h + 1),
                                                                off + qg * 512: off + (qg + 1) * 512], oT[oi][:], waits=[e4])
                it += 1
    P.emit()


def fft_tables(S):
    N2 = S // 128
    n = np.arange(128)
    a1 = 2 * np.pi * np.outer(n, n) / 128
    k1 = np.arange(128)[:, None]; n2 = np.arange(N2)[None, :]
    at = 2 * np.pi * k1 * n2 / S
    m = np.arange(N2)
    a2 = 2 * np.pi * np.outer(m, m) / N2
    w1 = np.stack([np.cos(a1), np.sin(a1), -np.sin(a1)], 1).astype(np.float32)
    tw = np.stack([np.cos(at), np.sin(at)], 1).astype(np.float32)
    w2 = np.stack([np.cos(a2), -np.sin(a2)], 1).astype(np.float32)
    return {f"w1": w1, f"tw{S}": tw, f"w2_{S}": w2}


def fft_phase(nc, name, AB, YS, FM, tabs, off, S):
    P = Phase(nc, name)
    N2 = S // 128
    KB = 16
    NB = 128 // KB
    ABs = P.sb("ABs", [128, N2, 512], BF16)
    Ypr = P.sb("Ypr", [128, N2, 256], BF16)
    Ypq = P.sb("Ypq", [128, N2, 256], BF16)
    w1 = P.sb("w1", [128, 3, 128], BF16)
    tw = P.sb("tw", [128, 2, N2], F32)
    w2 = P.sb("w2", [N2, 2, N2], BF16)
    tmp = [P.sb(f"tmp{i}", [128, 2, 256], F32) for i in range(2)]
    Yb = [P.sb(f"Yb{i}", [N2, 2, KB, 256], BF16) for i in range(2)]
    fo = [P.sb(f"fo{i}", [N2, KB, 256], BF16) for i in range(2)]
    pY = [P.ps(f"pY{i}", [128, 2, 256]) for i in range(2)]
    pX = [P.ps(f"pX{i}", [N2, 512]) for i in range(2)]
    free = {}
    F = free.get
    P.chan("w")
    P.dma("gpsimd", "w", w1[:], tabs["w1"])
    P.dma("gpsimd", "w", tw[:], tabs[f"tw{S}"])
    P.dma("gpsimd", "w", w2[:], tabs[f"w2_{S}"])
    P.chan("ab")
    ABv = AB[off:off + S, :].rearrange("(p n) c -> p n c", p=128)
    CH = min(8, N2)
    for c0 in range(0, N2, CH):
        t_ab = P.dma("sync", "ab", ABs[:, c0:c0 + CH, :], ABv[:, c0:c0 + CH, :])
    t_w = [P.chan_tok("w"), t_ab]
    for n2 in range(N2):
        b = n2 % 2
        A = ABs[:, n2, 0:256]
        B = ABs[:, n2, 256:512]
        P.op("tensor", lambda e, b=b, A=A: e.matmul(pY[b][:, 0, :], lhsT=w1[:, 0, :], rhs=A, start=True, stop=False),
             waits=[t_w, F(("pY", b))], sig=False)
        P.op("tensor", lambda e, b=b, B=B: e.matmul(pY[b][:, 0, :], lhsT=w1[:, 2, :], rhs=B, start=False, stop=True), sig=False)
        P.op("tensor", lambda e, b=b, A=A: e.matmul(pY[b][:, 1, :], lhsT=w1[:, 1, :], rhs=A, start=True, stop=False), sig=False)
        t1 = P.op("tensor", lambda e, b=b, B=B: e.matmul(pY[b][:, 1, :], lhsT=w1[:, 0, :], rhs=B, start=False, stop=True))
        tc = tw[:, 0, n2:n2 + 1]
        ts = tw[:, 1, n2:n2 + 1]
        a1 = P.op("vector", lambda e, b=b, ts=ts: e.tensor_scalar(out=tmp[b][:, 0, :], in0=pY[b][:, 1, :], scalar1=ts,
                                                                   scalar2=None, op0=ALU.mult), waits=[t1, F(("tmp", b))])
        a2 = P.op("vector", lambda e, b=b, tc=tc: e.tensor_scalar(out=tmp[b][:, 1, :], in0=pY[b][:, 1, :], scalar1=tc,
                                                                   scalar2=None, op0=ALU.mult), waits=[t1, F(("tmp", b))])
        a3 = P.op("vector", lambda e, b=b, tc=tc, n2=n2: e.scalar_tensor_tensor(
            out=Ypr[:, n2, :], in0=pY[b][:, 0, :], scalar=tc, in1=tmp[b][:, 0, :], op0=ALU.mult, op1=ALU.subtract),
            waits=[a1])
        a4 = P.op("vector", lambda e, b=b, ts=ts, n2=n2: e.scalar_tensor_tensor(
            out=Ypq[:, n2, :], in0=pY[b][:, 0, :], scalar=ts, in1=tmp[b][:, 1, :], op0=ALU.mult, op1=ALU.add),
            waits=[a2])
        free[("pY", b)] = [a2, a4]
        free[("tmp", b)] = a4
    P.chan("ys")
    for c0 in range(0, N2, CH):
        P.dma("gpsimd", "ys", YS[0, :, c0:c0 + CH, :], Ypr[:, c0:c0 + CH, :], waits=[a4])
        tys = P.dma("gpsimd", "ys", YS[1, :, c0:c0 + CH, :], Ypq[:, c0:c0 + CH, :], waits=[a4])
    YSv = YS.rearrange("r k n c -> n r k c")
    norm = float(1.0 / np.sqrt(64.0 * S))
    for i in range(2):
        P.chan(f"yb{i}")
        P.chan(f"fo{i}")
    FMv = FM[off:off + S, :].rearrange("(k2 k1) c -> k2 k1 c", k1=128)
    gi = 0
    for blk in range(NB):
        bi = blk % 2
        for r in range(2):
            tyb = P.dma("sync", f"yb{bi}", Yb[bi][:, r, :, :], YSv[:, r, blk * KB:(blk + 1) * KB, :],
                        waits=[tys, F(("Yb", bi))])
        for grp in range(KB // 2):
            pb = gi % 2
            P.op("tensor", lambda e, pb=pb, bi=bi, grp=grp: e.matmul(
                pX[pb][:], lhsT=w2[:, 0, :], rhs=Yb[bi][:, 0, 2 * grp:2 * grp + 2, :].rearrange("n k c -> n (k c)"),
                start=True, stop=False), waits=[tyb, t_w, F(("pX", pb))], sig=False)
            t2 = P.op("tensor", lambda e, pb=pb, bi=bi, grp=grp: e.matmul(
                pX[pb][:], lhsT=w2[:, 1, :], rhs=Yb[bi][:, 1, 2 * grp:2 * grp + 2, :].rearrange("n k c -> n (k c)"),
                start=False, stop=True))
            eng = "scalar" if gi % 2 == 0 else "vector"
            if eng == "scalar":
                t3 = P.op("scalar", lambda e, pb=pb, bi=bi, grp=grp: e.activation(
                    out=fo[bi][:, 2 * grp:2 * grp + 2, :].rearrange("n k c -> n (k c)"), in_=pX[pb][:], func=AF.Copy,
                    scale=norm), waits=[t2, F(("fo", bi))])
            else:
                t3 = P.op("vector", lambda e, pb=pb, bi=bi, grp=grp: e.tensor_scalar(
                    out=fo[bi][:, 2 * grp:2 * grp + 2, :].rearrange("n k c -> n (k c)"), in0=pX[pb][:], scalar1=norm,
                    scalar2=None, op0=ALU.mult), waits=[t2, F(("fo", bi))])
            free[("pX", pb)] = t3
            gi += 1
            last2 = [t3] + ([last2[0]] if grp else [])
        free[("Yb", bi)] = t2
        free[("fo", bi)] = P.dma("gpsimd", f"fo{bi}", FMv[:, blk * KB:(blk + 1) * KB, :], fo[bi][:], waits=last2)
    P.emit()


def o0_phase(nc, name, src, dst, mT, FM, w_out, ntok):
    P = Phase(nc, name)
    NT = ntok // 128
    wo = P.sb("wo", [128, 8, D], BF16)
    xs = [P.sb(f"x{i}", [128, D], F32) for i in range(2)]
    ms = [P.sb(f"m{i}", [128, 6, 128], BF16) for i in range(2)]
    fm = [P.sb(f"fm{i}", [128, 256], BF16) for i in range(2)]
    fT = [P.sb(f"fT{i}", [128, 2, 128], BF16) for i in range(2)]
    idf, idb, tid = load_consts(P)
    pT = P.ps("pT", [128, 2, 128], BF16)
    pYo = [P.ps(f"pYo{i}", [128, 512]) for i in range(2)]
    P.chan("w")
    wv = w_out.rearrange("(kc p) d -> p kc d", p=128)
    for kc in range(8):
        P.dma("gpsimd", "w", wo[:, kc, :], wv[:, kc, :])
    tw = P.chan_tok("w")
    for i in range(2):
        P.chan(f"ld{i}")
        P.chan(f"st{i}")
    mTv = mT[256:1024, :].rearrange("(c p) t -> p c t", p=128)
    free = {}
    F = free.get
    k = 0
    for T in range(NT):
        i = T % 2
        P.dma("sync", f"ld{i}", xs[i][:], src[T * 128:(T + 1) * 128, :], waits=[F(("st", i))])
        P.dma("sync", f"ld{i}", ms[i][:], mTv[:, :, T * 128:(T + 1) * 128], waits=[F(("mm", i))])
        tl = P.dma("sync", f"ld{i}", fm[i][:], FM[T * 128:(T + 1) * 128, :], waits=[F(("tp", i))])
        for c in range(2):
            tp = P.op("tensor", lambda e, c=c, i=i: e.transpose(pT[:, c, :], fm[i][:, c * 128:(c + 1) * 128], idb[:]),
                      waits=[tl, F("pT")] + tid, sig=(c == 1))
        free[("tp", i)] = tp
        te = P.op("scalar", lambda e, i=i: e.copy(out=fT[i][:], in_=pT[:]), waits=[tp, F(("mm", i))])
        free["pT"] = te
        tds = []
        for dh in range(2):
            pb = k % 2
            k += 1
            for kc in range(8):
                lhs = (lambda i=i, kc=kc: fT[i][:, kc, :]) if kc < 2 else (lambda i=i, kc=kc: ms[i][:, kc - 2, :])
                tm = P.op("tensor", lambda e, lhs=lhs, pb=pb, kc=kc, dh=dh: e.matmul(
                    pYo[pb][:], lhsT=lhs(), rhs=wo[:, kc, dh * 512:(dh + 1) * 512], start=(kc == 0), stop=(kc == 7)),
                    waits=[te, tl, tw, F(("pYo", pb))], sig=(kc == 7))
            td = P.op("vector", lambda e, pb=pb, i=i, dh=dh: e.tensor_tensor(
                out=xs[i][:, dh * 512:(dh + 1) * 512], in0=xs[i][:, dh * 512:(dh + 1) * 512], in1=pYo[pb][:], op=ALU.add),
                waits=[tm, tl])
            free[("pYo", pb)] = td
            tds.append(td)
        free[("mm", i)] = tm
        free[("st", i)] = P.dma("gpsimd", f"st{i}", dst[T * 128:(T + 1) * 128, :], xs[i][:], waits=tds)
    P.emit()


def att_phase2(nc, name, QT, KT, V, mT, off, S):
    P = Phase(nc, name)
    NKC = S // 128
    NQG = S // 512
    scale = HD ** -0.5
    KTp = [P.sb(f"KTp{i}", [128, S], BF16) for i in range(2)]
    Va = [P.sb(f"Va{g}", [128, NKC, HD + 1], BF16) for g in range(4)]
    QTs = [P.sb(f"QT{i}", [128, 512], BF16) for i in range(2)]
    PT = [P.sb(f"PT{i}", [128, 2, 512], BF16) for i in range(3)]
    onesf = P.sb("onesf", [128, 64], F32)
    rden = P.sb("rden", [128, 2, 512], F32)
    oc = [P.sb(f"oc{i}", [HD + 1, 512], F32) for i in range(2)]
    oT = [P.sb(f"oT{i}", [64, 512], BF16) for i in range(2)]
    pS = [P.ps(f"pS{i}", [128, 2, 512]) for i in range(2)]
    pO = P.ps("pO", [128, 2, 512])
    pB = P.ps("pB", [64, 512])
    free = {}
    F = free.get
    t_ones = P.op("vector", lambda e: e.memset(onesf[:], 1.0))
    P.chan("v")
    for g in range(4):
        tm = P.op("gpsimd", lambda e, g=g: e.memset(Va[g][:, :, HD:HD + 1], 1.0))
        Vv = V[off:off + S, g * HD:(g + 1) * HD].rearrange("(kc p) d -> p kc d", p=128)
        for c0 in range(0, NKC, 16):
            c1 = min(NKC, c0 + 16)
            tv = P.dma("sync", "v", Va[g][:, c0:c1, 0:HD], Vv[:, c0:c1, :], waits=[tm])
    for i in range(2):
        P.chan(f"k{i}")
        P.chan(f"q{i}")
        P.chan(f"o{i}")
    it = 0
    sidx = 0
    pending = [None]
    oidx = [0]

    def make_finish(a, qg, e1s, e3s):
        def finish():
            for hh in range(2):
                h = 2 * a + hh
                e2 = P.op("tensor", lambda e, hh=hh: e.matmul(pB[:], lhsT=onesf[64:65, 0:64], rhs=rden[64:65, hh, :],
                                                              start=True, stop=True), waits=[e1s[hh], t_ones, F("pB")])
                oi = oidx[0] % 2
                oidx[0] += 1
                e4 = P.op("vector", lambda e, oi=oi, hh=hh: e.tensor_tensor(out=oT[oi][:], in0=oc[hh][0:64, :], in1=pB[:],
                                                                           op=ALU.mult), waits=[e2, e3s[hh], F(("oT", oi))])
                free["pB"] = e4
                free[("oc", hh)] = e4
                free[("rden", hh)] = e2
                free[("oT", oi)] = P.dma("gpsimd", f"o{oi}", mT[256 + 64 * h:256 + 64 * (h + 1),
                                                                off + qg * 512: off + (qg + 1) * 512], oT[oi][:], waits=[e4])
        return finish

    for a in range(6):
        ki = a % 2
        g0, g1 = (2 * a) // 3, (2 * a + 1) // 3
        for c0 in range(0, S, 2048):
            c1 = min(S, c0 + 2048)
            P.dma("sync", f"k{ki}", KTp[ki][0:64, c0:c1], KT[g0, 0:64, off + c0:off + c1], waits=[F(("k", ki))])
            tk = P.dma("sync", f"k{ki}", KTp[ki][64:128, c0:c1], KT[g1, 64:128, off + c0:off + c1], waits=[F(("k", ki))])
        for qg in range(NQG):
            qi = it % 2
            tq = P.dma("sync", f"q{qi}", QTs[qi][:], QT[a, :, off + qg * 512: off + (qg + 1) * 512], waits=[F(("q", qi))])
            ts_tok = {}
            te_tok = {}

            def st_mm(kc, ki=ki, qi=qi, tq=tq):
                nonlocal sidx
                b = sidx % 2
                P.op("tensor", lambda e, b=b, kc=kc: e.matmul(
                    pS[b][:, 0, :], lhsT=KTp[ki][0:64, kc * 128:(kc + 1) * 128], rhs=QTs[qi][0:64, :], start=True, stop=True),
                    waits=[tk, tq, F(("pS", b))], sig=False)
                ts_tok[kc] = (P.op("tensor", lambda e, b=b, kc=kc: e.matmul(
                    pS[b][:, 1, :], lhsT=KTp[ki][64:128, kc * 128:(kc + 1) * 128], rhs=QTs[qi][64:128, :],
                    start=True, stop=True)), b)
                sidx += 1

            def ex(kc):
                tok, b = ts_tok[kc]
                t = P.op("scalar", lambda e, b=b, kc=kc: e.activation(out=PT[kc % 3][:], in_=pS[b][:], func=AF.Exp,
                                                                      scale=scale), waits=[tok])
                free[("pS", b)] = t
                te_tok[kc] = t

            def pv(kc, g0=g0, g1=g1):
                P.op("tensor", lambda e, kc=kc: e.matmul(
                    pO[0:HD + 1, 0, :], lhsT=Va[g0][:, kc, :], rhs=PT[kc % 3][:, 0, :], start=(kc == 0), stop=(kc == NKC - 1)),
                    waits=[te_tok[kc], tv, F("pO")], sig=False)
                return P.op("tensor", lambda e, kc=kc: e.matmul(
                    pO[0:HD + 1, 1, :], lhsT=Va[g1][:, kc, :], rhs=PT[kc % 3][:, 1, :], start=(kc == 0), stop=(kc == NKC - 1)),
                    sig=(kc == NKC - 1))

            st_mm(0)
            if NKC > 1:
                st_mm(1)
            tl = None
            for kc in range(NKC):
                ex(kc)
                tl = pv(kc)
                if kc + 2 < NKC:
                    st_mm(kc + 2)
                if kc == min(3, NKC - 1) and pending[0] is not None:
                    pending[0]()
                    pending[0] = None
            free[("q", qi)] = tl
            if qg == NQG - 1:
                free[("k", ki)] = tl
            e1s, e3s = [], []
            for hh in range(2):
                e3 = P.op("scalar", lambda e, hh=hh: e.copy(out=oc[hh][:], in_=pO[0:HD + 1, hh, :]),
                          waits=[tl, F(("oc", hh))])
                e1 = P.op("vector", lambda e, hh=hh: e.reciprocal(out=rden[64:65, hh, :], in_=oc[hh][64:65, :]),
                          waits=[e3, F(("rden", hh))])
                e1s.append(e1)
                e3s.append(e3)
            free["pO"] = e3s
            pending[0] = make_finish(a, qg, e1s, e3s)
            it += 1
    if pending[0] is not None:
        pending[0]()
    P.emit()


MH = 4
MD = 256
LN16 = math.log(16.0)


def mlstm_tables():
    t = np.arange(128)
    uf = (t[:, None] <= t[None, :]).astype(np.float32)
    ub = (t[:, None] >= t[None, :]).astype(np.float32)
    return {"um": np.stack([uf, ub, np.ones((128, 128), np.float32)], 1)}


def a1_phase(nc, name, src, gain, w_in, gate_bias, ZT, V1, O1, G, ntok):
    P = Phase(nc, name)
    NT = ntok // 128
    wc = P.sb("wc", [128, 8, 4112], BF16)
    gb = P.sb("gb", [128, D], F32)
    bias = P.sb("bias", [128, 16], F32)
    xs = [P.sb(f"x{i}", [128, D], F32) for i in range(2)]
    us = [P.sb(f"u{i}", [128, D], BF16) for i in range(2)]
    uT = [P.sb(f"uT{i}", [128, 8, 128], BF16) for i in range(2)]
    zs = [P.sb(f"zs{i}", [128, 16, 128], F32) for i in range(2)]
    vs = [P.sb(f"vs{i}", [128, D], BF16) for i in range(2)]
    os_ = [P.sb(f"os{i}", [128, D], BF16) for i in range(2)]
    gs = [P.sb(f"gs{i}", [128, 16], F32) for i in range(2)]
    st = P.sb("st", [128, 3 * 8], F32)
    idf, idb, tid = load_consts(P)
    pT = P.ps("pT", [128, D], BF16)
    pZ = P.ps("pZ", [128, 8, 128])
    pVO = P.ps("pVO", [128, 2048])
    pG = P.ps("pG", [128, 16])
    P.chan("w")
    w_v = w_in.rearrange("(kc p) f -> p kc f", p=128)
    for kc in range(8):
        P.dma("gpsimd", "w", wc[:, kc, :], w_v[:, kc, :])
    P.dma("gpsimd", "w", gb[:], gain.partition_broadcast(128))
    P.dma("gpsimd", "w", bias[:], gate_bias.partition_broadcast(128))
    tw = P.chan_tok("w")
    for i in range(2):
        P.chan(f"ld{i}")
        P.chan(f"so{i}")
    ZTv = ZT.rearrange("(c p) t -> p c t", p=128)
    free = {}
    F = free.get
    for T in range(NT):
        i = T % 2
        c = (T % 8) * 3
        tl = P.dma("sync", f"ld{i}", xs[i][:], src[T * 128:(T + 1) * 128, :], waits=[F(("x", i))])
        t4 = rms_tile(P, xs[i][:], gb[:], us[i][:], st[:, c:c + 1], st[:, c + 1:c + 2], st[:, c + 2:c + 3],
                      waits=[tl, tw, F(("u", i))])
        free[("x", i)] = t4
        for kc in range(8):
            tp = P.op("tensor", lambda e, kc=kc, i=i: e.transpose(pT[:, kc * 128:(kc + 1) * 128],
                                                                   us[i][:, kc * 128:(kc + 1) * 128], idb[:]),
                      waits=[t4, F("pT")] + tid, sig=(kc == 7))
        free[("u", i)] = tp
        te = P.op("scalar", lambda e, i=i: e.copy(out=uT[i][:], in_=pT[:].rearrange("p (k t) -> p k t", k=8)),
                  waits=[tp, F(("uT", i))])
        free["pT"] = te
        for hf in range(2):
            for cc in range(8):
                ch = hf * 8 + cc
                for kc in range(8):
                    tz = P.op("tensor", lambda e, cc=cc, ch=ch, kc=kc, i=i: e.matmul(
                        pZ[:, cc, :], lhsT=wc[:, kc, ch * 128:(ch + 1) * 128], rhs=uT[i][:, kc, :],
                        start=(kc == 0), stop=(kc == 7)), waits=[te, tw, F("pZ")], sig=(cc == 7 and kc == 7))
            eng = "vector" if hf == 0 else "scalar"
            if eng == "vector":
                tze = P.op("vector", lambda e, hf=hf, i=i: e.tensor_copy(out=zs[i][:, hf * 8:(hf + 1) * 8, :], in_=pZ[:]),
                           waits=[tz, F(("zs", i))])
            else:
                tze = P.op("scalar", lambda e, hf=hf, i=i: e.copy(out=zs[i][:, hf * 8:(hf + 1) * 8, :], in_=pZ[:]),
                           waits=[tz, F(("zs", i))])
            free["pZ"] = tze
            P.dma("gpsimd", f"so{i}", ZTv[:, hf * 8:(hf + 1) * 8, T * 128:(T + 1) * 128], zs[i][:, hf * 8:(hf + 1) * 8, :],
                  waits=[tze])
        for gq in range(4):
            for kc in range(8):
                tvo = P.op("tensor", lambda e, gq=gq, kc=kc, i=i: e.matmul(
                    pVO[:, gq * 512:(gq + 1) * 512], lhsT=uT[i][:, kc, :], rhs=wc[:, kc, 2048 + gq * 512:2048 + (gq + 1) * 512],
                    start=(kc == 0), stop=(kc == 7)), waits=[te, tw, F("pVO")], sig=(gq == 3 and kc == 7))
        for kc in range(8):
            tg = P.op("tensor", lambda e, kc=kc, i=i: e.matmul(pG[:], lhsT=uT[i][:, kc, :], rhs=wc[:, kc, 4096:4112],
                                                               start=(kc == 0), stop=(kc == 7)),
                      waits=[te, tw, F("pG")], sig=(kc == 7))
        free[("uT", i)] = tg
        ev = P.op("vector", lambda e, i=i: e.tensor_copy(out=vs[i][:], in_=pVO[:, 0:1024]), waits=[tvo, F(("so", i))])
        eo = P.op("scalar", lambda e, i=i: e.activation(out=os_[i][:], in_=pVO[:, 1024:2048], func=AF.Sigmoid),
                  waits=[tvo, F(("so", i))])
        free["pVO"] = [ev, eo]
        eg = P.op("vector", lambda e, i=i: e.tensor_tensor(out=gs[i][:], in0=pG[:], in1=bias[:], op=ALU.add),
                  waits=[tg, tw, F(("so", i))])
        free["pG"] = eg
        P.dma("gpsimd", f"so{i}", V1[T * 128:(T + 1) * 128, :], vs[i][:], waits=[ev])
        P.dma("gpsimd", f"so{i}", O1[T * 128:(T + 1) * 128, :], os_[i][:], waits=[eo])
        tso = P.dma("gpsimd", f"so{i}", G[T * 128:(T + 1) * 128, :], gs[i][:], waits=[eg])
        free[("so", i)] = tso
        free[("zs", i)] = tso
    P.emit()


def sweep_phase(nc, name, direction, ZT, QK, V1, O1, G, HB, conv_w, tabs, seqs, fin=None):
    P = Phase(nc, name)
    fwd = direction == "f"
    d8 = 0 if fwd else 8
    zs = [P.sb(f"zs{i}", [128, 16, 130], F32) for i in range(2)]
    acc = P.sb("acc", [128, 16, 128], F32)
    tmpc = P.sb("tmpc", [128, 16, 128], F32)
    qkT = [P.sb(f"qkT{i}", [128, 16, 128], BF16) for i in range(2)]
    va = [P.sb(f"va{i}", [128, MH, MD + 1], BF16) for i in range(2)]
    gt = [P.sb(f"gt{i}", [128, 8], F32) for i in range(2)]
    kt = [P.sb(f"kt{i}", [128, MH, MD], BF16) for i in range(2)]
    sqk = [P.sb(f"sqk{i}", [128, 128], BF16) for i in range(2)]
    Cst = P.sb("Cst", [128, MH, 2, MD + 1], F32)
    Cbf = P.sb("Cbf", [128, MH, 2, MD + 1], BF16)
    hbuf = [P.sb(f"hbuf{i}", [128, D], F32) for i in range(2)]
    um = P.sb("um", [128, 3, 128], F32)
    cw = P.sb("cw", [128, 3, 16], F32)
    gsm = [P.sb(f"gsm{i}", [128, 40], F32) for i in range(2)]
    dsm = [P.sb(f"dsm{i}", [128, 16], F32) for i in range(2)]
    idf, idb, tid = load_consts(P)
    pS = P.ps("pS", [128, 2, 128])
    pA = P.ps("pA", [128, 2, 512])
    pC = P.ps("pC", [128, 2, 512])
    pT = P.ps("pT", [128, 8, 128], BF16)
    pG = P.ps("pG", [128, 8])
    P.chan("w")
    P.dma("gpsimd", "w", um[:], tabs["um"])
    cwT = P.sb("cwT", [48, 128], F32)
    P.dma("gpsimd", "w", cwT[:], conv_w.rearrange("j (c p) -> (j c) p", p=128))
    if fwd:
        wo = P.sb("wo", [128, 8, D], BF16)
        gh = P.sb("gh", [128, D], F32)
        hbt = [P.sb(f"hbt{i}", [128, D], F32) for i in range(2)]
        og = [P.sb(f"og{i}", [128, D], BF16) for i in range(2)]
        xs = [P.sb(f"x{i}", [128, D], F32) for i in range(2)]
        sq = P.sb("sq", [128, D], F32)
        hn = P.sb("hn", [128, D], F32)
        mb = P.sb("mb", [128, D], BF16)
        mTs = P.sb("mTs", [128, 8, 128], BF16)
        hsm = P.sb("hsm", [128, 12], F32)
        pY = P.ps("pY", [128, 512])
        wv = fin["w_out"].rearrange("(kc p) d -> p kc d", p=128)
        for kc in range(8):
            P.dma("gpsimd", "w", wo[:, kc, :], wv[:, kc, :])
        P.dma("gpsimd", "w", gh[:], fin["head_norm"].partition_broadcast(128))
    tw0 = P.chan_tok("w")
    tcw1 = P.op("tensor", lambda e: e.transpose(pC[:, 0, 0:48], cwT[:], idf[0:48, 0:48]), waits=[tw0] + tid)
    tcw2 = P.op("vector", lambda e: e.tensor_copy(out=cw[:].rearrange("p j c -> p (j c)"), in_=pC[:, 0, 0:48]), waits=[tcw1])
    tw = [tw0, tcw2]
    tm = [P.op("gpsimd", lambda e, i=i: e.memset(va[i][:, :, MD:MD + 1], 1.0)) for i in range(2)]
    for i in range(2):
        P.chan(f"ld{i}")
        P.chan(f"st{i}")
    ZTv = ZT.rearrange("(c p) t -> p c t", p=128)
    QKv = QK.rearrange("(c p) t -> p c t", p=128)
    free = {"pC": tcw2}
    F = free.get
    umask = um[:, 0 if fwd else 1, :]
    uones = um[:, 2, :]

    sched = []
    off = 0
    for S in seqs:
        n = S // 128
        order = range(n) if fwd else range(n - 1, -1, -1)
        for k, ci in enumerate(order):
            sched.append((off + ci * 128, ci == 0, ci == n - 1, k == 0))
        off += S

    cstate = [None]
    for it, (t0, seq_first, seq_last, reset) in enumerate(sched):
        i = it % 2
        lo = 1 if seq_first else 0
        hi = 129 if seq_last else 130
        lw = [F(("zs", i))]
        if not fwd:
            P.dma("sync", f"ld{i}", zs[i][:, :, lo:hi], ZTv[:, :, t0 - 1 + lo:t0 - 1 + hi], waits=lw)
        else:
            P.dma("sync", f"ld{i}", qkT[i][:], QKv[:, :, t0:t0 + 128], waits=[F(("qkT", i))])
        P.dma("sync", f"ld{i}", va[i][:, :, 0:MD], V1[t0:t0 + 128, :].rearrange("p (h d) -> p h d", h=MH),
              waits=[tm[i], F(("va", i))])
        tl = P.dma("sync", f"ld{i}", gt[i][:], G[t0:t0 + 128, d8:d8 + 8], waits=[F(("gt", i))])
        if fwd:
            P.dma("sync", f"ld{i}", hbt[i][:], HB[t0:t0 + 128, :], waits=[F(("fin", i))])
            P.dma("sync", f"ld{i}", og[i][:], O1[t0:t0 + 128, :])
            tl = P.dma("sync", f"ld{i}", xs[i][:], fin["src"][t0:t0 + 128, :], waits=[F(("xst", i))])
        tz = []
        if seq_first and not fwd:
            tz.append(P.op("gpsimd", lambda e, i=i: e.memset(zs[i][:, :, 0:1], 0.0), waits=lw))
        if seq_last and not fwd:
            tz.append(P.op("gpsimd", lambda e, i=i: e.memset(zs[i][:, :, 129:130], 0.0), waits=lw))
        if reset:
            rw = [cstate[0], F("Cbf_r")] + [F(("Cst", h)) for h in range(MH)]
            r1 = P.op("gpsimd", lambda e: e.memset(Cst[:], 0.0), waits=rw)
            r2 = P.op("gpsimd", lambda e: e.memset(Cbf[:], 0.0), waits=rw)
            cstate[0] = [r1, r2]
        def cwb(j):
            return cw[:, j, :].unsqueeze(2).to_broadcast([128, 16, 128])
        if not fwd:
            c1 = P.op("vector", lambda e, i=i: e.tensor_tensor(out=acc[:], in0=zs[i][:, :, 0:128], in1=cwb(0), op=ALU.mult),
                      waits=[tl, tw, F("acc")] + tz)
            c2 = P.op("gpsimd", lambda e, i=i: e.tensor_tensor(out=tmpc[:], in0=zs[i][:, :, 1:129], in1=cwb(1), op=ALU.mult),
                      waits=[tl, tw, F("tmpc")] + tz)
            c3 = P.op("vector", lambda e: e.tensor_tensor(out=acc[:], in0=acc[:], in1=tmpc[:], op=ALU.add), waits=[c1, c2])
            c4 = P.op("gpsimd", lambda e, i=i: e.tensor_tensor(out=tmpc[:], in0=zs[i][:, :, 2:130], in1=cwb(2), op=ALU.mult),
                      waits=[c3, tl] + tz)
            free[("zs", i)] = c4
            c5 = P.op("vector", lambda e: e.tensor_tensor(out=acc[:], in0=acc[:], in1=tmpc[:], op=ALU.add), waits=[c4])
            free["tmpc"] = c5
            c6 = P.op("scalar", lambda e, i=i: e.activation(out=qkT[i][:], in_=acc[:], func=AF.Silu),
                      waits=[c5, F(("qkT", i))])
            free["acc"] = c6
            tqk = P.dma("gpsimd", f"st{i}", QKv[:, :, t0:t0 + 128], qkT[i][:], waits=[c6])
        else:
            c6 = tl
            tqk = None
        g_ = gsm[i]
        e1, l1, tmpg, al, ej, eL = (g_[:, 0:4], g_[:, 4:8], g_[:, 8:12], g_[:, 12:16], g_[:, 16:20], g_[:, 20:24])
        ig = gt[i][:, 0:4]
        fg = gt[i][:, 4:8]
        g1 = P.op("scalar", lambda e, e1=e1, fg=fg: e.activation(out=e1, in_=fg, func=AF.Exp, scale=-1.0),
                  waits=[tl, F(("gsm", i))])
        g2 = P.op("scalar", lambda e, e1=e1, l1=l1: e.activation(out=l1, in_=e1, func=AF.Ln, bias=1.0), waits=[g1])
        P.op("tensor", lambda e, l1=l1: e.matmul(pG[:, 0:4], lhsT=umask, rhs=l1, start=True, stop=True),
             waits=[g2, tw, F("pG")], sig=False)
        g3 = P.op("tensor", lambda e, l1=l1: e.matmul(pG[:, 4:8], lhsT=uones, rhs=l1, start=True, stop=True))
        g4 = P.op("vector", lambda e, tmpg=tmpg, ig=ig: e.tensor_tensor(out=tmpg, in0=ig, in1=pG[:, 0:4], op=ALU.add),
                  waits=[g3, tl])
        free[("gt", i)] = g4
        g5 = P.op("scalar", lambda e, al=al, tmpg=tmpg: e.activation(out=al, in_=tmpg, func=AF.Exp, bias=-LN16), waits=[g4])
        g6 = P.op("scalar", lambda e, ej=ej: e.activation(out=ej, in_=pG[:, 0:4], func=AF.Exp, scale=-1.0), waits=[g3])
        g7 = P.op("scalar", lambda e, eL=eL: e.activation(out=eL, in_=pG[:, 4:8], func=AF.Exp, scale=-1.0), waits=[g3])
        free["pG"] = [g4, g7]
        for c in range(8):
            tk = P.op("tensor", lambda e, c=c, i=i: e.transpose(pT[:, c, :], qkT[i][:, 8 + c, :], idb[:]),
                      waits=[c6, F("pT")] + tid, sig=(c == 7))
        k1 = P.op("vector", lambda e, i=i, al=al: e.tensor_tensor(
            out=kt[i][:], in0=pT[:].rearrange("p (h c) t -> p h (c t)", h=MH),
            in1=al.unsqueeze(2).to_broadcast([128, MH, MD]), op=ALU.mult), waits=[tk, g5, F(("kt", i))])
        free["pT"] = k1
        hts = []
        last_pe = None
        for h in range(MH):
            b = h % 2
            for dc in range(2):
                ts_ = P.op("tensor", lambda e, b=b, h=h, dc=dc, i=i: e.matmul(
                    pS[:, b, :], lhsT=qkT[i][:, 8 + 2 * h + dc, :], rhs=qkT[i][:, 2 * h + dc, :],
                    start=(dc == 0), stop=(dc == 1)), waits=[c6, F(("pS", b))], sig=(dc == 1))
            s1 = P.op("vector", lambda e, b=b, h=h, al=al: e.scalar_tensor_tensor(
                out=sqk[b][:], in0=pS[:, b, :], scalar=al[:, h:h + 1], in1=umask, op0=ALU.mult, op1=ALU.mult),
                waits=[ts_, g5, tw, F(("sqk", b))])
            free[("pS", b)] = s1
            P.op("tensor", lambda e, b=b, h=h, i=i: e.matmul(pA[:, b, 0:MD + 1], lhsT=sqk[b][:], rhs=va[i][:, h, :],
                                                             start=True, stop=False),
                 waits=[s1, tl, F(("pA", b))], sig=False)
            for dc in range(2):
                ta = P.op("tensor", lambda e, b=b, h=h, dc=dc, i=i: e.matmul(
                    pA[:, b, 0:MD + 1], lhsT=qkT[i][:, 2 * h + dc, :], rhs=Cbf[:, h, dc, :], start=False, stop=(dc == 1)),
                    waits=[cstate[0], F(("Cbf", h))], sig=(dc == 1))
            free[("sqk", b)] = ta
            for dc in range(2):
                tc_ = P.op("tensor", lambda e, h=h, dc=dc, i=i: e.matmul(
                    pC[:, dc, 0:MD + 1], lhsT=kt[i][:, h, dc * 128:(dc + 1) * 128], rhs=va[i][:, h, :],
                    start=True, stop=True), waits=[k1, tl, F("pC")], sig=(dc == 1))
            last_pe = tc_
            dd = dsm[i]
            d1, d2, rr, sc = (dd[:, 4 * h:4 * h + 1], dd[:, 4 * h + 1:4 * h + 2], dd[:, 4 * h + 2:4 * h + 3],
                              dd[:, 4 * h + 3:4 * h + 4])
            o0 = P.op("vector", lambda e, b=b, h=h, d1=d1, ej=ej: e.tensor_scalar(
                out=d1, in0=pA[:, b, MD:MD + 1], scalar1=ej[:, h:h + 1], scalar2=None, op0=ALU.mult),
                waits=[ta, g6, F(("dsm", i))])
            o1 = P.op("vector", lambda e, d1=d1: e.scalar_tensor_tensor(out=d1, in0=d1, scalar=-1.0, in1=d1,
                                                                       op0=ALU.mult, op1=ALU.max), waits=[o0])
            o2 = P.op("vector", lambda e, d1=d1, d2=d2: e.tensor_scalar(out=d2, in0=d1, scalar1=1.0, scalar2=None,
                                                                         op0=ALU.max), waits=[o1])
            o3 = P.op("vector", lambda e, d2=d2, rr=rr: e.reciprocal(out=rr, in_=d2), waits=[o2])
            o4 = P.op("vector", lambda e, rr=rr, sc=sc, ej=ej, h=h: e.tensor_tensor(out=sc, in0=rr, in1=ej[:, h:h + 1],
                                                                                     op=ALU.mult), waits=[o3])
            o5 = P.op("vector", lambda e, b=b, h=h, i=i, sc=sc: e.tensor_scalar(
                out=hbuf[i][:, h * MD:(h + 1) * MD], in0=pA[:, b, 0:MD], scalar1=sc, scalar2=None, op0=ALU.mult),
                waits=[o4, F(("hbuf", i))])
            free[("pA", b)] = o5
            hts.append(o5)
            uA = P.op("vector", lambda e, h=h, eL=eL: e.tensor_scalar(out=Cst[:, h, :, :], in0=Cst[:, h, :, :],
                                                                      scalar1=eL[:, h:h + 1], scalar2=None, op0=ALU.mult),
                      waits=[g7, cstate[0], F(("Cst", h))])
            u2 = P.op("vector", lambda e, h=h, eL=eL: e.scalar_tensor_tensor(
                out=Cst[:, h, :, :], in0=pC[:, :, 0:MD + 1], scalar=eL[:, h:h + 1], in1=Cst[:, h, :, :],
                op0=ALU.mult, op1=ALU.add), waits=[tc_, uA])
            free["pC"] = u2
            u3 = P.op("scalar", lambda e, h=h: e.copy(out=Cbf[:, h, :, :], in_=Cst[:, h, :, :]), waits=[u2, ta])
            free[("Cst", h)] = u3
            free[("Cbf", h)] = u3
        free["Cbf_r"] = ta
        free[("qkT", i)] = [last_pe, tqk]
        free[("kt", i)] = last_pe
        free[("va", i)] = last_pe
        free[("gsm", i)] = [u2, o4, k1]
        free[("dsm", i)] = hts[-1]
        if not fwd:
            tst = P.dma("gpsimd", f"st{i}", HB[t0:t0 + 128, :], hbuf[i][:], waits=hts)
            free[("hbuf", i)] = tst
            free[("qkT", i)] = [last_pe, tst]
        else:
            f1 = P.op("gpsimd", lambda e, i=i: e.tensor_tensor(out=hbuf[i][:], in0=hbuf[i][:], in1=hbt[i][:], op=ALU.add),
                      waits=hts + [tl])
            f2 = P.op("scalar", lambda e, i=i: e.activation(out=sq[:], in_=hbuf[i][:], func=AF.Square),
                      waits=[f1, F("sq")])
            ssh, sdh, rsh = hsm[:, 0:4], hsm[:, 4:8], hsm[:, 8:12]
            f3 = P.op("vector", lambda e: e.tensor_reduce(out=ssh, in_=sq[:].rearrange("p (h d) -> p h d", h=MH),
                                                          axis=AX.X, op=ALU.add), waits=[f2])
            free["sq"] = f3
            f4 = P.op("scalar", lambda e: e.activation(out=sdh, in_=ssh, func=AF.Sqrt, scale=1.0 / MD, bias=EPS), waits=[f3])
            f5 = P.op("vector", lambda e: e.reciprocal(out=rsh, in_=sdh), waits=[f4])
            f6 = P.op("vector", lambda e, i=i: e.tensor_tensor(
                out=hn[:].rearrange("p (h d) -> p h d", h=MH), in0=hbuf[i][:].rearrange("p (h d) -> p h d", h=MH),
                in1=rsh.unsqueeze(2).to_broadcast([128, MH, MD]), op=ALU.mult), waits=[f5, F("hn")])
            free[("hbuf", i)] = f6
            f7 = P.op("gpsimd", lambda e: e.tensor_tensor(out=hn[:], in0=hn[:], in1=gh[:], op=ALU.mult), waits=[f6, tw])
            f8 = P.op("vector", lambda e, i=i: e.tensor_tensor(out=mb[:], in0=hn[:], in1=og[i][:], op=ALU.mult),
                      waits=[f7, tl, F("mb")])
            free["hn"] = f8
            for kc in range(8):
                f9 = P.op("tensor", lambda e, kc=kc: e.transpose(pT[:, kc, :], mb[:, kc * 128:(kc + 1) * 128], idb[:]),
                          waits=[f8, F("pT")], sig=(kc == 7))
            free["mb"] = f9
            f10 = P.op("scalar", lambda e: e.copy(out=mTs[:], in_=pT[:]), waits=[f9, F("mTs")])
            free["pT"] = f10
            tds = []
            for dh in range(2):
                for kc in range(8):
                    f11 = P.op("tensor", lambda e, kc=kc, dh=dh: e.matmul(
                        pY[:], lhsT=mTs[:, kc, :], rhs=wo[:, kc, dh * 512:(dh + 1) * 512], start=(kc == 0), stop=(kc == 7)),
                        waits=[f10, tw, F("pY")], sig=(kc == 7))
                f12 = P.op("vector", lambda e, i=i, dh=dh: e.tensor_tensor(
                    out=xs[i][:, dh * 512:(dh + 1) * 512], in0=xs[i][:, dh * 512:(dh + 1) * 512], in1=pY[:], op=ALU.add),
                    waits=[f11, tl])
                free["pY"] = f12
                tds.append(f12)
            free["mTs"] = f11
            free[("fin", i)] = [f1, f8]
            free[("xst", i)] = P.dma("gpsimd", f"st{i}", fin["dst"][t0:t0 + 128, :], xs[i][:], waits=tds)
    P.emit()


def sweep_phase2(nc, name, direction, ZT, QK, V1, O1, G, HB, conv_w, tabs, seqs, fin=None):
    P = Phase(nc, name)
    fwd = direction == "f"
    d8 = 0 if fwd else 8
    ND = 10
    NB = 3
    qkT = [P.sb(f"qkT{i}", [128, 16, 128], BF16) for i in range(NB)]
    va = [P.sb(f"va{i}", [128, MH, MD + 1], BF16) for i in range(NB)]
    gt = [P.sb(f"gt{i}", [128, 8], F32) for i in range(NB)]
    kt = [P.sb(f"kt{i}", [128, MH, MD], BF16) for i in range(NB)]
    sqk = [[P.sb(f"sqk{i}_{h}", [128, 128], BF16) for h in range(MH)] for i in range(NB)]
    CstL = [P.sb(f"Cst{k}", [128, MH, 2, MD + 1], F32) for k in range(2)]
    CbfL = [P.sb(f"Cbf{k}", [128, MH, 2, MD + 1], BF16) for k in range(2)]
    hbuf = [P.sb(f"hbuf{i}", [128, D], F32) for i in range(NB)]
    um = P.sb("um", [128, 3, 128], F32)
    gsm = [P.sb(f"gsm{i}", [128, 32], F32) for i in range(NB)]
    dsm = P.sb("dsm", [128, 16], F32)
    idf, idb, tid = load_consts(P)
    pS = P.ps("pS", [128, 2, 512])
    pA = P.ps("pA", [128, 2, 512])
    pC = P.ps("pC", [128, 2, 512])
    pT = P.ps("pT", [128, 8, 128], BF16)
    pG = P.ps("pG", [128, 8])
    P.chan("w")
    P.dma("gpsimd", "w", um[:], tabs["um"])
    if not fwd:
        zs = [P.sb(f"zs{i}", [128, 16, 130], F32) for i in range(NB)]
        acc = P.sb("acc", [128, 16, 128], F32)
        tmpc = P.sb("tmpc", [128, 16, 128], F32)
        cw = P.sb("cw", [128, 3, 16], F32)
        cwT = P.sb("cwT", [48, 128], F32)
        P.dma("gpsimd", "w", cwT[:], conv_w.rearrange("j (c p) -> (j c) p", p=128))
    else:
        wo = P.sb("wo", [128, 8, D], BF16)
        gh = P.sb("gh", [128, D], F32)
        hbt = [P.sb(f"hbt{i}", [128, D], F32) for i in range(NB)]
        og = [P.sb(f"og{i}", [128, D], BF16) for i in range(NB)]
        xs = [P.sb(f"x{i}", [128, D], F32) for i in range(NB)]
        ogh = [P.sb(f"ogh{i}", [128, D], F32) for i in range(NB)]
        sq = P.sb("sq", [128, D], F32)
        hn = P.sb("hn", [128, D], F32)
        mb = P.sb("mb", [128, D], BF16)
        mTs = P.sb("mTs", [128, 8, 128], BF16)
        hsm = P.sb("hsm", [128, 12], F32)
        wv = fin["w_out"].rearrange("(kc p) d -> p kc d", p=128)
        for kc in range(8):
            P.dma("gpsimd", "w", wo[:, kc, :], wv[:, kc, :])
        P.dma("gpsimd", "w", gh[:], fin["head_norm"].partition_broadcast(128))
    tw0 = P.chan_tok("w")
    free = {}
    F = free.get
    if not fwd:
        tcw1 = P.op("tensor", lambda e: e.transpose(pC[:, 0, 0:48], cwT[:], idf[0:48, 0:48]), waits=[tw0] + tid)
        tcw2 = P.op("vector", lambda e: e.tensor_copy(out=cw[:].rearrange("p j c -> p (j c)"), in_=pC[:, 0, 0:48]),
                    waits=[tcw1])
        tw = [tw0, tcw2]
        free["pC"] = tcw2
    else:
        tw = [tw0]
    tm = [P.op("gpsimd", lambda e, i=i: e.memset(va[i][:, :, MD:MD + 1], 1.0)) for i in range(NB)]
    for i in range(NB):
        P.chan(f"ld{i}")
        P.chan(f"st{i}")
    ZTv = ZT.rearrange("(c p) t -> p c t", p=128)
    QKv = QK.rearrange("(c p) t -> p c t", p=128)
    umask = um[:, 0 if fwd else 1, :]
    uones = um[:, 2, :]

    def seq_chunks(off, S):
        n = S // 128
        order = range(n) if fwd else range(n - 1, -1, -1)
        return [(off + ci * 128, ci == 0, ci == n - 1, k == 0) for k, ci in enumerate(order)]
    sched = []
    offs = []
    off = 0
    for S in seqs:
        offs.append(off)
        off += S
    k = 0
    while k < len(seqs):
        if k + 1 < len(seqs) and seqs[k] == seqs[k + 1]:
            c0s, c1s = seq_chunks(offs[k], seqs[k]), seq_chunks(offs[k + 1], seqs[k + 1])
            assert len(sched) % 2 == 0
            for x0, x1 in zip(c0s, c1s):
                sched.append(x0 + (0,))
                sched.append(x1 + (1,))
            k += 2
        else:
            sched += [x + (0,) for x in seq_chunks(offs[k], seqs[k])]
            k += 1
    A = {}

    def stage_a(it):
        t0, seq_first, seq_last, reset, sid = sched[it]
        i = it % NB
        if not fwd:
            lo = 1 if seq_first else 0
            hi = 129 if seq_last else 130
            lw = [F(("zs", i))]
            P.dma("sync", f"ld{i}", zs[i][:, :, lo:hi], ZTv[:, :, t0 - 1 + lo:t0 - 1 + hi], waits=lw)
        else:
            P.dma("sync", f"ld{i}", qkT[i][:], QKv[:, :, t0:t0 + 128], waits=[F(("qkT", i))])
        P.dma("sync", f"ld{i}", va[i][:, :, 0:MD], V1[t0:t0 + 128, :].rearrange("p (h d) -> p h d", h=MH),
              waits=[tm[i], F(("va", i))])
        tl = P.dma("sync", f"ld{i}", gt[i][:], G[t0:t0 + 128, d8:d8 + 8], waits=[F(("gt", i))])
        if fwd:
            P.dma("sync", f"ld{i}", hbt[i][:], HB[t0:t0 + 128, :], waits=[F(("fin", i))])
            P.dma("sync", f"ld{i}", og[i][:], O1[t0:t0 + 128, :])
            tl = P.dma("sync", f"ld{i}", xs[i][:], fin["src"][t0:t0 + 128, :], waits=[F(("xst", i))])
        tqk = None
        togh = None
        if fwd:
            togh = P.op("gpsimd", lambda e: e.tensor_tensor(out=ogh[i][:], in0=og[i][:], in1=gh[:], op=ALU.mult),
                        waits=[tl, tw, F(("ogh", i))])
        if not fwd:
            tz = []
            if seq_first:
                tz.append(P.op("gpsimd", lambda e: e.memset(zs[i][:, :, 0:1], 0.0), waits=lw))
            if seq_last:
                tz.append(P.op("gpsimd", lambda e: e.memset(zs[i][:, :, 129:130], 0.0), waits=lw))
            cend = []
            for eng, c0, c1 in (("vector", 0, ND), ("gpsimd", ND, 16)):
                nch = c1 - c0

                def cwb(j, c0=c0, c1=c1, nch=nch):
                    return cw[:, j, c0:c1].unsqueeze(2).to_broadcast([128, nch, 128])
                k1_ = P.op(eng, lambda e, c0=c0, c1=c1, cwb=cwb: e.tensor_tensor(
                    out=acc[:, c0:c1, :], in0=zs[i][:, c0:c1, 0:128], in1=cwb(0), op=ALU.mult),
                    waits=[tl, tw, F("acc")] + tz)
                k2_ = P.op(eng, lambda e, c0=c0, c1=c1, cwb=cwb: e.tensor_tensor(
                    out=tmpc[:, c0:c1, :], in0=zs[i][:, c0:c1, 1:129], in1=cwb(1), op=ALU.mult), waits=[k1_])
                k3_ = P.op(eng, lambda e, c0=c0, c1=c1: e.tensor_tensor(
                    out=acc[:, c0:c1, :], in0=acc[:, c0:c1, :], in1=tmpc[:, c0:c1, :], op=ALU.add), waits=[k2_])
                k4_ = P.op(eng, lambda e, c0=c0, c1=c1, cwb=cwb: e.tensor_tensor(
                    out=tmpc[:, c0:c1, :], in0=zs[i][:, c0:c1, 2:130], in1=cwb(2), op=ALU.mult), waits=[k3_])
                k5_ = P.op(eng, lambda e, c0=c0, c1=c1: e.tensor_tensor(
                    out=acc[:, c0:c1, :], in0=acc[:, c0:c1, :], in1=tmpc[:, c0:c1, :], op=ALU.add), waits=[k4_])
                cend.append(k5_)
            free[("zs", i)] = cend
            c6 = P.op("scalar", lambda e: e.activation(out=qkT[i][:], in_=acc[:], func=AF.Silu),
                      waits=cend + [F(("qkT", i))])
            free["acc"] = c6
            tqk = P.dma("gpsimd", f"st{i}", QKv[:, :, t0:t0 + 128], qkT[i][:], waits=[c6])
        else:
            c6 = tl
        g_ = gsm[i]
        e1, l1, tmpg, al, einv, eL = (g_[:, 0:4], g_[:, 4:8], g_[:, 8:12], g_[:, 12:16], g_[:, 16:20], g_[:, 20:24])
        ig = gt[i][:, 0:4]
        fg = gt[i][:, 4:8]
        g1 = P.op("scalar", lambda e: e.activation(out=e1, in_=fg, func=AF.Exp, scale=-1.0), waits=[tl, F(("gsm", i))])
        g2 = P.op("scalar", lambda e: e.activation(out=l1, in_=e1, func=AF.Ln, bias=1.0), waits=[g1])
        P.op("tensor", lambda e: e.matmul(pG[:, 0:4], lhsT=umask, rhs=l1, start=True, stop=True),
             waits=[g2, tw, F("pG")], sig=False)
        g3 = P.op("tensor", lambda e: e.matmul(pG[:, 4:8], lhsT=uones, rhs=l1, start=True, stop=True))
        g4 = P.op("vector", lambda e: e.tensor_tensor(out=tmpg, in0=ig, in1=pG[:, 0:4], op=ALU.add), waits=[g3, tl])
        free[("gt", i)] = g4
        g5 = P.op("scalar", lambda e: e.activation(out=al, in_=tmpg, func=AF.Exp, bias=-LN16), waits=[g4])
        g6 = P.op("scalar", lambda e: e.activation(out=einv, in_=pG[:, 0:4], func=AF.Exp), waits=[g3])
        g7 = P.op("scalar", lambda e: e.activation(out=eL, in_=pG[:, 4:8], func=AF.Exp, scale=-1.0), waits=[g3])
        free["pG"] = [g4, g7]
        for c in range(8):
            tk = P.op("tensor", lambda e, c=c: e.transpose(pT[:, c, :], qkT[i][:, 8 + c, :], idb[:]),
                      waits=[c6, F("pT")] + tid, sig=(c == 7))
        k1 = P.op("vector", lambda e: e.tensor_tensor(
            out=kt[i][:], in0=pT[:].rearrange("p (h c) t -> p h (c t)", h=MH),
            in1=al.unsqueeze(2).to_broadcast([128, MH, MD]), op=ALU.mult), waits=[tk, g5, F(("kt", i))])
        free["pT"] = k1
        s1s = []
        for h in range(MH):
            b = h % 2
            for dc in range(2):
                ts_ = P.op("tensor", lambda e, b=b, h=h, dc=dc: e.matmul(
                    pS[:, b, 0:128], lhsT=qkT[i][:, 8 + 2 * h + dc, :], rhs=qkT[i][:, 2 * h + dc, :],
                    start=(dc == 0), stop=(dc == 1)), waits=[c6, F(("pS", b))], sig=(dc == 1))
            s1 = P.op("vector", lambda e, b=b, h=h: e.scalar_tensor_tensor(
                out=sqk[i][h][:], in0=pS[:, b, 0:128], scalar=al[:, h:h + 1], in1=umask, op0=ALU.mult, op1=ALU.mult),
                waits=[ts_, g5, tw, F(("sqk", i, h))])
            free[("pS", b)] = s1
            s1s.append(s1)
        A[it] = dict(tl=tl, c6=c6, k1=k1, g6=g6, g7=g7, s1s=s1s, tqk=tqk, einv=einv, eL=eL, togh=togh)

    def stage_b(it):
        t0, seq_first, seq_last, reset, sid = sched[it]
        i = it % NB
        Cst, Cbf = CstL[sid], CbfL[sid]
        a = A.pop(it)
        tl, c6, k1, g6, g7, s1s, tqk, einv, eL = (a["tl"], a["c6"], a["k1"], a["g6"], a["g7"], a["s1s"], a["tqk"],
                                                  a["einv"], a["eL"])
        togh = a["togh"]
        sqs = []
        rtok = None
        if reset:
            rw = [F(("Cbf_r", sid))] + [F(("Cst", sid, h)) for h in range(MH)]
            r1 = P.op("gpsimd", lambda e: e.memset(Cst[:], 0.0), waits=rw)
            r2 = P.op("gpsimd", lambda e: e.memset(Cbf[:], 0.0), waits=rw)
            rtok = [r1, r2]
        hts = []
        last_pe = None
        gl = []
        for h in range(MH):
            b = h % 2
            for dc in range(2):
                tc_ = P.op("tensor", lambda e, h=h, dc=dc: e.matmul(
                    pC[:, dc, 0:MD + 1], lhsT=kt[i][:, h, dc * 128:(dc + 1) * 128], rhs=va[i][:, h, :],
                    start=True, stop=True), waits=[k1, tl, F("pC")], sig=(dc == 1))
            last_pe = tc_
            uA = P.op("gpsimd", lambda e, h=h: e.tensor_tensor(
                out=Cst[:, h, :, :], in0=Cst[:, h, :, :], in1=eL[:, h:h + 1].unsqueeze(2).to_broadcast([128, 2, MD + 1]),
                op=ALU.mult), waits=[g7, rtok, F(("Cst", sid, h))])
            u2 = P.op("vector", lambda e, h=h: e.scalar_tensor_tensor(
                out=Cst[:, h, :, :], in0=pC[:, :, 0:MD + 1], scalar=eL[:, h:h + 1], in1=Cst[:, h, :, :],
                op0=ALU.mult, op1=ALU.add), waits=[tc_, uA, g7])
            free["pC"] = u2
            P.op("tensor", lambda e, b=b, h=h: e.matmul(pA[:, b, 0:MD + 1], lhsT=sqk[i][h][:], rhs=va[i][:, h, :],
                                                        start=True, stop=False),
                 waits=[s1s[h], tl, F(("pA", b))], sig=False)
            for dc in range(2):
                ta = P.op("tensor", lambda e, b=b, h=h, dc=dc: e.matmul(
                    pA[:, b, 0:MD + 1], lhsT=qkT[i][:, 2 * h + dc, :], rhs=Cbf[:, h, dc, :], start=False, stop=(dc == 1)),
                    waits=[c6, rtok, F(("Cbf", sid, h))], sig=(dc == 1))
            free[("sqk", i, h)] = ta
            u3 = P.op("scalar", lambda e, h=h: e.copy(out=Cbf[:, h, :, :], in_=Cst[:, h, :, :]), waits=[u2, ta])
            free[("Cst", sid, h)] = u3
            free[("Cbf", sid, h)] = u3
            ad, dmx, rr = dsm[:, 4 * h:4 * h + 1], dsm[:, 4 * h + 1:4 * h + 2], dsm[:, 4 * h + 2:4 * h + 3]
            o0 = P.op("scalar", lambda e, b=b, ad=ad: e.activation(out=ad, in_=pA[:, b, MD:MD + 1], func=AF.Abs),
                      waits=[ta, F(("dsm", h))])
            o1 = P.op("vector", lambda e, h=h, ad=ad, dmx=dmx: e.tensor_tensor(out=dmx, in0=ad, in1=einv[:, h:h + 1],
                                                                               op=ALU.max), waits=[o0, g6])
            o3 = P.op("vector", lambda e, dmx=dmx, rr=rr: e.reciprocal(out=rr, in_=dmx), waits=[o1])
            if not fwd:
                o5 = P.op("vector", lambda e, b=b, h=h, rr=rr: e.tensor_scalar(
                    out=hbuf[i][:, h * MD:(h + 1) * MD], in0=pA[:, b, 0:MD], scalar1=rr, scalar2=None, op0=ALU.mult),
                    waits=[o3, F(("hbuf", i))])
            else:
                o5 = P.op("vector", lambda e, b=b, h=h, rr=rr: e.scalar_tensor_tensor(
                    out=hbuf[i][:, h * MD:(h + 1) * MD], in0=pA[:, b, 0:MD], scalar=rr,
                    in1=hbt[i][:, h * MD:(h + 1) * MD], op0=ALU.mult, op1=ALU.add), waits=[o3, tl, F(("hbuf", i))])
                sqs.append(P.op("scalar", lambda e, h=h: e.activation(
                    out=sq[:, h * MD:(h + 1) * MD], in_=hbuf[i][:, h * MD:(h + 1) * MD], func=AF.Square,
                    accum_out=hsm[:, h:h + 1]), waits=[o5, F("hsm")]))
            free[("pA", b)] = o5
            free[("dsm", h)] = o5
            hts.append(o5)
            gl += [u2, o1]
        free[("Cbf_r", sid)] = ta
        last_pe = ta
        free[("kt", i)] = last_pe
        free[("va", i)] = last_pe
        free[("gsm", i)] = gl
        if not fwd:
            tst = P.dma("gpsimd", f"st{i}", HB[t0:t0 + 128, :], hbuf[i][:], waits=hts)
            free[("hbuf", i)] = tst
            free[("qkT", i)] = [last_pe, tst]
        else:
            free[("qkT", i)] = last_pe
            ssh, sdh, rsh = hsm[:, 0:4], hsm[:, 4:8], hsm[:, 8:12]
            f4 = P.op("scalar", lambda e: e.activation(out=sdh, in_=ssh, func=AF.Sqrt, scale=1.0 / MD, bias=EPS), waits=sqs)
            f5 = P.op("vector", lambda e: e.reciprocal(out=rsh, in_=sdh), waits=[f4])
            for h in range(MH):
                f8 = P.op("vector", lambda e, h=h: e.scalar_tensor_tensor(
                    out=mb[:, h * MD:(h + 1) * MD], in0=hbuf[i][:, h * MD:(h + 1) * MD], scalar=rsh[:, h:h + 1],
                    in1=ogh[i][:, h * MD:(h + 1) * MD], op0=ALU.mult, op1=ALU.mult), waits=[f5, togh, F("mb")])
            free[("hbuf", i)] = f8
            free["hsm"] = f8
            free[("ogh", i)] = f8
            f1 = sqs[-1]
            for kc in range(8):
                f9 = P.op("tensor", lambda e, kc=kc: e.transpose(pT[:, kc, :], mb[:, kc * 128:(kc + 1) * 128], idb[:]),
                          waits=[f8, F("pT")], sig=(kc == 7))
            free["mb"] = f9
            f10 = P.op("scalar", lambda e: e.copy(out=mTs[:], in_=pT[:]), waits=[f9, F("mTs")])
            free["pT"] = f10
            tds = []
            for dh in range(2):
                for kc in range(8):
                    f11 = P.op("tensor", lambda e, kc=kc, dh=dh: e.matmul(
                        pC[:, 0, :], lhsT=mTs[:, kc, :], rhs=wo[:, kc, dh * 512:(dh + 1) * 512], start=(kc == 0), stop=(kc == 7)),
                        waits=[f10, tw, F("pC")], sig=(kc == 7))
                f12 = P.op("vector", lambda e, dh=dh: e.tensor_tensor(
                    out=xs[i][:, dh * 512:(dh + 1) * 512], in0=xs[i][:, dh * 512:(dh + 1) * 512], in1=pC[:, 0, :], op=ALU.add),
                    waits=[f11, tl])
                free["pC"] = f12
                tds.append(f12)
            free["mTs"] = f11
            free[("fin", i)] = [f8, togh]
            free[("xst", i)] = P.dma("gpsimd", f"st{i}", fin["dst"][t0:t0 + 128, :], xs[i][:], waits=tds)

    n = len(sched)
    stage_a(0)
    for it in range(n):
        if it + 1 < n:
            stage_a(it + 1)
        stage_b(it)
    P.emit()


def a1_phase2(nc, name, src, gain, w_in, gate_bias, ZT, V1, O1, G, ntok):
    P = Phase(nc, name)
    NT = ntok // 128
    NG = ntok // 512
    wc = P.sb("wc", [128, 8, 4112], BF16)
    gb = P.sb("gb", [128, D], F32)
    bias = P.sb("bias", [128, 16], F32)
    xs = [P.sb(f"x{i}", [128, D], F32) for i in range(2)]
    us = [P.sb(f"u{i}", [128, D], BF16) for i in range(2)]
    uT = [P.sb(f"uT{i}", [128, 8, 512], BF16) for i in range(2)]
    zs = [P.sb(f"zs{i}", [128, 16, 512], F32) for i in range(2)]
    vs = [P.sb(f"vs{i}", [128, D], BF16) for i in range(2)]
    os_ = [P.sb(f"os{i}", [128, D], BF16) for i in range(2)]
    gs = [P.sb(f"gs{i}", [128, 16], F32) for i in range(2)]
    st = P.sb("st", [128, 3 * 8], F32)
    idf, idb, tid = load_consts(P)
    pT = P.ps("pT", [128, D], BF16)
    pZ = [P.ps(f"pZ{i}", [128, 512]) for i in range(2)]
    pVO = P.ps("pVO", [128, 2048])
    pG = P.ps("pG", [128, 16])
    P.chan("w")
    w_v = w_in.rearrange("(kc p) f -> p kc f", p=128)
    for kc in range(8):
        P.dma("gpsimd", "w", wc[:, kc, :], w_v[:, kc, :])
    P.dma("gpsimd", "w", gb[:], gain.partition_broadcast(128))
    P.dma("gpsimd", "w", bias[:], gate_bias.partition_broadcast(128))
    tw = P.chan_tok("w")
    for i in range(2):
        P.chan(f"ld{i}")
        P.chan(f"so{i}")
        P.chan(f"sz{i}")
    ZTv = ZT.rearrange("(c p) t -> p c t", p=128)
    free = {}
    F = free.get
    zc = 0
    for g in range(NG):
        gi = g % 2
        tes = []
        for tt in range(4):
            T = 4 * g + tt
            i = T % 2
            c = (T % 8) * 3
            tl = P.dma("sync", f"ld{i}", xs[i][:], src[T * 128:(T + 1) * 128, :], waits=[F(("x", i))])
            t4 = rms_tile(P, xs[i][:], gb[:], us[i][:], st[:, c:c + 1], st[:, c + 1:c + 2], st[:, c + 2:c + 3],
                          waits=[tl, tw, F(("u", i))])
            free[("x", i)] = t4
            for kc in range(8):
                tp = P.op("tensor", lambda e, kc=kc, i=i: e.transpose(pT[:, kc * 128:(kc + 1) * 128],
                                                                       us[i][:, kc * 128:(kc + 1) * 128], idb[:]),
                          waits=[t4, F("pT")] + tid, sig=(kc == 7))
            free[("u", i)] = tp
            te = P.op("scalar", lambda e, gi=gi, tt=tt: e.copy(out=uT[gi][:, :, tt * 128:(tt + 1) * 128],
                                                               in_=pT[:].rearrange("p (k t) -> p k t", k=8)),
                      waits=[tp, F(("uT", gi))])
            free["pT"] = te
            tes.append(te)
            for gq in range(4):
                for kc in range(8):
                    tvo = P.op("tensor", lambda e, gq=gq, kc=kc, gi=gi, tt=tt: e.matmul(
                        pVO[:, gq * 512:(gq + 1) * 512], lhsT=uT[gi][:, kc, tt * 128:(tt + 1) * 128],
                        rhs=wc[:, kc, 2048 + gq * 512:2048 + (gq + 1) * 512],
                        start=(kc == 0), stop=(kc == 7)), waits=[te, tw, F("pVO")], sig=(gq == 3 and kc == 7))
            for kc in range(8):
                tg = P.op("tensor", lambda e, kc=kc, gi=gi, tt=tt: e.matmul(
                    pG[:], lhsT=uT[gi][:, kc, tt * 128:(tt + 1) * 128], rhs=wc[:, kc, 4096:4112],
                    start=(kc == 0), stop=(kc == 7)), waits=[te, tw, F("pG")], sig=(kc == 7))
            ev = P.op("vector", lambda e, i=i: e.tensor_copy(out=vs[i][:], in_=pVO[:, 0:1024]), waits=[tvo, F(("so", i))])
            eo = P.op("scalar", lambda e, i=i: e.activation(out=os_[i][:], in_=pVO[:, 1024:2048], func=AF.Sigmoid),
                      waits=[tvo, F(("so", i))])
            free["pVO"] = [ev, eo]
            eg = P.op("vector", lambda e, i=i: e.tensor_tensor(out=gs[i][:], in0=pG[:], in1=bias[:], op=ALU.add),
                      waits=[tg, tw, F(("so", i))])
            free["pG"] = eg
            P.dma("gpsimd", f"so{i}", V1[T * 128:(T + 1) * 128, :], vs[i][:], waits=[ev])
            P.dma("gpsimd", f"so{i}", O1[T * 128:(T + 1) * 128, :], os_[i][:], waits=[eo])
            free[("so", i)] = P.dma("gpsimd", f"so{i}", G[T * 128:(T + 1) * 128, :], gs[i][:], waits=[eg])
        tzes = []
        for ch in range(16):
            pb = zc % 2
            zc += 1
            for kc in range(8):
                tz = P.op("tensor", lambda e, pb=pb, ch=ch, kc=kc, gi=gi: e.matmul(
                    pZ[pb][:], lhsT=wc[:, kc, ch * 128:(ch + 1) * 128], rhs=uT[gi][:, kc, :],
                    start=(kc == 0), stop=(kc == 7)), waits=tes + [tw, F(("pZ", pb))], sig=(kc == 7))
            if ch % 2 == 0:
                tze = P.op("vector", lambda e, pb=pb, ch=ch, gi=gi: e.tensor_copy(out=zs[gi][:, ch, :], in_=pZ[pb][:]),
                           waits=[tz, F(("zs", gi))])
            else:
                tze = P.op("scalar", lambda e, pb=pb, ch=ch, gi=gi: e.copy(out=zs[gi][:, ch, :], in_=pZ[pb][:]),
                           waits=[tz, F(("zs", gi))])
            free[("pZ", pb)] = tze
            tzes.append(tze)
            if ch % 4 == 3:
                tsz = P.dma("gpsimd", f"sz{gi}", ZTv[:, ch - 3:ch + 1, g * 512:(g + 1) * 512], zs[gi][:, ch - 3:ch + 1, :],
                            waits=tzes[-4:])
        free[("uT", gi)] = tz
        free[("zs", gi)] = tsz
    P.emit()


SEQS = [4096, 4096, 8192]


def build_program(seqs=SEQS):
    nc = bass.Bass("TRN2", target_bir_lowering=False)
    ntok = sum(seqs)

    def inp(n, s, dt=F32):
        return nc.dram_tensor(n, list(s), dt, kind="ExternalInput").ap()

    def scr(n, s, dt=F32):
        return nc.dram_tensor(n, list(s), dt).ap()

    x = inp("x", [ntok, D])
    y = nc.dram_tensor("y", [ntok, D], F32, kind="ExternalOutput").ap()
    W = {}
    for l in range(2):
        for k in (1, 2):
            W[f"f{k}n{l}"] = inp(f"f{k}n{l}", [D])
            W[f"f{k}i{l}"] = inp(f"f{k}i{l}", [D, 2 * DFF])
            W[f"f{k}o{l}"] = inp(f"f{k}o{l}", [DFF, D])
        W[f"mn{l}"] = inp(f"mn{l}", [D])
    W["abi"] = inp("abi", [D, 1536]); W["abq"] = inp("abq", [64]); W["abk"] = inp("abk", [64]); W["abo"] = inp("abo", [D, D])
    W["ci"] = inp("ci", [D, 4112]); W["cgb"] = inp("cgb", [16]); W["cc"] = inp("cc", [3, 2048]); W["chn"] = inp("chn", [D])
    W["co"] = inp("co", [D, D])
    tabs_np = const_tables(seqs)
    tabs = {k: inp(k, v.shape) for k, v in tabs_np.items()}
    xa = scr("xa", [ntok, D]); xb = scr("xb", [ntok, D]); xc = scr("xc", [ntok, D]); xd = scr("xd", [ntok, D])
    AB = scr("AB", [ntok, 512], BF16); QT = scr("QT", [6, 128, ntok], BF16); KT = scr("KT", [4, 128, ntok], BF16)
    V = scr("V", [ntok, 256], BF16); mT = scr("mT", [1024, ntok], BF16); FM = scr("FM", [ntok, 256], BF16)
    ffn_phase(nc, "fa", x, xa, W["f1n0"], W["f1i0"], W["f1o0"], ntok)
    a0_phase2(nc, "a0", xa, W["mn0"], W["abi"], W["abq"], W["abk"], tabs, AB, QT, KT, V, seqs)
    off = 0
    for si, S in enumerate(seqs):
        YS = scr(f"YS{si}", [2, 128, S // 128, 256], BF16)
        fft_phase(nc, f"ff{si}", AB, YS, FM, tabs, off, S)
        att_phase2(nc, f"at{si}", QT, KT, V, mT, off, S)
        off += S
    o0_phase(nc, "o0", xa, xb, mT, FM, W["abo"], ntok)
    ffn_phase(nc, "fb", xb, xc, W["f2n0"], W["f2i0"], W["f2o0"], ntok)
    ffn_phase(nc, "fc", xc, xd, W["f1n1"], W["f1i1"], W["f1o1"], ntok)
    xe = mlstm_layer(nc, xd, W, tabs, seqs)
    ffn_phase(nc, "fd", xe, y, W["f2n1"], W["f2i1"], W["f2o1"], ntok)
    return nc, tabs_np


def const_tables(seqs):
    t = host_tables()
    for S in sorted(set(seqs)):
        t.update(fft_tables(S))
    t.update(mlstm_tables())
    return t


def mlstm_layer(nc, xd, W, tabs, seqs):
    ntok = sum(seqs)
    ZT = nc.dram_tensor("ZT", [2048, ntok], F32).ap()
    V1 = nc.dram_tensor("V1", [ntok, D], BF16).ap()
    O1 = nc.dram_tensor("O1", [ntok, D], BF16).ap()
    G = nc.dram_tensor("G", [ntok, 16], F32).ap()
    QK = nc.dram_tensor("QK", [2048, ntok], BF16).ap()
    HB = nc.dram_tensor("HB", [ntok, D], F32).ap()
    xe = nc.dram_tensor("xe", [ntok, D], F32).ap()
    a1_phase2(nc, "a1", xd, W["mn1"], W["ci"], W["cgb"], ZT, V1, O1, G, ntok)
    sweep_phase2(nc, "sb", "b", ZT, QK, V1, O1, G, HB, W["cc"], tabs, seqs)
    sweep_phase2(nc, "sf", "f", ZT, QK, V1, O1, G, HB, W["cc"], tabs, seqs,
                fin={"w_out": W["co"], "head_norm": W["chn"], "src": xd, "dst": xe})
    return xe


_CACHE = {}


def kernel(x_prompt, x_sample, ffn1_norm, ffn1_w_in, ffn1_w_out, mix_norm, ab_w_in, ab_q_norm, ab_k_norm, ab_w_out,
           c_w_in, c_gate_bias, c_conv, c_head_norm, c_w_out, ffn2_norm, ffn2_w_in, ffn2_w_out):
    f = lambda a: np.ascontiguousarray(np.asarray(a, dtype=np.float32))
    if "nc" not in _CACHE:
        _CACHE["nc"] = build_program()
    nc, tabs_np = _CACHE["nc"]
    xp, xs = f(x_prompt), f(x_sample)
    shared = {}
    for l in range(2):
        shared[f"f1n{l}"] = f(ffn1_norm[l]); shared[f"f1i{l}"] = f(ffn1_w_in[l]); shared[f"f1o{l}"] = f(ffn1_w_out[l])
        shared[f"f2n{l}"] = f(ffn2_norm[l]); shared[f"f2i{l}"] = f(ffn2_w_in[l]); shared[f"f2o{l}"] = f(ffn2_w_out[l])
        shared[f"mn{l}"] = f(mix_norm[l])
    shared["abi"] = f(ab_w_in[0]); shared["abq"] = f(ab_q_norm[0]); shared["abk"] = f(ab_k_norm[0]); shared["abo"] = f(ab_w_out[0])
    shared["ci"] = f(c_w_in[0]); shared["cgb"] = f(c_gate_bias[0]); shared["cc"] = f(c_conv[0]); shared["chn"] = f(c_head_norm[0])
    shared["co"] = f(c_w_out[0])
    shared.update(tabs_np)
    in_maps = []
    for c in range(8):
        xc = np.concatenate([xp[2 * c], xp[2 * c + 1], xs[c]], axis=0)
        in_maps.append({"x": xc, **shared})
    res = run_bass_kernel_spmd(nc, in_maps, core_ids=list(range(8)))
    yp = np.empty_like(xp)
    ys = np.empty_like(xs)
    for c in range(8):
        yc = res.results[c]["y"]
        yp[2 * c] = yc[0:4096]
        yp[2 * c + 1] = yc[4096:8192]
        ys[c] = yc[8192:16384]
    return (yp, ys)
```
